# Optimizing a Trainium2 kernel written in Bass

```python
import jax, jax.numpy as jnp
from jax import lax
import numpy as np

D_MODEL = 2048
BATCH = 2
SEQ = 4096
DEPTH = 2
DEC_BATCH = 128
DEC_SEQ = 8
PAST_LEN = 8192
PAGE_SIZE = 128

F32 = jnp.float32
NEG_INF = -1e30
EPS = 1e-6

BRANCH_W = D_MODEL // 2
N_BRANCH = 3
A_HEAD_DIM = 64
A_HEADS = BRANCH_W // A_HEAD_DIM
A_KV_HEADS = 4
A_GROUP = A_HEADS // A_KV_HEADS
A_KV_W = A_KV_HEADS * A_HEAD_DIM
WINDOW = 128
LRU_WIDTH = BRANCH_W
LRU_BLOCKS = 16
LRU_BLOCK = LRU_WIDTH // LRU_BLOCKS
CONV_WIDTH = 4
LRU_C = 8.0
R_HEADS = 4
R_HEAD_DIM = BRANCH_W // R_HEADS
RET_CHUNK = 128
RET_THETA = 10000.0
MEM_LEN = 256
X_HEADS = 4
X_HEAD_DIM = 128
X_W = X_HEADS * X_HEAD_DIM
D_FF = 4 * D_MODEL
SPLIT_SIZES = (BRANCH_W, A_KV_W, A_KV_W, LRU_WIDTH, LRU_WIDTH, BRANCH_W, BRANCH_W, BRANCH_W, BRANCH_W, N_BRANCH * D_MODEL)
IN_W = sum(SPLIT_SIZES)

kernel_name = 'hybrid_swa_rglru_retention_decoder_step'


def rmsnorm(x, g):
    xf = x.astype(F32)
    y = xf * lax.rsqrt(jnp.mean(xf * xf, axis=-1, keepdims=True) + EPS)
    return (y * g.astype(F32)).astype(x.dtype)


def split_in(proj):
    cuts = [int(c) for c in np.cumsum(SPLIT_SIZES)[:-1]]
    return jnp.split(proj, cuts, axis=-1)


def window_mask(qpos, kpos):
    return (kpos <= qpos) & (qpos - kpos < WINDOW) & (kpos >= 0)


def sink_softmax(s, mask, sink):
    s = jnp.where(mask, s, NEG_INF)
    sk = jnp.broadcast_to(sink[:, :, None, None], s.shape[:-1] + (1,))
    return jax.nn.softmax(jnp.concatenate([s, sk], axis=-1), axis=-1)[..., :-1]


def swa_prompt(q, k, v, sink):
    b, t = q.shape[:2]
    nb = t // WINDOW
    qb = q.reshape(b, nb, WINDOW, A_KV_HEADS, A_GROUP, A_HEAD_DIM)
    pad = ((0, 0), (WINDOW, 0), (0, 0), (0, 0))
    kp, vp = jnp.pad(k, pad), jnp.pad(v, pad)

    def band(z):
        prev = z[:, :t].reshape(b, nb, WINDOW, A_KV_HEADS, A_HEAD_DIM)
        cur = z[:, WINDOW:].reshape(b, nb, WINDOW, A_KV_HEADS, A_HEAD_DIM)
        return jnp.concatenate([prev, cur], axis=2)

    kb, vb = band(kp), band(vp)
    blk = jnp.arange(nb)[:, None] * WINDOW
    qpos = blk + jnp.arange(WINDOW)[None, :]
    kpos = blk - WINDOW + jnp.arange(2 * WINDOW)[None, :]
    mask = window_mask(qpos[:, :, None], kpos[:, None, :])
    s = jnp.einsum('bnqhgd,bnkhd->bnhgqk', qb, kb, preferred_element_type=F32) * (A_HEAD_DIM ** -0.5)
    p = sink_softmax(s, mask[None, :, None, None], sink)
    o = jnp.einsum('bnhgqk,bnkhd->bnqhgd', p, vb.astype(F32)).reshape(b, t, BRANCH_W)
    buf = min(WINDOW, PAST_LEN)
    return o, k[:, t - buf:], v[:, t - buf:]


def swa_sample(q, k, v, k_buf, v_buf, sink):
    t = q.shape[1]
    buf = k_buf.shape[1]
    kk = jnp.concatenate([k_buf.astype(k.dtype), k], axis=1)
    vv = jnp.concatenate([v_buf.astype(v.dtype), v], axis=1)
    qpos = PAST_LEN + jnp.arange(t)
    kpos = PAST_LEN - buf + jnp.arange(buf + t)
    mask = window_mask(qpos[:, None], kpos[None, :])
    s = jnp.einsum('bqhgd,bkhd->bhgqk', q, kk, preferred_element_type=F32) * (A_HEAD_DIM ** -0.5)
    p = sink_softmax(s, mask, sink)
    o = jnp.einsum('bhgqk,bkhd->bqhgd', p, vv.astype(F32)).reshape(q.shape[0], t, BRANCH_W)
    return o, kk[:, -buf:], vv[:, -buf:]


def causal_conv(x, buf, w, bias):
    t = x.shape[1]
    xp = jnp.concatenate([buf.astype(x.dtype), x], axis=1)
    y = bias + sum(xp[:, j:j + t] * w[j] for j in range(CONV_WIDTH))
    return y, xp[:, -(CONV_WIDTH - 1):]


def rglru(x, h0, wa, ba, wx, bx, lam, seq_start):
    b, t, _ = x.shape
    x = x.astype(F32)
    xb = x.reshape(b, t, LRU_BLOCKS, LRU_BLOCK)
    r = jax.nn.sigmoid(jnp.einsum('btni,nij->btnj', xb, wa.astype(F32)).reshape(b, t, LRU_WIDTH) + ba.astype(F32))
    i = jax.nn.sigmoid(jnp.einsum('btni,nij->btnj', xb, wx.astype(F32)).reshape(b, t, LRU_WIDTH) + bx.astype(F32))
    log_a = -LRU_C * r * jax.nn.softplus(-lam.astype(F32))
    a = jnp.exp(log_a)
    mult = jnp.sqrt(-jnp.expm1(2.0 * log_a))
    if seq_start:
        mult = mult.at[:, 0].set(1.0)
    u = mult * (i * x)
    u = u.at[:, 0].add(a[:, 0] * h0.astype(F32))
    _, h = lax.associative_scan(lambda e1, e2: (e1[0] * e2[0], e2[0] * e1[1] + e2[1]), (a, u), axis=1)
    return h, h[:, -1]


def rotary_every_two(x, pos):
    half = x.shape[-1] // 2
    inv = 1.0 / (RET_THETA ** jnp.linspace(0.0, 1.0, half, dtype=F32))
    ang = pos.astype(F32)[:, None] * inv[None, :]
    cos = jnp.cos(ang)[None, :, None, :]
    sin = jnp.sin(ang)[None, :, None, :]
    x1, x2 = x[..., 0::2], x[..., 1::2]
    return jnp.stack([x1 * cos - x2 * sin, x2 * cos + x1 * sin], axis=-1).reshape(x.shape)


def ret_log_decay():
    return jnp.log1p(-jnp.exp2(-5.0 - jnp.arange(R_HEADS, dtype=F32)))


def ret_chunk(s, qkv):
    q, k, v = qkv
    c = q.shape[1]
    n = jnp.arange(c, dtype=F32)
    lg = ret_log_decay()
    diff = n[:, None] - n[None, :]
    dmat = jnp.where(diff >= 0, jnp.exp(jnp.maximum(diff, 0.0)[None] * lg[:, None, None]), 0.0)
    inner = jnp.einsum('bqhd,bkhd->bhqk', q, k) * dmat
    q_dec = jnp.exp((n[:, None] + 1.0) * lg[None, :])[None, :, :, None]
    k_dec = jnp.exp((c - 1.0 - n)[:, None] * lg[None, :])[None, :, :, None]
    o = jnp.einsum('bhqk,bkhe->bqhe', inner, v) + jnp.einsum('bqhd,bhde->bqhe', q * q_dec, s)
    s_new = jnp.exp(c * lg)[None, :, None, None] * s + jnp.einsum('bkhd,bkhe->bhde', k * k_dec, v)
    return s_new, o


def retention(q, k, v, s0, pos):
    b, t = q.shape[:2]
    q = rotary_every_two(q.astype(F32), pos)
    k = rotary_every_two(k.astype(F32), pos) * (R_HEAD_DIM ** -0.5)
    v = v.astype(F32)
    c = min(RET_CHUNK, t)
    nc = t // c

    def chunks(z):
        return z.reshape(b, nc, c, R_HEADS, R_HEAD_DIM).swapaxes(0, 1)

    s, o = lax.scan(ret_chunk, s0.astype(F32), (chunks(q), chunks(k), chunks(v)))
    return o.swapaxes(0, 1).reshape(b, t, R_HEADS, R_HEAD_DIM), s


def group_norm(o, g):
    mu = jnp.mean(o, axis=-1, keepdims=True)
    var = jnp.mean(jnp.square(o - mu), axis=-1, keepdims=True)
    return (o - mu) * lax.rsqrt(var + EPS) * g.astype(F32)


def cross_attn(u, mem_k, mem_v, w_xq, w_xo):
    b, t, _ = u.shape
    q = (u @ w_xq).reshape(b, t, X_HEADS, X_HEAD_DIM)
    s = jnp.einsum('bthd,bmhd->bhtm', q, mem_k, preferred_element_type=F32) * (X_HEAD_DIM ** -0.5)
    p = jax.nn.softmax(s, axis=-1)
    o = jnp.einsum('bhtm,bmhd->bthd', p, mem_v.astype(F32)).reshape(b, t, X_W)
    return o.astype(u.dtype) @ w_xo


def trunk_layer(h, pos, mem_k, mem_v, win_k, win_v, conv_buf, lru_h, ret_s, seq_start,
                norm_mix, w_in, attn_sink, conv_w, conv_b, lru_wa, lru_ba, lru_wx, lru_bx, lru_lambda,
                ret_gn, w_branch, w_out, norm_cross, w_xq, w_xo, norm_ffn, w_up, w_down):
    b, t, _ = h.shape
    u = rmsnorm(h, norm_mix)
    qa, ka, va, xr, yr, qc, kc, vc, gc, gates = split_in(u @ w_in)
    qa = qa.reshape(b, t, A_KV_HEADS, A_GROUP, A_HEAD_DIM)
    ka = ka.reshape(b, t, A_KV_HEADS, A_HEAD_DIM)
    va = va.reshape(b, t, A_KV_HEADS, A_HEAD_DIM)
    sink = attn_sink.astype(F32).reshape(A_KV_HEADS, A_GROUP)
    if win_k is None:
        oa, new_wk, new_wv = swa_prompt(qa, ka, va, sink)
    else:
        oa, new_wk, new_wv = swa_sample(qa, ka, va, win_k, win_v, sink)
    xc, new_conv = causal_conv(xr, conv_buf, conv_w, conv_b)
    hr, new_lru = rglru(xc, lru_h, lru_wa, lru_ba, lru_wx, lru_bx, lru_lambda, seq_start)
    ob = jax.nn.gelu(yr.astype(F32)) * hr
    rc, new_ret = retention(qc.reshape(b, t, R_HEADS, R_HEAD_DIM), kc.reshape(b, t, R_HEADS, R_HEAD_DIM),
                            vc.reshape(b, t, R_HEADS, R_HEAD_DIM), ret_s, pos)
    oc = jax.nn.silu(gc.astype(F32)) * group_norm(rc, ret_gn).reshape(b, t, BRANCH_W)
    branches = jnp.stack([oa, ob, oc], axis=2).astype(h.dtype)
    gate = jax.nn.sigmoid(gates.astype(F32).reshape(b, t, N_BRANCH, D_MODEL))
    merged = jnp.einsum('btnd,btnd->btd', gate, jnp.einsum('btnw,nwd->btnd', branches, w_branch).astype(F32))
    h = h + merged.astype(h.dtype) @ w_out
    h = h + cross_attn(rmsnorm(h, norm_cross), mem_k, mem_v, w_xq, w_xo)
    f = jax.nn.relu(rmsnorm(h, norm_ffn) @ w_up)
    h = h + (f * f) @ w_down
    return h, new_wk, new_wv, new_conv, new_lru, new_ret


def setup_inputs(seed: int = 0) -> dict:
    key = jax.random.key(seed)
    ks = iter(jax.random.split(key, 48))

    def nrm(shape, scale):
        return scale * jax.random.normal(next(ks), shape, F32)

    wb = min(WINDOW, PAST_LEN)
    u = jax.random.uniform(next(ks), (DEPTH, LRU_WIDTH), dtype=F32, minval=0.9, maxval=0.999)
    a = u ** (1.0 / LRU_C)
    lam = jnp.log(a) - jnp.log1p(-a)
    return {
        'x_prompt': nrm((BATCH, SEQ, D_MODEL), 1.0),
        'x_sample': nrm((DEC_BATCH, DEC_SEQ, D_MODEL), 1.0),
        'mem_prompt': nrm((BATCH, MEM_LEN, D_MODEL), 1.0),
        'cache_win_k': nrm((DEPTH, DEC_BATCH, wb, A_KV_HEADS, A_HEAD_DIM), 1.0),
        'cache_win_v': nrm((DEPTH, DEC_BATCH, wb, A_KV_HEADS, A_HEAD_DIM), 1.0),
        'state_conv': nrm((DEPTH, DEC_BATCH, CONV_WIDTH - 1, LRU_WIDTH), 1.0),
        'state_lru': nrm((DEPTH, DEC_BATCH, LRU_WIDTH), 0.5),
        'state_ret': nrm((DEPTH, DEC_BATCH, R_HEADS, R_HEAD_DIM, R_HEAD_DIM), 0.3),
        'cache_mem_k': nrm((DEPTH, DEC_BATCH, MEM_LEN, X_HEADS, X_HEAD_DIM), 1.0),
        'cache_mem_v': nrm((DEPTH, DEC_BATCH, MEM_LEN, X_HEADS, X_HEAD_DIM), 1.0),
        'norm_mix': 1.0 + nrm((DEPTH, D_MODEL), 0.01),
        'w_in': nrm((DEPTH, D_MODEL, IN_W), D_MODEL ** -0.5),
        'attn_sink': nrm((DEPTH, A_HEADS), 1.0),
        'conv_w': nrm((DEPTH, CONV_WIDTH, LRU_WIDTH), CONV_WIDTH ** -0.5),
        'conv_b': nrm((DEPTH, LRU_WIDTH), 0.01),
        'lru_wa': nrm((DEPTH, LRU_BLOCKS, LRU_BLOCK, LRU_BLOCK), LRU_BLOCK ** -0.5),
        'lru_ba': nrm((DEPTH, LRU_WIDTH), 0.01),
        'lru_wx': nrm((DEPTH, LRU_BLOCKS, LRU_BLOCK, LRU_BLOCK), LRU_BLOCK ** -0.5),
        'lru_bx': nrm((DEPTH, LRU_WIDTH), 0.01),
        'lru_lambda': lam,
        'ret_gn': 1.0 + nrm((DEPTH, R_HEADS, R_HEAD_DIM), 0.01),
        'w_branch': nrm((DEPTH, N_BRANCH, BRANCH_W, D_MODEL), BRANCH_W ** -0.5),
        'w_out': nrm((DEPTH, D_MODEL, D_MODEL), D_MODEL ** -0.5),
        'norm_cross': 1.0 + nrm((DEPTH, D_MODEL), 0.01),
        'w_xq': nrm((DEPTH, D_MODEL, X_W), D_MODEL ** -0.5),
        'w_xk': nrm((DEPTH, D_MODEL, X_W), D_MODEL ** -0.5),
        'w_xv': nrm((DEPTH, D_MODEL, X_W), D_MODEL ** -0.5),
        'w_xo': nrm((DEPTH, X_W, D_MODEL), X_W ** -0.5),
        'norm_ffn': 1.0 + nrm((DEPTH, D_MODEL), 0.01),
        'w_up': nrm((DEPTH, D_MODEL, D_FF), D_MODEL ** -0.5),
        'w_down': nrm((DEPTH, D_FF, D_MODEL), D_FF ** -0.5),
        'norm_final': 1.0 + nrm((D_MODEL,), 0.01),
    }


def reference(x_prompt, x_sample, mem_prompt, cache_win_k, cache_win_v, state_conv, state_lru, state_ret,
              cache_mem_k, cache_mem_v, norm_mix, w_in, attn_sink, conv_w, conv_b, lru_wa, lru_ba, lru_wx, lru_bx,
              lru_lambda, ret_gn, w_branch, w_out, norm_cross, w_xq, w_xk, w_xv, w_xo, norm_ffn, w_up, w_down,
              norm_final):
    bp, tp, _ = x_prompt.shape
    ts = x_sample.shape[1]
    mlen = mem_prompt.shape[1]
    pos_p = jnp.arange(tp)
    pos_s = PAST_LEN + jnp.arange(ts)
    conv0 = jnp.zeros((bp, CONV_WIDTH - 1, LRU_WIDTH), x_prompt.dtype)
    lru0 = jnp.zeros((bp, LRU_WIDTH), F32)
    ret0 = jnp.zeros((bp, R_HEADS, R_HEAD_DIM, R_HEAD_DIM), F32)
    hp, hs = x_prompt, x_sample
    p_wk, p_wv, p_conv, p_lru, p_ret, p_mk, p_mv = [], [], [], [], [], [], []
    s_wk, s_wv, s_conv, s_lru, s_ret = [], [], [], [], []
    for l in range(DEPTH):
        lw = (norm_mix[l], w_in[l], attn_sink[l], conv_w[l], conv_b[l], lru_wa[l], lru_ba[l], lru_wx[l],
              lru_bx[l], lru_lambda[l], ret_gn[l], w_branch[l], w_out[l], norm_cross[l], w_xq[l], w_xo[l],
              norm_ffn[l], w_up[l], w_down[l])
        mk = (mem_prompt @ w_xk[l]).reshape(bp, mlen, X_HEADS, X_HEAD_DIM)
        mv = (mem_prompt @ w_xv[l]).reshape(bp, mlen, X_HEADS, X_HEAD_DIM)
        hp, wk, wv, cv, hl, sr = trunk_layer(hp, pos_p, mk, mv, None, None, conv0, lru0, ret0, True, *lw)
        p_wk.append(wk); p_wv.append(wv); p_conv.append(cv); p_lru.append(hl); p_ret.append(sr)
        p_mk.append(mk); p_mv.append(mv)
        hs, wk, wv, cv, hl, sr = trunk_layer(hs, pos_s, cache_mem_k[l], cache_mem_v[l], cache_win_k[l],
                                             cache_win_v[l], state_conv[l], state_lru[l], state_ret[l], False, *lw)
        s_wk.append(wk); s_wv.append(wv); s_conv.append(cv); s_lru.append(hl); s_ret.append(sr)
    y_prompt = rmsnorm(hp, norm_final)
    y_sample = rmsnorm(hs, norm_final)
    return (y_prompt, y_sample,
            jnp.stack(p_wk), jnp.stack(p_wv), jnp.stack(p_conv), jnp.stack(p_lru), jnp.stack(p_ret),
            jnp.stack(p_mk), jnp.stack(p_mv),
            jnp.stack(s_wk), jnp.stack(s_wv), jnp.stack(s_conv), jnp.stack(s_lru), jnp.stack(s_ret))
```

```python
import contextlib
import numpy as np
import ml_dtypes
import concourse.bass as bass
import concourse.mybir as mybir
from concourse.bass_utils import run_bass_kernel_spmd

F32 = mybir.dt.float32
BF16 = mybir.dt.bfloat16
AF = mybir.ActivationFunctionType
ALU = mybir.AluOpType
AX = mybir.AxisListType

ENGS = ("pe", "dve", "act", "pool", "sp")
DMA_SLOTS = {"sp": 30, "pool": 24, "act": 8}
SAME_ENGINE_SYNC = ("dve", "act", "pool")


class Tr:
    __slots__ = ("name", "w", "r", "prev_r")

    def __init__(self, name):
        self.name = name
        self.w = {}
        self.r = {}
        self.prev_r = {}


def _flat(d):
    out = []
    for k, v in d.items():
        if k == "dma":
            out.extend(v)
        else:
            out.append(v)
    return out


def _add(d, ins):
    if ins.is_dma:
        d.setdefault("dma", []).append(ins)
    else:
        d[ins.eng] = ins


class Ins:
    __slots__ = ("eng", "fn", "deps", "is_dma", "waited", "semval", "slot", "dval", "inc")

    def __init__(self, eng, fn, is_dma):
        self.eng = eng
        self.fn = fn
        self.deps = []
        self.is_dma = is_dma
        self.waited = False
        self.semval = 0
        self.slot = None
        self.dval = 0
        self.inc = 16


class K:
    def __init__(self, nc):
        self.nc = nc
        self.q = {e: [] for e in ENGS}
        self.stack = contextlib.ExitStack()
        self.dma_count = {e: 0 for e in DMA_SLOTS}
        self.slot_last = {e: [None] * n for e, n in DMA_SLOTS.items()}
        self.sb_off = 16512
        self.n_t = 0
        self.ps_rr = 0

    def sbuf(self, name, shape, dtype, off=None):
        esz = 2 if dtype == BF16 else 4
        nbytes = int(np.prod(shape[1:])) * esz
        if off is None:
            off = self.sb_off
            self.sb_off = (off + nbytes + 31) // 32 * 32
            assert self.sb_off <= SB_END, ("SBUF overflow", name, self.sb_off)
        else:
            assert off + nbytes <= SB_END, ("SBUF overflow", name)
        self.n_t += 1
        h = self.nc.alloc_sbuf_tensor_at(f"{name}_{self.n_t}", list(shape), dtype, offset=off)
        return h, Tr(name)

    def psum_banks(self):
        self.banks = []
        for i in range(8):
            h = self.nc.alloc_psum_tensor(f"psb{i}", [128, 512], F32)
            self.banks.append((h, Tr(f"psb{i}")))
        return self.banks

    def ps(self):
        b = self.banks[self.ps_rr % 6]
        self.ps_rr += 1
        return b

    def _record(self, ins, reads, writes, join):
        deps = ins.deps
        for t in reads:
            deps.extend(_flat(t.w))
        for t in writes:
            if join and not t.r:
                deps.extend(_flat(t.prev_r))
            else:
                deps.extend(_flat(t.w))
                deps.extend(_flat(t.r))
        for t in reads:
            _add(t.r, ins)
        for t in writes:
            if join and not t.r:
                _add(t.w, ins)
            else:
                t.prev_r = t.r
                t.r = {}
                t.w = {}
                _add(t.w, ins)
        self.q[ins.eng].append(ins)
        return ins

    def op(self, eng, fn, reads=(), writes=(), join=False):
        return self._record(Ins(eng, fn, False), reads, writes, join)

    def dma(self, q, out, in_, reads=(), writes=(), join=False, **kw):
        ins = Ins(q, (lambda e: e.dma_start(out=out, in_=in_, **kw)), True)
        n = self.dma_count[q]
        self.dma_count[q] = n + 1
        ns = DMA_SLOTS[q]
        slot = n % ns
        ins.slot = (q, slot)
        ins.dval = 16 * (n // ns + 1)
        prev = self.slot_last[q][slot]
        if prev is not None:
            ins.deps.append(prev)
        self.slot_last[q][slot] = ins
        return self._record(ins, reads, writes, join)

    def barrier(self):
        lastc = []
        for e in ENGS:
            for ins in reversed(self.q[e]):
                if not ins.is_dma and ins.fn is not None:
                    lastc.append(ins)
                    break
        dmas = [s for q in self.slot_last for s in self.slot_last[q] if s is not None]
        for e in ENGS:
            b = Ins(e, None, False)
            b.deps = list(lastc) + list(dmas)
            self.q[e].append(b)

    def emit(self):
        nc = self.nc
        st = self.stack
        esem = {e: st.enter_context(nc.semaphore(f"es_{e}")) for e in ENGS}
        dsem = {(q, i): st.enter_context(nc.semaphore(f"ds_{q}{i}")) for q, n in DMA_SLOTS.items() for i in range(n)}
        fin = Ins("sp", None, False)
        fin.deps = [s for q in self.slot_last for s in self.slot_last[q] if s is not None]
        self.q["sp"].append(fin)
        for e in ENGS:
            for ins in self.q[e]:
                for d in ins.deps:
                    if d.is_dma or d.fn is None:
                        continue
                    if d.eng == e and e not in SAME_ENGINE_SYNC:
                        continue
                    d.waited = True
        for e in ENGS:
            c = 0
            for ins in self.q[e]:
                if ins.is_dma or ins.fn is None:
                    continue
                if ins.waited:
                    c += 1
                    ins.semval = c
        stats = {}

        def run(e, eh):
            seen = {}
            nw = 0
            for ins in self.q[e]:
                for d in ins.deps:
                    if d.is_dma:
                        key = d.slot
                        sem = dsem[key]
                        val = d.dval
                    else:
                        if d.fn is None:
                            continue
                        if d.eng == e and e not in SAME_ENGINE_SYNC:
                            continue
                        key = d.eng
                        sem = esem[d.eng]
                        val = d.semval
                    if seen.get(key, 0) >= val:
                        continue
                    seen[key] = val
                    eh.wait_ge(sem, val)
                    nw += 1
                if ins.fn is None:
                    continue
                bi = ins.fn(eh)
                if ins.is_dma:
                    bi.then_inc(dsem[ins.slot], 16)
                elif ins.waited:
                    bi.then_inc(esem[e], 1)
            stats[e] = (len(self.q[e]), nw)

        with nc.Block() as block:
            @block.tensor
            def _(eh):
                run("pe", eh)

            @block.vector
            def _(eh):
                run("dve", eh)

            @block.scalar
            def _(eh):
                run("act", eh)

            @block.gpsimd
            def _(eh):
                run("pool", eh)

            @block.sync
            def _(eh):
                run("sp", eh)
        self.stats = stats
        st.close()


def OPF(name, *a, **kw):
    return lambda e: getattr(e, name)(*a, **kw)


def bcast(ap, axis, n):
    dims = [list(d) for d in ap.ap]
    dims.insert(axis, [0, n])
    return bass.AP(ap.tensor, ap.offset, dims)


SB_END = 229312
D = 2048
KD = 16
L = 2
SEQ = 4096
NGRP = 4
TP = 1024
TS = 128
NSEQ = 16
TTOT = SEQ + TS
IN_W = 13824
C_QA, C_KA, C_VA, C_XR, C_YR, C_QC, C_KC, C_VC, C_GC, C_G = 0, 1024, 1280, 1536, 2560, 3584, 4608, 5632, 6656, 7680
EPS = 1e-6
NEG = -1e30
GAM = [1.0 - 2.0 ** (-5.0 - h) for h in range(4)]


def grp_cols(g):
    if g == 0:
        return 0, TP + TS
    return TP + TS + (g - 1) * TP, TP


def tgroups(T):
    out = []
    t = 0
    while t < T:
        n = min(512, T - t)
        out.append((t, n))
        t += n
    return out


def weight_plan():
    plan = {}
    off = [0]

    def add(key, src, r0, nr, cols):
        cols = np.asarray(cols, dtype=np.int64)
        assert (nr // 128) * len(cols) <= 8192
        plan[key] = (src, r0, nr, cols, off[0])
        off[0] += (nr // 128) * len(cols)

    ar = np.arange
    kd = []
    for kv in range(4):
        c = C_KA + kv * 64 + ar(64)
        kd += [c, c]
    add("a_kdup", "w_in", 0, D, np.concatenate(kd))
    add("a_vk", "w_in", 0, D, np.concatenate([C_VA + ar(256), C_KA + ar(256)]))
    for i in range(2):
        add(f"a_q{i}", "w_in", 0, D, C_QA + i * 512 + ar(512))
    for i in range(4):
        cols = np.concatenate([C_XR + (2 * i) * 128 + ar(128), C_YR + (2 * i) * 128 + ar(128),
                               C_XR + (2 * i + 1) * 128 + ar(128), C_YR + (2 * i + 1) * 128 + ar(128)])
        add(f"b_xy{i}", "w_in", 0, D, cols)
    for h in range(4):
        add(f"c_qk{h}", "w_in", 0, D, np.concatenate([C_QC + h * 256 + ar(256), C_KC + h * 256 + ar(256)]))
        add(f"c_vg{h}", "w_in", 0, D, np.concatenate([C_VC + h * 256 + ar(256), C_GC + h * 256 + ar(256)]))
    for b in range(3):
        for mb in range(4):
            pass
        for mb in range(8):
            add(f"d_g{b}_{mb}", "w_in", 0, D, C_G + b * D + mb * 256 + ar(256))
            add(f"d_w{b}_{mb}", f"w_branch{b}", 0, 1024, mb * 256 + ar(256))
    for mb in range(4):
        add(f"e_o{mb}", "w_out", 0, D, mb * 512 + ar(512))
    add("x_q", "w_xq", 0, D, ar(512))
    add("x_k", "w_xk", 0, D, ar(512))
    add("x_v", "w_xv", 0, D, ar(512))
    add("x_o", "w_xo", 0, 512, ar(2048))
    for e in range(8):
        for i in range(2):
            add(f"f_u{e}_{i}", "w_up", 0, D, e * 1024 + i * 512 + ar(512))
        for i in range(2):
            add(f"f_d{e}_{i}", "w_down", e * 1024, 1024, i * 1024 + ar(1024))
    return plan, off[0]


def const_tables():
    c = {}
    i = np.arange(128)[:, None]
    j = np.arange(256)[None, :]
    full = (j > i) & (j <= i + 128)
    c["maskA_full"] = np.where(full, 0.0, NEG).astype(np.float32)
    c["maskA_first"] = np.where(full & (j >= 128), 0.0, NEG).astype(np.float32)
    s = np.arange(128) // 8
    t = np.arange(128) % 8
    ms = np.zeros((128, 256), bool)
    ms[:, :128] = np.arange(128)[None, :] >= (t[:, None] + 1)
    ms[:, 128:] = (s[:, None] == s[None, :]) & (t[None, :] <= t[:, None])
    c["maskA_samp"] = np.where(ms, 0.0, NEG).astype(np.float32)
    kk = np.arange(128)[:, None]
    qq = np.arange(128)[None, :]
    dtp = np.zeros((128, 4, 128), np.float64)
    dts = np.zeros((128, 4, 128), np.float64)
    for h in range(4):
        lg = np.log(GAM[h])
        dtp[:, h, :] = np.where(qq >= kk, np.exp(np.maximum(qq - kk, 0) * lg), 0.0)
        same = (s[:, None] == s[None, :]) & (t[None, :] >= t[:, None])
        dts[:, h, :] = np.where(same, np.exp(np.maximum(t[None, :] - t[:, None], 0) * lg), 0.0)
    c["DTp"] = dtp.astype(np.float32)
    c["DTs"] = dts.astype(np.float32)
    dec = np.zeros((128, 16), np.float64)
    n = np.arange(128)
    for h in range(4):
        lg = np.log(GAM[h])
        dec[:, h] = np.exp((n + 1.0) * lg)
        dec[:, 4 + h] = np.exp((127.0 - n) * lg)
        dec[:, 8 + h] = np.exp((t + 1.0) * lg)
        dec[:, 12 + h] = np.exp((7.0 - t) * lg)
    c["dec"] = dec.astype(np.float32)
    inv = (1.0 / (10000.0 ** np.linspace(0.0, 1.0, 128, dtype=np.float32))).astype(np.float32)
    pos = np.zeros((33, 128), np.float32)
    pos[:32] = np.arange(4096, dtype=np.float32).reshape(32, 128)
    pos[32] = 8192.0 + t
    ang = (pos[:, :, None] * inv[None, None, :]).astype(np.float32)
    cs, sn = np.cos(ang).astype(np.float32), np.sin(ang).astype(np.float32)
    c["rot"] = np.stack([cs, cs / 16.0, sn, sn / 16.0], axis=2).astype(np.float32)
    c["ident"] = np.eye(128, dtype=np.float32)
    bm = (s[None, :] == np.arange(16)[:, None]).astype(np.float32)
    c["bm"] = np.broadcast_to(bm[None], (128, 16, 128)).copy()
    c["bmv"] = (s[:, None] == np.arange(16)[None, :]).astype(np.float32)
    return c


class Cfg:
    NL = 2
    NG = 4
    DBG = False
    STOP = None


def build_program(cfg):
    nc = bass.Bass("TRN2", target_bir_lowering=False)
    k = K(nc)
    plan, wtot = weight_plan()
    NL, NG = cfg.NL, cfg.NG
    dbg_out = {}

    def din(name, shape, dt=F32):
        return nc.dram_tensor(name, list(shape), dt, kind="ExternalInput").ap()

    def dout(name, shape):
        return nc.dram_tensor(name, list(shape), F32, kind="ExternalOutput").ap()

    xT = din("xT", [128, KD, TTOT])
    memT = din("memT", [128, KD, 256])
    wl = [din(f"wl{l}", [128, wtot]) for l in range(cfg.NL)]
    small = din("small", [L, 128, 16 * 4 + 8 * 4 + 8 * 4 + 16])
    bdw = din("bdw", [L, 2, 128, 8, 128])
    gnb = din("gnb", [L, 128, 1024])
    ctab = {n: din("c_" + n, v.shape) for n, v in const_tables().items()}
    kcT = din("kcT", [L, 64, NSEQ, 4, 128])
    vc = din("vc", [L, 128, NSEQ, 256])
    kc_tm = din("kc_tm", [L, NSEQ, 128, 256])
    vc_tm = din("vc_tm", [L, NSEQ, 128, 256])
    sconv = din("sconv", [L, 128, 8, NSEQ, 3])
    slru = din("slru", [L, 128, 8, NSEQ])
    sret = din("sret", [L, NSEQ, 4, 256, 256])
    mkT = din("mkT", [L, 128, NSEQ, 4, 256])
    mv = din("mv", [L, 128, NSEQ, 2, 512])

    yT = dout("yT", [128, KD, TTOT])
    o_pwk = dout("o_pwk", [L, 128, 256])
    o_pwv = dout("o_pwv", [L, 128, 256])
    o_pconv = dout("o_pconv", [L, 128, 8, 3])
    o_plru = dout("o_plru", [L, 128, 8])
    o_pret = dout("o_pret", [L, 128, 4 * 2 * 256])
    o_pmk = dout("o_pmk", [L, 256, 512])
    o_pmv = dout("o_pmv", [L, 256, 512])
    o_swk = dout("o_swk", [L, NSEQ, 128, 256])
    o_swv = dout("o_swv", [L, NSEQ, 128, 256])
    o_sconv = dout("o_sconv", [L, 128, 8, NSEQ, 3])
    o_slru = dout("o_slru", [L, 128, 8, NSEQ])
    o_sret = dout("o_sret", [L, NSEQ, 4, 256, 256])
    hscr = nc.dram_tensor("hscr", [128, KD, TTOT], F32, kind="Internal").ap()
    hscr_tr = [[Tr(f"hscr{g}_{kc}") for kc in range(KD)] for g in range(NGRP)]
    sscr = nc.dram_tensor("sscr", [128, 2048], F32, kind="Internal").ap()
    sscr_t = Tr("sscr")

    def dbg(name, ap_sb, tr, shape):
        if not cfg.DBG:
            return
        o = dout("dbg_" + name, shape)
        dbg_out[name] = shape
        k.dma("pool", o, ap_sb, reads=tr)

    banks = k.psum_banks()
    TM = TP + TS
    ident_f, ident_f_t = k.sbuf("ident_f", [128, 128], F32)
    ident_b, ident_b_t = k.sbuf("ident_b", [128, 128], BF16)
    ones_b, ones_b_t = k.sbuf("ones_b", [128, 128], BF16)
    maskA = {n: k.sbuf(n, [128, 256], F32) for n in ("maskA_full", "maskA_first", "maskA_samp")}
    DTp, DTp_t = k.sbuf("DTp", [128, 4, 128], F32)
    DTs, DTs_t = k.sbuf("DTs", [128, 4, 128], F32)
    dec, dec_t = k.sbuf("dec", [128, 16], F32)
    bm, bm_t = k.sbuf("bm", [128, 16, 128], BF16)
    bmv, bmv_t = k.sbuf("bmv", [128, 16], F32)
    eps_t, eps_tt = k.sbuf("eps", [128, 1], F32)
    smallt, small_t = k.sbuf("small", [128, 144], F32)
    cneg, cneg_t = k.sbuf("cneg", [128, 8], F32)
    bdw_t = [k.sbuf(f"bdw{i}", [128, 8, 128], BF16) for i in range(2)]
    kcar, kcar_t = k.sbuf("kcar", [128, 4, 128], BF16)
    vcar, vcar_t = k.sbuf("vcar", [128, 256], BF16)
    convcar, convcar_t = k.sbuf("convcar", [128, 8, 3], F32)
    hcar, hcar_t = k.sbuf("hcar", [128, 8], F32)
    memK, memK_t = k.sbuf("memK", [128, 4, 256], BF16)
    memV, memV_t = k.sbuf("memV", [128, 2, 512], BF16)
    NRING = 2
    ring = [k.sbuf(f"wring{i}", [128, 8192], BF16) for i in range(NRING)]
    ring_i = [0]
    uT, _ = k.sbuf("uT", [128, KD, TM], BF16)
    uT_t = [Tr(f"uT{kc}") for kc in range(KD)]
    R0 = k.sb_off
    hT, _ = k.sbuf("hT", [128, KD, TM], F32)
    hT_t = [Tr(f"hT{kc}") for kc in range(KD)]
    X0 = k.sb_off
    XSZ = SB_END - X0
    print("SBUF: R0", R0, "X0", X0, "X size", XSZ)

    Sq, Id, Cp = AF.Square, AF.Identity, AF.Copy

    def wload(l, key):
        src, r0, nr, cols, off = plan[key]
        KC, NCc = nr // 128, len(cols)
        h, tr = ring[ring_i[0] % NRING]
        ring_i[0] += 1
        k.dma("pool", h[:, 0:KC * NCc], wl[l][:, off:off + KC * NCc], writes=[tr])
        return h[:, 0:KC * NCc].rearrange("p (k n) -> p k n", k=KC), tr, KC, NCc

    def wload2(l, keyA, keyB):
        sa, ra_, nra, ca, offa = plan[keyA]
        sb_, rb_, nrb, cb, offb = plan[keyB]
        KA, NA, KB, NB = nra // 128, len(ca), nrb // 128, len(cb)
        assert offb == offa + KA * NA and KA * NA + KB * NB <= 8192
        h, tr = ring[ring_i[0] % NRING]
        ring_i[0] += 1
        tot = KA * NA + KB * NB
        k.dma("pool", h[:, 0:tot], wl[l][:, offa:offa + tot], writes=[tr])
        va = h[:, 0:KA * NA].rearrange("p (k n) -> p k n", k=KA)
        vb = h[:, KA * NA:tot].rearrange("p (k n) -> p k n", k=KB)
        return va, vb, tr, KA, KB

    def mm(out, lhsT, rhs, start, stop, reads, pst):
        k.op("pe", OPF("matmul", out, lhsT=lhsT, rhs=rhs, start=start, stop=stop), reads=reads, writes=[pst], join=not start)

    def dense_fm(w, wtr, KC, m_list, xsrc, xtrs, T, evac):
        for m in m_list:
            for (t0, tn) in tgroups(T):
                ps, pst = k.ps()
                for kc in range(KC):
                    mm(ps[:, 0:tn], w[:, kc, m * 128:(m + 1) * 128], xsrc(kc, t0, tn), kc == 0, kc == KC - 1, [wtr, xtrs[kc]], pst)
                evac(m, t0, tn, ps, pst)

    def dense_tm(w, wtr, KC, NCc, xsrc, xtrs, tiles, evac):
        for tt in tiles:
            ps, pst = k.ps()
            for kc in range(KC):
                mm(ps[:, 0:NCc], xsrc(kc, tt * 128, 128), w[:, kc, 0:NCc], kc == 0, kc == KC - 1, [wtr, xtrs[kc]], pst)
            evac(tt, ps, pst)

    uT_src = lambda kc, t0, tn: uT[:, kc, t0:t0 + tn]

    def rmsnorm_to_uT(l, T, gcol, src_h, src_tr, final_out=None):
        sq, sq_t = k.sbuf("sq", [128, 2, TM], BF16, off=X0)
        sq_tr = [Tr("sq0"), Tr("sq1")]
        rstd, rstd_t = k.sbuf("rstd", [128, TM], F32, off=X0 + 2 * TM * 2)
        tg = tgroups(T)
        pss = [banks[6], banks[7], k.ps()][:len(tg)]
        for kc in range(KD):
            j = kc % 2
            k.op("act", OPF("activation", out=sq[:, j, 0:T], in_=src_h[:, kc, 0:T], func=Sq), reads=[src_tr[kc]], writes=[sq_tr[j]])
            for i, (t0, tn) in enumerate(tg):
                ps, pst = pss[i]
                mm(ps[:, 0:tn], ones_b[:, :], sq[:, j, t0:t0 + tn], kc == 0, kc == KD - 1, [sq_tr[j], ones_b_t], pst)
        for i, (t0, tn) in enumerate(tg):
            ps, pst = pss[i]
            k.op("act", OPF("activation", out=rstd[:, t0:t0 + tn], in_=ps[:, 0:tn], func=AF.Sqrt, scale=1.0 / D, bias=eps_t[:, 0:1]), reads=[pst, eps_tt], writes=[rstd_t], join=(i > 0))
        k.op("dve", OPF("reciprocal", out=rstd[:, 0:T], in_=rstd[:, 0:T]), reads=[rstd_t], writes=[rstd_t])
        for kc in range(KD):
            if final_out is None:
                k.op("dve", OPF("scalar_tensor_tensor", out=uT[:, kc, 0:T], in0=src_h[:, kc, 0:T], scalar=smallt[:, gcol + kc:gcol + kc + 1], in1=rstd[:, 0:T], op0=ALU.mult, op1=ALU.mult),
                     reads=[src_tr[kc], rstd_t, small_t], writes=[uT_t[kc]])
            else:
                yo, yc0 = final_out
                k.op("dve", OPF("scalar_tensor_tensor", out=src_h[:, kc, 0:T], in0=src_h[:, kc, 0:T], scalar=smallt[:, gcol + kc:gcol + kc + 1], in1=rstd[:, 0:T], op0=ALU.mult, op1=ALU.mult),
                     reads=[src_tr[kc], rstd_t, small_t], writes=[src_tr[kc]])
                k.dma("sp", yo[:, kc, yc0:yc0 + T], src_h[:, kc, 0:T], reads=[src_tr[kc]])

    k.op("dve", OPF("memset", eps_t[:], EPS), writes=[eps_tt])
    k.dma("sp", ident_f[:], ctab["ident"], writes=[ident_f_t])
    k.dma("pool", ident_b[:], ctab["ident"], writes=[ident_b_t])
    k.op("dve", OPF("memset", ones_b[:], 1.0), writes=[ones_b_t])
    for n in maskA:
        k.dma("sp", maskA[n][0][:], ctab[n], writes=[maskA[n][1]])
    k.dma("sp", DTp[:], ctab["DTp"], writes=[DTp_t])
    k.dma("sp", DTs[:], ctab["DTs"], writes=[DTs_t])
    k.dma("sp", dec[:], ctab["dec"], writes=[dec_t])
    k.dma("pool", bm[:], ctab["bm"], writes=[bm_t])
    k.dma("sp", bmv[:], ctab["bmv"], writes=[bmv_t])

    for l in range(NL):
        k.dma("sp", smallt[:], small[l], writes=[small_t])
        for i in range(2):
            k.dma("pool", bdw_t[i][0][:], bdw[l, i], writes=[bdw_t[i][1]])
        G1, G2, G3, GF, CW, CB, BA, BX, LAM, SINK = 0, 16, 32, 48, 64, 96, 104, 112, 120, 128
        k.op("act", OPF("activation", out=cneg[:], in_=smallt[:, LAM:LAM + 8], func=AF.Exp, scale=-1.0), reads=[small_t], writes=[cneg_t])
        k.op("act", OPF("activation", out=cneg[:], in_=cneg[:], func=AF.Ln, bias=1.0), reads=[cneg_t], writes=[cneg_t])
        k.op("dve", OPF("tensor_scalar", out=cneg[:], in0=cneg[:], scalar1=-8.0, scalar2=None, op0=ALU.mult), reads=[cneg_t], writes=[cneg_t])
        k.op("dve", OPF("memset", kcar[:], 0.0), writes=[kcar_t])
        k.op("dve", OPF("memset", vcar[:], 0.0), writes=[vcar_t])
        k.op("dve", OPF("memset", convcar[:], 0.0), writes=[convcar_t])
        k.op("dve", OPF("memset", hcar[:], 0.0), writes=[hcar_t])

        for g in range(NG):
            c0, T = grp_cols(g)
            has_s = (g == 0)
            ntile = T // 128
            src = xT if l == 0 else hscr
            k.barrier()
            for kc in range(KD):
                rd = [hscr_tr[g][kc]] if l > 0 else []
                k.dma("sp", hT[:, kc, 0:T], src[:, kc, c0:c0 + T], reads=rd, writes=[hT_t[kc]])
            rmsnorm_to_uT(l, T, G1, hT, hT_t)
            dbg(f"u_{l}_{g}", uT[:, :, 0:T], uT_t, [128, KD, T])
            if cfg.STOP == "u":
                k.emit()
                return nc, k, dbg_out
            k.barrier()
            MRG = SB_END - KD * TM * 2
            assert MRG >= X0, (MRG, X0)
            mergedT, _ = k.sbuf("mergedT", [128, KD, TM], BF16, off=MRG)
            mg_t = [Tr(f"mg{m}") for m in range(KD)]
            obT, _ = k.sbuf("obT", [128, 8, TM], BF16, off=R0)
            ob_t = [Tr(f"ob{c}") for c in range(8)]
            wa = [R0 + 8 * TM * 2, MRG]

            def walloc(name, shape, dt, region=None):
                r = wa if region is None else region
                esz = 2 if dt == BF16 else 4
                nb = (int(np.prod(shape[1:])) * esz + 31) // 32 * 32
                assert r[0] + nb <= r[1], ("work area overflow", name, r[0] + nb - r[1])
                h, t = k.sbuf(name, shape, dt, off=r[0])
                r[0] += nb
                return h, t

            SCL = 0.125
            sg2 = [walloc(f"sg_{i}", [128, 512], F32) for i in range(2)]
            wa_save = wa[0]

            def softmax_rows(ps, pst, mask, mask_t, sinkcol, scale, tmp):
                Sm, Sm_t, P, P_t, Pn, Pn_t, PnT, PnT_t, stt, stt_t = tmp
                if mask is not None:
                    k.op("dve", OPF("scalar_tensor_tensor", out=Sm[:, :], in0=ps[:, 0:256], scalar=scale, in1=mask[:, :], op0=ALU.mult, op1=ALU.add), reads=[pst, mask_t], writes=[Sm_t])
                else:
                    k.op("dve", OPF("tensor_scalar", out=Sm[:, :], in0=ps[:, 0:256], scalar1=scale, scalar2=None, op0=ALU.mult), reads=[pst], writes=[Sm_t])
                k.op("dve", OPF("reduce_max", out=stt[:, 0:1], in_=Sm[:, :], axis=AX.X), reads=[Sm_t], writes=[stt_t])
                if sinkcol is not None:
                    k.op("dve", OPF("tensor_scalar", out=stt[:, 1:2], in0=stt[:, 0:1], scalar1=sinkcol, scalar2=-1.0, op0=ALU.max, op1=ALU.mult), reads=[stt_t, small_t], writes=[stt_t])
                else:
                    k.op("dve", OPF("tensor_scalar", out=stt[:, 1:2], in0=stt[:, 0:1], scalar1=-1.0, scalar2=None, op0=ALU.mult), reads=[stt_t], writes=[stt_t])
                k.op("act", OPF("activation", out=P[:, :], in_=Sm[:, :], func=AF.Exp, bias=stt[:, 1:2], scale=1.0, accum_out=stt[:, 2:3]), reads=[Sm_t, stt_t], writes=[P_t, stt_t])
                if sinkcol is not None:
                    k.op("act", OPF("activation", out=stt[:, 3:4], in_=sinkcol, func=AF.Exp, bias=stt[:, 1:2], scale=1.0), reads=[stt_t, small_t], writes=[stt_t])
                    k.op("dve", OPF("tensor_tensor", out=stt[:, 2:3], in0=stt[:, 2:3], in1=stt[:, 3:4], op=ALU.add), reads=[stt_t], writes=[stt_t])
                k.op("dve", OPF("reciprocal", out=stt[:, 4:5], in_=stt[:, 2:3]), reads=[stt_t], writes=[stt_t])
                k.op("dve", OPF("tensor_scalar", out=Pn[:, :], in0=P[:, :], scalar1=stt[:, 4:5], scalar2=None, op0=ALU.mult), reads=[P_t, stt_t], writes=[Pn_t])
                pt, ptt = k.ps()
                ptb = pt[:, :].bitcast(BF16)
                for c in range(2):
                    k.op("pe", OPF("transpose", out=ptb[:, c * 128:(c + 1) * 128], in_=Pn[:, c * 128:(c + 1) * 128], identity=ident_b[:, :]), reads=[Pn_t, ident_b_t], writes=[ptt], join=(c > 0))
                k.op("act", OPF("activation", out=PnT[:, :], in_=ptb[:, 0:256], func=Cp), reads=[ptt], writes=[PnT_t])

            def sm_tmp(tag, region=None):
                Sm, Sm_t = walloc("Sm" + tag, [128, 256], F32, region)
                P, P_t = walloc("P" + tag, [128, 256], F32, region)
                Pn, Pn_t = walloc("Pn" + tag, [128, 256], BF16, region)
                PnT, PnT_t = walloc("PnT" + tag, [128, 256], BF16, region)
                stt, stt_t = walloc("stt" + tag, [128, 8], F32, region)
                return (Sm, Sm_t, P, P_t, Pn, Pn_t, PnT, PnT_t, stt, stt_t)

            ra = [MRG, SB_END]
            kT, kT_t = walloc("kT", [128, 4, 128 + TP], BF16, ra)
            V, V_t = walloc("V", [128, 9, 256], BF16, ra)
            ksT, ksT_t = walloc("ksT", [128, 4, 128], BF16, ra)
            vs, vs_t = walloc("vs", [128, 256], BF16, ra)
            if has_s:
                KcT, KcT_t = walloc("KcT", [128, NSEQ, 4, 128], BF16, ra)
                Vc, Vc_t = walloc("Vc", [128, NSEQ, 256], BF16)
                qz, qz_t = walloc("qz", [128, NSEQ, 128], BF16)
                for hf in range(2):
                    k.dma("pool", KcT[hf * 64:(hf + 1) * 64], kcT[l], writes=[KcT_t], join=(hf > 0))
                k.dma("pool", Vc[:], vc[l], writes=[Vc_t])
                k.dma("sp", o_swk[l][:, 0:120, :], kc_tm[l][:, 8:128, :])
                k.dma("sp", o_swv[l][:, 0:120, :], vc_tm[l][:, 8:128, :])
            qTb = [walloc(f"qTb{i}", [128, TM], BF16) for i in range(2)]
            smt = [sm_tmp(f"a{i}") for i in range(2)]
            kvf, kvf_t = walloc("kvf", [128, 512], F32)
            k.op("act", OPF("activation", out=kT[:, :, 0:128], in_=kcar[:, :, :], func=Cp), reads=[kcar_t], writes=[kT_t])
            k.op("act", OPF("activation", out=V[:, 0, :], in_=vcar[:, :], func=Cp), reads=[vcar_t], writes=[V_t])
            w, wtr, KC, NCc = wload(l, "a_kdup")

            def ev_k(m, t0, tn, ps, pst):
                if t0 < TP:
                    k.op("act", OPF("activation", out=kT[:, m, 128 + t0:128 + t0 + tn], in_=ps[:, 0:tn], func=Cp), reads=[pst], writes=[kT_t], join=True)
                else:
                    k.op("act", OPF("activation", out=ksT[:, m, :], in_=ps[:, 0:128], func=Cp), reads=[pst], writes=[ksT_t], join=True)
            dense_fm(w, wtr, KC, range(4), uT_src, uT_t, T, ev_k)
            if cfg.STOP == "A1":
                k.emit()
                return nc, k, dbg_out
            w, wtr, KC, NCc = wload(l, "a_vk")

            def ev_vk(tt, ps, pst):
                if tt < 8:
                    k.op("act", OPF("activation", out=V[:, tt + 1, :], in_=ps[:, 0:256], func=Cp), reads=[pst], writes=[V_t], join=True)
                    if g == NG - 1 and tt == 7 and cfg.STOP != "A2m":
                        k.op("act", OPF("activation", out=kvf[:, :], in_=ps[:, 0:512], func=Cp), reads=[pst], writes=[kvf_t])
                        if cfg.STOP != "A2v1":
                            k.dma("sp", o_pwv[l], kvf[:, 0:256], reads=[kvf_t])
                            k.dma("sp", o_pwk[l], kvf[:, 256:512], reads=[kvf_t])
                else:
                    k.op("act", OPF("activation", out=vs[:, :], in_=ps[:, 0:256], func=Cp), reads=[pst], writes=[vs_t])
                    if cfg.STOP != "A2m":
                        k.op("act", OPF("activation", out=kvf[:, :], in_=ps[:, 0:512], func=Cp), reads=[pst], writes=[kvf_t])
                    if cfg.STOP not in ("A2x", "A2m", "A2v1"):
                        for s_ in range(NSEQ):
                            k.dma("sp", o_swv[l][s_, 120:128, :], kvf[s_ * 8:(s_ + 1) * 8, 0:256], reads=[kvf_t])
                            k.dma("sp", o_swk[l][s_, 120:128, :], kvf[s_ * 8:(s_ + 1) * 8, 256:512], reads=[kvf_t])
            dense_tm(w, wtr, KC, 512, uT_src, uT_t, range(ntile), ev_vk)
            if cfg.STOP in ("A2", "A2x", "A2m", "A2v1"):
                k.emit()
                return nc, k, dbg_out
            k.op("act", OPF("activation", out=kcar[:, :, :], in_=kT[:, :, TP:TP + 128], func=Cp), reads=[kT_t], writes=[kcar_t])
            k.op("act", OPF("activation", out=vcar[:, :], in_=V[:, 8, :], func=Cp), reads=[V_t], writes=[vcar_t])
            for qi in range(2):
                w, wtr, KC, NCc = wload(l, f"a_q{qi}")
                for mloc in range(4):
                    hp = qi * 4 + mloc
                    kvh = hp // 2
                    qb, qb_t = qTb[hp % 2]
                    for (t0, tn) in tgroups(T):
                        ps, pst = k.ps()
                        for kc in range(KC):
                            mm(ps[:, 0:tn], w[:, kc, mloc * 128:(mloc + 1) * 128], uT[:, kc, t0:t0 + tn], kc == 0, kc == KC - 1, [wtr, uT_t[kc]], pst)
                        k.op("act", OPF("activation", out=qb[:, t0:t0 + tn], in_=ps[:, 0:tn], func=Cp), reads=[pst], writes=[qb_t], join=True)
                    for blk in range(8):
                        pso, pso_t = k.ps()
                        for hh in range(2):
                            head = 2 * hp + hh
                            po_ = hh * 64
                            tmp = smt[hh]
                            ps, pst = k.ps()
                            mm(ps[:, 0:256], qb[po_:po_ + 64, blk * 128:(blk + 1) * 128], kT[po_:po_ + 64, kvh, blk * 128:blk * 128 + 256], True, True, [qb_t, kT_t], pst)
                            mk_ = maskA["maskA_first"] if (g == 0 and blk == 0) else maskA["maskA_full"]
                            softmax_rows(ps, pst, mk_[0], mk_[1], smallt[:, SINK + head:SINK + head + 1], SCL, tmp)
                            PnT, PnT_t = tmp[6], tmp[7]
                            for c in range(2):
                                k.op("pe", OPF("matmul", pso[po_:po_ + 64, 0:128], lhsT=V[:, blk + c, kvh * 64:(kvh + 1) * 64], rhs=PnT[:, c * 128:(c + 1) * 128], start=(c == 0), stop=(c == 1)),
                                     reads=[V_t, PnT_t], writes=[pso_t], join=not (hh == 0 and c == 0))
                        k.op("act", OPF("activation", out=obT[:, hp, blk * 128:(blk + 1) * 128], in_=pso[:, 0:128], func=Cp), reads=[pso_t], writes=[ob_t[hp]], join=True)
                    if cfg.STOP == "A3p":
                        k.emit()
                        return nc, k, dbg_out
                    if has_s:
                        k.op("dve", OPF("tensor_tensor", out=qz[:, :, :], in0=bcast(qb[:, TP:TM], 1, NSEQ), in1=bm[:, :, :], op=ALU.mult), reads=[qb_t, bm_t], writes=[qz_t])
                        pso, pso_t = k.ps()
                        for hh in range(2):
                            head = 2 * hp + hh
                            po_ = hh * 64
                            tmp = smt[hh]
                            ps, pst = k.ps()
                            for s_ in range(NSEQ):
                                mm(ps[:, 0:128], qz[po_:po_ + 64, s_, :], KcT[po_:po_ + 64, s_, kvh, :], s_ == 0, s_ == NSEQ - 1, [qz_t, KcT_t], pst)
                            k.op("pe", OPF("matmul", ps[:, 128:256], lhsT=qb[po_:po_ + 64, TP:TM], rhs=ksT[po_:po_ + 64, kvh, :], start=True, stop=True), reads=[qb_t, ksT_t], writes=[pst], join=True)
                            mk_ = maskA["maskA_samp"]
                            softmax_rows(ps, pst, mk_[0], mk_[1], smallt[:, SINK + head:SINK + head + 1], SCL, tmp)
                            PnT, PnT_t = tmp[6], tmp[7]
                            k.op("pe", OPF("matmul", pso[po_:po_ + 64, 0:128], lhsT=vs[:, kvh * 64:(kvh + 1) * 64], rhs=PnT[:, 128:256], start=True, stop=False),
                                 reads=[vs_t, PnT_t], writes=[pso_t], join=(hh > 0))
                            for s_ in range(NSEQ):
                                k.op("pe", OPF("matmul", pso[po_:po_ + 64, s_ * 8:(s_ + 1) * 8], lhsT=Vc[:, s_, kvh * 64:(kvh + 1) * 64], rhs=PnT[:, s_ * 8:(s_ + 1) * 8], start=False, stop=(s_ == NSEQ - 1)),
                                     reads=[Vc_t, PnT_t], writes=[pso_t], join=True)
                        k.op("act", OPF("activation", out=obT[:, hp, TP:TM], in_=pso[:, 0:128], func=Cp), reads=[pso_t], writes=[ob_t[hp]], join=True)
            dbg(f"oaT_{l}_{g}", obT[:, :, 0:T], ob_t, [128, 8, T])
            if cfg.STOP == "A":
                k.emit()
                return nc, k, dbg_out
            k.barrier()

            def branch_merge(b, first):
                cnt = 0
                for mb in range(8):
                    wg, ww, wgt, KCg, KCw = wload2(l, f"d_g{b}_{mb}", f"d_w{b}_{mb}")
                    wwt = wgt
                    for mloc in range(2):
                        m = mb * 2 + mloc
                        for (t0, tn) in tgroups(T):
                            sg, sg_t = sg2[cnt % 2]
                            cnt += 1
                            psg, psg_t = k.ps()
                            for kc in range(KCg):
                                mm(psg[:, 0:tn], wg[:, kc, mloc * 128:(mloc + 1) * 128], uT[:, kc, t0:t0 + tn], kc == 0, kc == KCg - 1, [wgt, uT_t[kc]], psg_t)
                            psw, psw_t = k.ps()
                            for kc in range(KCw):
                                mm(psw[:, 0:tn], ww[:, kc, mloc * 128:(mloc + 1) * 128], obT[:, kc, t0:t0 + tn], kc == 0, kc == KCw - 1, [wwt, ob_t[kc]], psw_t)
                            k.op("act", OPF("activation", out=sg[:, 0:tn], in_=psg[:, 0:tn], func=AF.Sigmoid), reads=[psg_t], writes=[sg_t])
                            if first:
                                k.op("dve", OPF("tensor_tensor", out=mergedT[:, m, t0:t0 + tn], in0=psw[:, 0:tn], in1=sg[:, 0:tn], op=ALU.mult),
                                     reads=[psw_t, sg_t], writes=[mg_t[m]], join=True)
                            else:
                                k.op("dve", OPF("tensor_tensor", out=sg[:, 0:tn], in0=psw[:, 0:tn], in1=sg[:, 0:tn], op=ALU.mult), reads=[psw_t, sg_t], writes=[sg_t])
                                k.op("dve", OPF("tensor_tensor", out=mergedT[:, m, t0:t0 + tn], in0=mergedT[:, m, t0:t0 + tn], in1=sg[:, 0:tn], op=ALU.add),
                                     reads=[sg_t, mg_t[m]], writes=[mg_t[m]])

            branch_merge(0, True)
            k.barrier()

            wa[0] = wa_save
            xp, xp_t = walloc("xp", [128, 3 + TP], F32)
            xc, xc_t = walloc("xc", [128, TM], F32)
            xcb, xcb_t = walloc("xcb", [128, TM], BF16)
            rr, rr_t = walloc("rr", [128, TM], F32)
            ig, ig_t = walloc("ig", [128, TM], F32)
            aa, aa_t = walloc("aa", [128, TM], F32)
            m2, m2_t = walloc("m2", [128, TM], F32)
            hh_, hh_t = walloc("hh", [128, TM], F32)
            yv, yv_t = walloc("yv", [128, TM], F32)
            gt, gt_t = walloc("gt", [128, TM], F32)
            if has_s:
                xps, xps_t = walloc("xps", [128, NSEQ, 11], F32)
                h0s, h0s_t = walloc("h0s", [128, 8, NSEQ], F32)
                slo, slo_t = walloc("slo", [128, 8, NSEQ], F32)
                tm16, tm16_t = walloc("tm16", [128, NSEQ], F32)
                k.dma("sp", h0s[:], slru[l], writes=[h0s_t])

            def v3(ap):
                return ap.rearrange("p (s t) -> p s t", t=8)

            for bi in range(4):
                w, wtr, KC, NCc = wload(l, f"b_xy{bi}")
                for j in range(2):
                    cc = 2 * bi + j
                    k.op("dve", OPF("tensor_copy", out=xp[:, 0:3], in_=convcar[:, cc, :]), reads=[convcar_t], writes=[xp_t])
                    if has_s:
                        k.dma("sp", xps[:, :, 0:3], sconv[l][:, cc, :, :], writes=[xps_t])
                    for (t0, tn) in tgroups(T):
                        ps, pst = k.ps()
                        for kc in range(KC):
                            mm(ps[:, 0:tn], w[:, kc, (2 * j) * 128:(2 * j + 1) * 128], uT[:, kc, t0:t0 + tn], kc == 0, kc == KC - 1, [wtr, uT_t[kc]], pst)
                        if t0 < TP:
                            k.op("act", OPF("activation", out=xp[:, 3 + t0:3 + t0 + tn], in_=ps[:, 0:tn], func=Cp), reads=[pst], writes=[xp_t], join=True)
                        else:
                            k.op("act", OPF("activation", out=xps[:, :, 3:11], in_=v3(ps[:, 0:128]), func=Cp), reads=[pst], writes=[xps_t], join=True)
                    cw = lambda jj, cc=cc: smallt[:, CW + cc * 4 + jj:CW + cc * 4 + jj + 1]
                    k.op("dve", OPF("tensor_scalar", out=xc[:, 0:TP], in0=xp[:, 0:TP], scalar1=cw(0), scalar2=smallt[:, CB + cc:CB + cc + 1], op0=ALU.mult, op1=ALU.add), reads=[xp_t, small_t], writes=[xc_t])
                    for jj in range(1, 4):
                        k.op("dve", OPF("scalar_tensor_tensor", out=xc[:, 0:TP], in0=xp[:, jj:jj + TP], scalar=cw(jj), in1=xc[:, 0:TP], op0=ALU.mult, op1=ALU.add), reads=[xp_t, xc_t, small_t], writes=[xc_t])
                    k.op("dve", OPF("tensor_copy", out=convcar[:, cc, :], in_=xp[:, TP:TP + 3]), reads=[xp_t], writes=[convcar_t])
                    if has_s:
                        xcs = v3(xc[:, TP:TM])
                        k.op("dve", OPF("tensor_scalar", out=xcs, in0=xps[:, :, 0:8], scalar1=cw(0), scalar2=smallt[:, CB + cc:CB + cc + 1], op0=ALU.mult, op1=ALU.add), reads=[xps_t, small_t], writes=[xc_t])
                        for jj in range(1, 4):
                            k.op("dve", OPF("scalar_tensor_tensor", out=xcs, in0=xps[:, :, jj:jj + 8], scalar=cw(jj), in1=xcs, op0=ALU.mult, op1=ALU.add), reads=[xps_t, xc_t, small_t], writes=[xc_t])
                        k.dma("sp", o_sconv[l][:, cc, :, :], xps[:, :, 8:11], reads=[xps_t])
                    k.op("act", OPF("activation", out=xcb[:, 0:T], in_=xc[:, 0:T], func=Cp), reads=[xc_t], writes=[xcb_t])
                    for gi, (dst, dst_t, bcol) in enumerate(((rr, rr_t, BA), (ig, ig_t, BX))):
                        for (t0, tn) in tgroups(T):
                            ps, pst = k.ps()
                            mm(ps[:, 0:tn], bdw_t[gi][0][:, cc, :], xcb[:, t0:t0 + tn], True, True, [bdw_t[gi][1], xcb_t], pst)
                            k.op("act", OPF("activation", out=dst[:, t0:t0 + tn], in_=ps[:, 0:tn], func=AF.Sigmoid, bias=smallt[:, bcol + cc:bcol + cc + 1], scale=1.0),
                                 reads=[pst, small_t], writes=[dst_t], join=True)
                    k.op("act", OPF("activation", out=aa[:, 0:T], in_=rr[:, 0:T], func=AF.Exp, scale=cneg[:, cc:cc + 1]), reads=[rr_t, cneg_t], writes=[aa_t])
                    k.op("dve", OPF("tensor_tensor", out=m2[:, 0:T], in0=aa[:, 0:T], in1=aa[:, 0:T], op=ALU.mult), reads=[aa_t], writes=[m2_t])
                    k.op("dve", OPF("tensor_scalar", out=m2[:, 0:T], in0=m2[:, 0:T], scalar1=-1.0, scalar2=1.0, op0=ALU.mult, op1=ALU.add), reads=[m2_t], writes=[m2_t])
                    k.op("act", OPF("activation", out=m2[:, 0:T], in_=m2[:, 0:T], func=AF.Sqrt), reads=[m2_t], writes=[m2_t])
                    if g == 0:
                        k.op("dve", OPF("memset", m2[:, 0:1], 1.0), reads=[], writes=[m2_t])
                    k.op("dve", OPF("tensor_tensor", out=ig[:, 0:T], in0=ig[:, 0:T], in1=xc[:, 0:T], op=ALU.mult), reads=[ig_t, xc_t], writes=[ig_t])
                    k.op("dve", OPF("tensor_tensor", out=ig[:, 0:T], in0=ig[:, 0:T], in1=m2[:, 0:T], op=ALU.mult), reads=[ig_t, m2_t], writes=[ig_t])
                    if has_s:
                        a0 = v3(aa[:, TP:TM])[:, :, 0]
                        u0 = v3(ig[:, TP:TM])[:, :, 0]
                        k.op("dve", OPF("tensor_tensor", out=tm16[:, :], in0=a0, in1=h0s[:, cc, :], op=ALU.mult), reads=[aa_t, h0s_t], writes=[tm16_t])
                        k.op("dve", OPF("tensor_tensor", out=u0, in0=u0, in1=tm16[:, :], op=ALU.add), reads=[ig_t, tm16_t], writes=[ig_t])
                        k.op("dve", OPF("memset", a0, 0.0), reads=[tm16_t], writes=[aa_t])
                    k.op("dve", OPF("tensor_tensor_scan", out=hh_[:, 0:TP], data0=aa[:, 0:TP], data1=ig[:, 0:TP], initial=hcar[:, cc:cc + 1], op0=ALU.mult, op1=ALU.add), reads=[aa_t, ig_t, hcar_t], writes=[hh_t])
                    if has_s:
                        k.op("dve", OPF("tensor_tensor_scan", out=hh_[:, TP:TM], data0=aa[:, TP:TM], data1=ig[:, TP:TM], initial=0.0, op0=ALU.mult, op1=ALU.add), reads=[aa_t, ig_t], writes=[hh_t], join=True)
                        k.op("dve", OPF("tensor_copy", out=slo[:, cc, :], in_=v3(hh_[:, TP:TM])[:, :, 7]), reads=[hh_t], writes=[slo_t])
                    k.op("dve", OPF("tensor_copy", out=hcar[:, cc:cc + 1], in_=hh_[:, TP - 1:TP]), reads=[hh_t], writes=[hcar_t])
                    for (t0, tn) in tgroups(T):
                        ps, pst = k.ps()
                        for kc in range(KC):
                            mm(ps[:, 0:tn], w[:, kc, (2 * j + 1) * 128:(2 * j + 2) * 128], uT[:, kc, t0:t0 + tn], kc == 0, kc == KC - 1, [wtr, uT_t[kc]], pst)
                        k.op("act", OPF("activation", out=yv[:, t0:t0 + tn], in_=ps[:, 0:tn], func=Cp), reads=[pst], writes=[yv_t], join=True)
                    k.op("dve", OPF("tensor_tensor", out=gt[:, 0:T], in0=yv[:, 0:T], in1=yv[:, 0:T], op=ALU.mult), reads=[yv_t], writes=[gt_t])
                    k.op("dve", OPF("tensor_scalar", out=gt[:, 0:T], in0=gt[:, 0:T], scalar1=0.044715, scalar2=1.0, op0=ALU.mult, op1=ALU.add), reads=[gt_t], writes=[gt_t])
                    k.op("dve", OPF("tensor_tensor", out=gt[:, 0:T], in0=gt[:, 0:T], in1=yv[:, 0:T], op=ALU.mult), reads=[gt_t, yv_t], writes=[gt_t])
                    k.op("act", OPF("activation", out=gt[:, 0:T], in_=gt[:, 0:T], func=AF.Sigmoid, scale=1.5957691216057308), reads=[gt_t], writes=[gt_t])
                    k.op("dve", OPF("tensor_tensor", out=gt[:, 0:T], in0=gt[:, 0:T], in1=yv[:, 0:T], op=ALU.mult), reads=[gt_t, yv_t], writes=[gt_t])
                    k.op("dve", OPF("tensor_tensor", out=obT[:, cc, 0:T], in0=gt[:, 0:T], in1=hh_[:, 0:T], op=ALU.mult), reads=[gt_t, hh_t], writes=[ob_t[cc]])
            if has_s:
                k.dma("sp", o_slru[l], slo[:], reads=[slo_t])
            dbg(f"obT_{l}_{g}", obT[:, :, 0:T], ob_t, [128, 8, T])
            if cfg.STOP == "B":
                k.emit()
                return nc, k, dbg_out
            branch_merge(1, False)
            k.barrier()

            wa[0] = wa_save
            qT_all, qT_all_t = walloc("qT_all", [128, 2, TM], BF16)
            kT_all, kT_all_t = walloc("kT_all", [128, 2, TM], BF16)
            qdT_all, qdT_all_t = walloc("qdT_all", [128, 2, TM], BF16)
            kd_all, kd_all_t = walloc("kd_all", [128, 9, 256], BF16)
            rtb = [walloc(f"rt{i}", [128, 2, 2, 128], F32) for i in range(2)]
            t14 = [walloc(f"t14_{i}", [128, 2, 128], F32) for i in range(4)]
            rot, rot_t = walloc("rot", [128, 2, 256], BF16)
            qd, qd_t = walloc("qd", [128, 256], BF16)
            vb2 = [walloc(f"vb{i}", [128, 256], BF16) for i in range(2)]
            sgl2 = [walloc(f"sgl{i}", [128, 256], F32) for i in range(2)]
            itm2 = [walloc(f"itm{i}", [128, 128], BF16) for i in range(2)]
            yy, yy_t = walloc("yy", [128, 256], F32)
            oc, oc_t = walloc("oc", [128, 256], BF16)
            gst, gst_t = walloc("gst", [128, 16], F32)
            gn_sb, gn_sb_t = walloc("gn_sb", [128, 256], F32)
            Sf, Sf_t = walloc("Sf", [128, 4, 2, 256], F32)
            Sb, Sb_t = walloc("Sb", [128, 4, 2, 256], BF16)
            if g == 0:
                k.op("dve", OPF("memset", Sf[:], 0.0), writes=[Sf_t])
            else:
                k.dma("sp", Sf[:, :, :, :].rearrange("p h c e -> p (h c e)"), sscr, reads=[sscr_t], writes=[Sf_t])
            k.op("act", OPF("activation", out=Sb[:, :, :, :], in_=Sf[:, :, :, :], func=Cp), reads=[Sf_t], writes=[Sb_t])
            if has_s:
                Sst, Sst_t = walloc("Sst", [128, 4, 2, 256], BF16)
                Sfs, Sfs_t = walloc("Sfs", [128, 2, 2, 256], F32)
                vexp, vexp_t = walloc("vexp", [128, 4, 256], BF16)
                oTs, oTs_t = walloc("oTs", [128, 2, 128], F32)
                Sn2 = [walloc(f"Sn{i}", [128, 2, 256], F32) for i in range(1)]

            def gn_gate(po, po_t, h, sgl, sgl_t, dst_cols):
                k.op("dve", OPF("bn_stats", out=gst[:, 0:6], in_=po[:, 0:256]), reads=[po_t], writes=[gst_t])
                k.op("dve", OPF("bn_aggr", out=gst[:, 8:10], in_=gst[:, 0:6]), reads=[gst_t], writes=[gst_t])
                k.op("act", OPF("activation", out=gst[:, 10:11], in_=gst[:, 9:10], func=AF.Sqrt, bias=eps_t[:, 0:1], scale=1.0), reads=[gst_t, eps_tt], writes=[gst_t])
                k.op("dve", OPF("reciprocal", out=gst[:, 10:11], in_=gst[:, 10:11]), reads=[gst_t], writes=[gst_t])
                k.op("dve", OPF("tensor_scalar", out=yy[:, :], in0=po[:, 0:256], scalar1=gst[:, 8:9], scalar2=gst[:, 10:11], op0=ALU.subtract, op1=ALU.mult), reads=[po_t, gst_t], writes=[yy_t])
                k.op("dve", OPF("tensor_tensor", out=yy[:, :], in0=yy[:, :], in1=gn_sb[:, :], op=ALU.mult), reads=[yy_t, gn_sb_t], writes=[yy_t])
                k.op("dve", OPF("tensor_tensor", out=oc[:, :], in0=yy[:, :], in1=sgl[:, :], op=ALU.mult), reads=[yy_t, sgl_t], writes=[oc_t])
                pt, ptt = k.ps()
                ptb = pt[:, :].bitcast(BF16)
                for ec in range(2):
                    k.op("pe", OPF("transpose", out=ptb[:, ec * 128:(ec + 1) * 128], in_=oc[:, ec * 128:(ec + 1) * 128], identity=ident_b[:, :]), reads=[oc_t, ident_b_t], writes=[ptt], join=(ec > 0))
                for ec in range(2):
                    k.op("act", OPF("activation", out=obT[:, 2 * h + ec, dst_cols[0]:dst_cols[1]], in_=ptb[:, ec * 128:(ec + 1) * 128], func=Cp), reads=[ptt], writes=[ob_t[2 * h + ec]], join=True)

            for h in range(4):
                k.dma("sp", gn_sb[:], gnb[l][:, h * 256:(h + 1) * 256], writes=[gn_sb_t])
                w1, w1t, KC, _ = wload(l, f"c_qk{h}")
                for tt in range(ntile):
                    is_s = (tt == 8)
                    cols = (tt * 128, (tt + 1) * 128)
                    rt, rt_t = rtb[tt % 2]
                    k.dma("sp", rt[:], ctab["rot"][32 if is_s else g * 8 + tt], writes=[rt_t])
                    ps, pst = k.ps()
                    for kc in range(KC):
                        mm(ps[:, 0:512], uT[:, kc, cols[0]:cols[1]], w1[:, kc, 0:512], kc == 0, kc == KC - 1, [w1t, uT_t[kc]], pst)
                    psv = ps[:, 0:512].rearrange("p (a d two) -> p a d two", a=2, two=2)
                    x1, x2 = psv[:, :, :, 0], psv[:, :, :, 1]
                    cs, sn = rt[:, 0, :, :], rt[:, 1, :, :]
                    for i_, (xa, tb) in enumerate(((x1, cs), (x2, sn), (x2, cs), (x1, sn))):
                        k.op("dve", OPF("tensor_tensor", out=t14[i_][0][:, :, :], in0=xa, in1=tb, op=ALU.mult), reads=[pst, rt_t], writes=[t14[i_][1]])
                    rv = rot[:, :, :].rearrange("p a (d two) -> p a d two", two=2)
                    k.op("dve", OPF("tensor_tensor", out=rv[:, :, :, 0], in0=t14[0][0][:, :, :], in1=t14[1][0][:, :, :], op=ALU.subtract), reads=[t14[0][1], t14[1][1]], writes=[rot_t])
                    k.op("dve", OPF("tensor_tensor", out=rv[:, :, :, 1], in0=t14[2][0][:, :, :], in1=t14[3][0][:, :, :], op=ALU.add), reads=[t14[2][1], t14[3][1]], writes=[rot_t], join=True)
                    dq = (8 if is_s else 0) + h
                    dk = (12 if is_s else 4) + h
                    k.op("dve", OPF("tensor_scalar", out=qd[:, :], in0=rot[:, 0, :], scalar1=dec[:, dq:dq + 1], scalar2=None, op0=ALU.mult), reads=[rot_t, dec_t], writes=[qd_t])
                    k.op("dve", OPF("tensor_scalar", out=kd_all[:, tt, :], in0=rot[:, 1, :], scalar1=dec[:, dk:dk + 1], scalar2=None, op0=ALU.mult), reads=[rot_t, dec_t], writes=[kd_all_t], join=True)
                    pt, ptt = k.ps()
                    ptb = pt[:, :].bitcast(BF16)
                    srcs = [(rot, rot_t, 0, 0), (rot, rot_t, 0, 1), (rot, rot_t, 1, 0), (rot, rot_t, 1, 1)]
                    for i_, (sh, sh_t, a_, dc) in enumerate(srcs):
                        k.op("pe", OPF("transpose", out=ptb[:, i_ * 128:(i_ + 1) * 128], in_=rot[:, a_, dc * 128:(dc + 1) * 128], identity=ident_b[:, :]), reads=[rot_t, ident_b_t], writes=[ptt], join=(i_ > 0))
                    for dc in range(2):
                        k.op("pe", OPF("transpose", out=ptb[:, (4 + dc) * 128:(5 + dc) * 128], in_=qd[:, dc * 128:(dc + 1) * 128], identity=ident_b[:, :]), reads=[qd_t, ident_b_t], writes=[ptt], join=True)
                    p3 = lambda lo: ptb[:, lo * 128:(lo + 2) * 128].rearrange("p (c n) -> p c n", c=2)
                    k.op("act", OPF("activation", out=qT_all[:, :, cols[0]:cols[1]], in_=p3(0), func=Cp), reads=[ptt], writes=[qT_all_t], join=True)
                    k.op("act", OPF("activation", out=kT_all[:, :, cols[0]:cols[1]], in_=p3(2), func=Cp), reads=[ptt], writes=[kT_all_t], join=True)
                    k.op("act", OPF("activation", out=qdT_all[:, :, cols[0]:cols[1]], in_=p3(4), func=Cp), reads=[ptt], writes=[qdT_all_t], join=True)
                w2, w2t, KC, _ = wload(l, f"c_vg{h}")
                for tt in range(ntile):
                    is_s = (tt == 8)
                    cols = (tt * 128, (tt + 1) * 128)
                    vb, vb_t = vb2[tt % 2]
                    sgl, sgl_t = sgl2[tt % 2]
                    itm, itm_t = itm2[tt % 2]
                    ps, pst = k.ps()
                    for kc in range(KC):
                        mm(ps[:, 0:512], uT[:, kc, cols[0]:cols[1]], w2[:, kc, 0:512], kc == 0, kc == KC - 1, [w2t, uT_t[kc]], pst)
                    k.op("act", OPF("activation", out=vb[:, :], in_=ps[:, 0:256], func=Cp), reads=[pst], writes=[vb_t])
                    k.op("act", OPF("activation", out=sgl[:, :], in_=ps[:, 256:512], func=AF.Silu), reads=[pst], writes=[sgl_t])
                    pi, pi_t = k.ps()
                    for dc in range(2):
                        mm(pi[:, 0:128], kT_all[:, dc, cols[0]:cols[1]], qT_all[:, dc, cols[0]:cols[1]], dc == 0, dc == 1, [kT_all_t, qT_all_t], pi_t)
                    DT_, DT_t = (DTs, DTs_t) if is_s else (DTp, DTp_t)
                    k.op("dve", OPF("tensor_tensor", out=itm[:, :], in0=pi[:, 0:128], in1=DT_[:, h, :], op=ALU.mult), reads=[pi_t, DT_t], writes=[itm_t])
                    if not is_s:
                        po, po_t = k.ps()
                        mm(po[:, 0:256], itm[:, :], vb[:, :], True, False, [itm_t, vb_t], po_t)
                        for dc in range(2):
                            mm(po[:, 0:256], qdT_all[:, dc, cols[0]:cols[1]], Sb[:, h, dc, :], False, dc == 1, [qdT_all_t, Sb_t], po_t)
                        gn_gate(po, po_t, h, sgl, sgl_t, cols)
                        for dc in range(2):
                            pu, pu_t = k.ps()
                            mm(pu[:, 0:256], kd_all[:, tt, dc * 128:(dc + 1) * 128], vb[:, :], True, True, [kd_all_t, vb_t], pu_t)
                            k.op("dve", OPF("scalar_tensor_tensor", out=Sf[:, h, dc, :], in0=Sf[:, h, dc, :], scalar=float(GAM[h] ** 128), in1=pu[:, 0:256], op0=ALU.mult, op1=ALU.add), reads=[pu_t, Sf_t], writes=[Sf_t])
                        k.op("act", OPF("activation", out=Sb[:, h, :, :], in_=Sf[:, h, :, :], func=Cp), reads=[Sf_t], writes=[Sb_t])
                    else:
                        pots = [k.ps(), k.ps()]
                        for ec in range(2):
                            k.op("pe", OPF("matmul", pots[ec][0][:, 0:128], lhsT=vb[:, ec * 128:(ec + 1) * 128], rhs=itm[:, :], start=True, stop=False), reads=[vb_t, itm_t], writes=[pots[ec][1]])
                        for sq in range(4):
                            for c_ in range(2):
                                k.dma("pool", Sst[:, :, c_, :], sret[l][4 * sq:4 * sq + 4, h, c_ * 128:(c_ + 1) * 128, :].rearrange("s d e -> d s e"), writes=[Sst_t], join=(c_ > 0))
                            for s4 in range(4):
                                s_ = 4 * sq + s4
                                for ec in range(2):
                                    for dc in range(2):
                                        k.op("pe", OPF("matmul", pots[ec][0][:, s_ * 8:s_ * 8 + 8], lhsT=Sst[:, s4, dc, ec * 128:(ec + 1) * 128], rhs=qdT_all[:, dc, TP + s_ * 8:TP + s_ * 8 + 8], start=False, stop=(dc == 1 and s_ == NSEQ - 1)),
                                             reads=[Sst_t, qdT_all_t], writes=[pots[ec][1]], join=True)
                        for ec in range(2):
                            k.op("act", OPF("activation", out=oTs[:, ec, :], in_=pots[ec][0][:, 0:128], func=Cp), reads=[pots[ec][1]], writes=[oTs_t], join=(ec > 0))
                        po, po_t = k.ps()
                        for ec in range(2):
                            k.op("pe", OPF("transpose", out=po[:, ec * 128:(ec + 1) * 128], in_=oTs[:, ec, :], identity=ident_f[:, :]), reads=[oTs_t, ident_f_t], writes=[po_t], join=(ec > 0))
                        gn_gate(po, po_t, h, sgl, sgl_t, cols)
                        for sq in range(4):
                            k.op("dve", OPF("tensor_tensor", out=vexp[:, :, :], in0=bcast(vb[:, :], 1, 4), in1=bcast(bmv[:, 4 * sq:4 * sq + 4], 2, 256), op=ALU.mult), reads=[vb_t, bmv_t], writes=[vexp_t])
                            for pr in range(2):
                                for c_ in range(2):
                                    k.dma("sp", Sfs[:, :, c_, :], sret[l][4 * sq + 2 * pr:4 * sq + 2 * pr + 2, h, c_ * 128:(c_ + 1) * 128, :].rearrange("s d e -> d s e"), writes=[Sfs_t], join=(c_ > 0))
                                for dc in range(2):
                                    Sn, Sn_t = Sn2[0]
                                    pu, pu_t = k.ps()
                                    mm(pu[:, 0:512], kd_all[:, 8, dc * 128:(dc + 1) * 128], vexp[:, 2 * pr:2 * pr + 2, :].rearrange("p s e -> p (s e)"), True, True, [kd_all_t, vexp_t], pu_t)
                                    k.op("dve", OPF("scalar_tensor_tensor", out=Sn[:, :, :], in0=Sfs[:, :, dc, :], scalar=float(GAM[h] ** 8), in1=pu[:, 0:512].rearrange("p (s e) -> p s e", s=2), op0=ALU.mult, op1=ALU.add),
                                         reads=[pu_t, Sfs_t], writes=[Sn_t])
                                    s0 = 4 * sq + 2 * pr
                                    k.dma("sp", o_sret[l][s0:s0 + 2, h, dc * 128:(dc + 1) * 128, :].rearrange("s d e -> d s e"), Sn[:, :, :], reads=[Sn_t])
            dbg(f"ocT_{l}_{g}", obT[:, :, 0:T], ob_t, [128, 8, T])
            if cfg.STOP == "C":
                k.emit()
                return nc, k, dbg_out
            k.dma("sp", sscr, Sf[:, :, :, :].rearrange("p h c e -> p (h c e)"), reads=[Sf_t], writes=[sscr_t])
            if g == NG - 1:
                k.dma("sp", o_pret[l], Sf[:, :, :, :].rearrange("p h c e -> p (h c e)"), reads=[Sf_t])
            branch_merge(2, False)
            dbg(f"mgT_{l}_{g}", mergedT[:, :, 0:T], mg_t, [128, KD, T])
            if cfg.STOP == "D":
                k.emit()
                return nc, k, dbg_out
            k.barrier()

            for mb in range(4):
                w, wtr, KC, _ = wload(l, f"e_o{mb}")
                for mloc in range(4):
                    m = mb * 4 + mloc
                    rd = [hscr_tr[g][m]] if l > 0 else []
                    k.dma("sp", hT[:, m, 0:T], src[:, m, c0:c0 + T], reads=rd, writes=[hT_t[m]])
                    for (t0, tn) in tgroups(T):
                        ps, pst = k.ps()
                        for kc in range(KC):
                            mm(ps[:, 0:tn], w[:, kc, mloc * 128:(mloc + 1) * 128], mergedT[:, kc, t0:t0 + tn], kc == 0, kc == KC - 1, [wtr, mg_t[kc]], pst)
                        k.op("dve", OPF("tensor_tensor", out=hT[:, m, t0:t0 + tn], in0=ps[:, 0:tn], in1=hT[:, m, t0:t0 + tn], op=ALU.add), reads=[pst, hT_t[m]], writes=[hT_t[m]])
            dbg(f"h1T_{l}_{g}", hT[:, :, 0:T], hT_t, [128, KD, T])
            if cfg.STOP == "E":
                k.emit()
                return nc, k, dbg_out
            k.barrier()

            rmsnorm_to_uT(l, T, G2, hT, hT_t)
            k.barrier()
            xa = [X0, SB_END]
            qxT, qxT_t = walloc("qxT", [128, 4, TM], BF16, xa)
            oxT, oxT_t = walloc("oxT", [128, 4, TM], BF16, xa)
            xsm = [sm_tmp(f"x{i}", xa) for i in range(2)]
            if g == 0:
                xa_save = xa[0]
                mf, mf_t = walloc("mf", [128, 512], F32, xa)
                memTb, memTb_t = walloc("memTb", [128, KD, 256], BF16, xa)
                k.dma("pool", memTb[:], memT, writes=[memTb_t])
                wk_, wkt, KC, _ = wload(l, "x_k")
                msrc = lambda kc, t0, tn: memTb[:, kc, t0:t0 + tn]
                mtr = [memTb_t] * KD

                def ev_mk(m, t0, tn, ps, pst):
                    k.op("act", OPF("activation", out=memK[:, m, :], in_=ps[:, 0:256], func=Cp), reads=[pst], writes=[memK_t], join=True)
                dense_fm(wk_, wkt, KC, range(4), msrc, mtr, 256, ev_mk)

                def ev_mkt(tt, ps, pst):
                    k.op("act", OPF("activation", out=mf[:, :], in_=ps[:, 0:512], func=Cp), reads=[pst], writes=[mf_t])
                    k.dma("sp", o_pmk[l][tt * 128:(tt + 1) * 128, :], mf[:, :], reads=[mf_t])
                dense_tm(wk_, wkt, KC, 512, msrc, mtr, range(2), ev_mkt)
                wv_, wvt, KC, _ = wload(l, "x_v")

                def ev_mvt(tt, ps, pst):
                    k.op("act", OPF("activation", out=memV[:, tt, :], in_=ps[:, 0:512], func=Cp), reads=[pst], writes=[memV_t], join=True)
                    k.op("act", OPF("activation", out=mf[:, :], in_=ps[:, 0:512], func=Cp), reads=[pst], writes=[mf_t])
                    k.dma("sp", o_pmv[l][tt * 128:(tt + 1) * 128, :], mf[:, :], reads=[mf_t])
                dense_tm(wv_, wvt, KC, 512, msrc, mtr, range(2), ev_mvt)
                k.barrier()
                xa[0] = xa_save
                qzx, qzx_t = walloc("qzx", [128, NSEQ, 128], BF16, xa)
                mkh, mkh_t = walloc("mkh", [128, NSEQ, 256], BF16, xa)
                mvh, mvh_t = walloc("mvh", [128, NSEQ, 2, 128], BF16, xa)
            wq_, wqt, KC, _ = wload(l, "x_q")

            def ev_q(m, t0, tn, ps, pst):
                k.op("act", OPF("activation", out=qxT[:, m, t0:t0 + tn], in_=ps[:, 0:tn], func=Cp), reads=[pst], writes=[qxT_t], join=True)
            dense_fm(wq_, wqt, KC, range(4), uT_src, uT_t, T, ev_q)
            XS = 128.0 ** -0.5
            for hd in range(4):
                for tt in range(8):
                    tmp = xsm[tt % 2]
                    ps, pst = k.ps()
                    mm(ps[:, 0:256], qxT[:, hd, tt * 128:(tt + 1) * 128], memK[:, hd, :], True, True, [qxT_t, memK_t], pst)
                    softmax_rows(ps, pst, None, None, None, XS, tmp)
                    PnT, PnT_t = tmp[6], tmp[7]
                    pso, pso_t = k.ps()
                    for mc in range(2):
                        mm(pso[:, 0:128], memV[:, mc, hd * 128:(hd + 1) * 128], PnT[:, mc * 128:(mc + 1) * 128], mc == 0, mc == 1, [memV_t, PnT_t], pso_t)
                    k.op("act", OPF("activation", out=oxT[:, hd, tt * 128:(tt + 1) * 128], in_=pso[:, 0:128], func=Cp), reads=[pso_t], writes=[oxT_t], join=True)
                if has_s:
                    tmp = xsm[0]
                    k.dma("pool", mkh[:], mkT[l][:, :, hd, :], writes=[mkh_t])
                    k.dma("pool", mvh[:], mv[l][:, :, :, hd * 128:(hd + 1) * 128], writes=[mvh_t])
                    k.op("dve", OPF("tensor_tensor", out=qzx[:, :, :], in0=bcast(qxT[:, hd, TP:TM], 1, NSEQ), in1=bm[:, :, :], op=ALU.mult), reads=[qxT_t, bm_t], writes=[qzx_t])
                    ps, pst = k.ps()
                    for s_ in range(NSEQ):
                        mm(ps[:, 0:256], qzx[:, s_, :], mkh[:, s_, :], s_ == 0, s_ == NSEQ - 1, [qzx_t, mkh_t], pst)
                    softmax_rows(ps, pst, None, None, None, XS, tmp)
                    PnT, PnT_t = tmp[6], tmp[7]
                    pso, pso_t = k.ps()
                    for s_ in range(NSEQ):
                        for mc in range(2):
                            k.op("pe", OPF("matmul", pso[:, s_ * 8:(s_ + 1) * 8], lhsT=mvh[:, s_, mc, :], rhs=PnT[:, mc * 128 + s_ * 8:mc * 128 + s_ * 8 + 8], start=(mc == 0), stop=(mc == 1)),
                                 reads=[mvh_t, PnT_t], writes=[pso_t], join=not (s_ == 0 and mc == 0))
                    k.op("act", OPF("activation", out=oxT[:, hd, TP:TM], in_=pso[:, 0:128], func=Cp), reads=[pso_t], writes=[oxT_t], join=True)
            wo_, wot, KC, _ = wload(l, "x_o")
            for m in range(KD):
                for (t0, tn) in tgroups(T):
                    ps, pst = k.ps()
                    for kc in range(KC):
                        mm(ps[:, 0:tn], wo_[:, kc, m * 128:(m + 1) * 128], oxT[:, kc, t0:t0 + tn], kc == 0, kc == KC - 1, [wot, oxT_t], pst)
                    k.op("dve", OPF("tensor_tensor", out=hT[:, m, t0:t0 + tn], in0=ps[:, 0:tn], in1=hT[:, m, t0:t0 + tn], op=ALU.add), reads=[pst, hT_t[m]], writes=[hT_t[m]])
            dbg(f"h2T_{l}_{g}", hT[:, :, 0:T], hT_t, [128, KD, T])
            if cfg.STOP == "X":
                k.emit()
                return nc, k, dbg_out
            k.barrier()

            rmsnorm_to_uT(l, T, G3, hT, hT_t)
            k.barrier()
            xa = [X0, SB_END]
            fT2 = [walloc(f"fT{i}", [128, 8, TM], BF16, xa) for i in range(2)]
            fr2 = [walloc(f"fr{i}", [128, 512], F32, xa) for i in range(2)]
            frc = [0]

            def ffn_up(e_):
                fT, fT_t = fT2[e_ % 2]
                for i in range(2):
                    w, wtr, KC, _ = wload(l, f"f_u{e_}_{i}")
                    for mloc in range(4):
                        fc = i * 4 + mloc
                        for (t0, tn) in tgroups(T):
                            fr, fr_t = fr2[frc[0] % 2]
                            frc[0] += 1
                            ps, pst = k.ps()
                            for kc in range(KC):
                                mm(ps[:, 0:tn], w[:, kc, mloc * 128:(mloc + 1) * 128], uT[:, kc, t0:t0 + tn], kc == 0, kc == KC - 1, [wtr, uT_t[kc]], pst)
                            k.op("act", OPF("activation", out=fr[:, 0:tn], in_=ps[:, 0:tn], func=AF.Relu), reads=[pst], writes=[fr_t])
                            k.op("pool", OPF("tensor_tensor", out=fT[:, fc, t0:t0 + tn], in0=fr[:, 0:tn], in1=fr[:, 0:tn], op=ALU.mult), reads=[fr_t], writes=[fT_t], join=True)

            def ffn_down(e_):
                fT, fT_t = fT2[e_ % 2]
                for i in range(2):
                    w, wtr, KC, _ = wload(l, f"f_d{e_}_{i}")
                    for mloc in range(8):
                        m = i * 8 + mloc
                        for (t0, tn) in tgroups(T):
                            ps, pst = k.ps()
                            for kc in range(KC):
                                mm(ps[:, 0:tn], w[:, kc, mloc * 128:(mloc + 1) * 128], fT[:, kc, t0:t0 + tn], kc == 0, kc == KC - 1, [wtr, fT_t], pst)
                            k.op("dve", OPF("tensor_tensor", out=hT[:, m, t0:t0 + tn], in0=ps[:, 0:tn], in1=hT[:, m, t0:t0 + tn], op=ALU.add), reads=[pst, hT_t[m]], writes=[hT_t[m]])

            ffn_up(0)
            for e_ in range(8):
                if e_ < 7:
                    ffn_up(e_ + 1)
                ffn_down(e_)
            dbg(f"h3T_{l}_{g}", hT[:, :, 0:T], hT_t, [128, KD, T])
            if cfg.STOP == "F":
                k.emit()
                return nc, k, dbg_out
            k.barrier()
            if l < L - 1:
                for kc in range(KD):
                    k.dma("sp", hscr[:, kc, c0:c0 + T], hT[:, kc, 0:T], reads=[hT_t[kc]], writes=[hscr_tr[g][kc]])
            else:
                rmsnorm_to_uT(l, T, GF, hT, hT_t, final_out=(yT, c0))
        k.dma("sp", o_pconv[l], convcar[:], reads=[convcar_t])
        k.dma("sp", o_plru[l], hcar[:], reads=[hcar_t])
    t_emit = k.emit()
    return nc, k, dbg_out


def _fm(a):
    t = a.shape[0]
    return np.ascontiguousarray(a.T.reshape(KD, 128, t).transpose(1, 0, 2))


def pack_weights(inp, l):
    plan, wtot = weight_plan()
    srcs = {"w_in": inp["w_in"][l], "w_out": inp["w_out"][l], "w_xq": inp["w_xq"][l], "w_xk": inp["w_xk"][l],
            "w_xv": inp["w_xv"][l], "w_xo": inp["w_xo"][l], "w_up": inp["w_up"][l], "w_down": inp["w_down"][l]}
    for b in range(3):
        srcs[f"w_branch{b}"] = inp["w_branch"][l, b]
    arr = np.empty((128, wtot), np.float32)
    for key, (src, r0, nr, cols, off) in plan.items():
        W = srcs[src][r0:r0 + nr]
        c0, c1 = int(cols[0]), int(cols[-1]) + 1
        if c1 - c0 == len(cols) and np.all(np.diff(cols) == 1):
            W = W[:, c0:c1]
        else:
            W = W[:, cols]
        KC = nr // 128
        arr[:, off:off + KC * len(cols)] = W.reshape(KC, 128, len(cols)).transpose(1, 0, 2).reshape(128, -1)
    return arr


def prep_inputs(inp, cfg, n_cores=8):
    f32 = np.float32
    inp = {k_: np.asarray(v) for k_, v in inp.items()}
    common = {}
    for l in range(cfg.NL):
        common[f"wl{l}"] = pack_weights(inp, l)
    small = np.zeros((L, 128, 144), f32)
    bdw = np.zeros((L, 2, 128, 8, 128), f32)
    gnb = np.zeros((L, 128, 1024), f32)
    pp = lambda v: v.reshape(-1, 128).T
    for l in range(L):
        small[l, :, 0:16] = pp(inp["norm_mix"][l])
        small[l, :, 16:32] = pp(inp["norm_cross"][l])
        small[l, :, 32:48] = pp(inp["norm_ffn"][l])
        small[l, :, 48:64] = pp(inp["norm_final"])
        small[l, :, 64:96] = inp["conv_w"][l].reshape(4, 8, 128).transpose(2, 1, 0).reshape(128, 32)
        small[l, :, 96:104] = pp(inp["conv_b"][l])
        small[l, :, 104:112] = pp(inp["lru_ba"][l])
        small[l, :, 112:120] = pp(inp["lru_bx"][l])
        small[l, :, 120:128] = pp(inp["lru_lambda"][l])
        small[l, :, 128:144] = np.broadcast_to(inp["attn_sink"][l][None, :], (128, 16))
        for i, nm in enumerate(("lru_wa", "lru_wx")):
            w = inp[nm][l]
            for cc in range(8):
                for bn in range(2):
                    bdw[l, i, bn * 64:(bn + 1) * 64, cc, bn * 64:(bn + 1) * 64] = w[2 * cc + bn]
        gnb[l] = np.broadcast_to(inp["ret_gn"][l].reshape(1, 1024), (128, 1024))
    common["small"] = small
    common["bdw"] = bdw
    common["gnb"] = gnb
    for n, v in const_tables().items():
        common["c_" + n] = np.ascontiguousarray(v, dtype=f32)
    maps = []
    for c in range(n_cores):
        m = dict(common)
        xs = inp["x_sample"][c * NSEQ:(c + 1) * NSEQ].reshape(TS, D)
        if c < 2:
            xp = inp["x_prompt"][c]
            mem = inp["mem_prompt"][c]
        else:
            xp = np.zeros((SEQ, D), f32)
            mem = np.zeros((256, D), f32)
        m["xT"] = _fm(np.concatenate([xp[0:TP], xs, xp[TP:]], axis=0))
        m["memT"] = _fm(mem)
        sl = slice(c * NSEQ, (c + 1) * NSEQ)
        ck, cv = inp["cache_win_k"][:, sl], inp["cache_win_v"][:, sl]
        m["kcT"] = np.ascontiguousarray(ck.transpose(0, 4, 1, 3, 2))
        m["vc"] = np.ascontiguousarray(cv.reshape(L, NSEQ, 128, 256).transpose(0, 2, 1, 3))
        m["kc_tm"] = np.ascontiguousarray(ck.reshape(L, NSEQ, 128, 256))
        m["vc_tm"] = np.ascontiguousarray(cv.reshape(L, NSEQ, 128, 256))
        m["sconv"] = np.ascontiguousarray(inp["state_conv"][:, sl].reshape(L, NSEQ, 3, 8, 128).transpose(0, 4, 3, 1, 2))
        m["slru"] = np.ascontiguousarray(inp["state_lru"][:, sl].reshape(L, NSEQ, 8, 128).transpose(0, 3, 2, 1))
        m["sret"] = np.ascontiguousarray(inp["state_ret"][:, sl])
        m["mkT"] = np.ascontiguousarray(inp["cache_mem_k"][:, sl].transpose(0, 4, 1, 3, 2))
        m["mv"] = np.ascontiguousarray(inp["cache_mem_v"][:, sl].reshape(L, NSEQ, 2, 128, 512).transpose(0, 3, 1, 2, 4))
        maps.append(m)
    return maps


_PROG = {}


def run(inp, cfg=None, n_cores=8):
    cfg = cfg or Cfg()
    key = (cfg.NL, cfg.NG, cfg.DBG, cfg.STOP)
    if key not in _PROG:
        _PROG[key] = build_program(cfg)
    nc, k, dbg_out = _PROG[key]
    maps = prep_inputs(inp, cfg, n_cores)
    res = run_bass_kernel_spmd(nc, maps, core_ids=list(range(n_cores)))
    return res.results


def kernel(**inputs):
    r = run(inputs, Cfg(), 8)
    f32 = np.float32
    yp = np.zeros((2, SEQ, D), f32)
    ys = np.zeros((128, 8, D), f32)
    for c in range(8):
        y = r[c]["yT"].transpose(1, 0, 2).reshape(D, TTOT).T
        ys[c * NSEQ:(c + 1) * NSEQ] = y[TP:TP + TS].reshape(NSEQ, 8, D)
        if c < 2:
            yp[c, 0:TP] = y[0:TP]
            yp[c, TP:] = y[TP + TS:]
    st = lambda name, f: np.stack([f(r[c][name]) for c in range(2)], axis=1)
    p_wk = st("o_pwk", lambda a: a.reshape(L, 128, 4, 64))
    p_wv = st("o_pwv", lambda a: a.reshape(L, 128, 4, 64))
    p_conv = st("o_pconv", lambda a: a.transpose(0, 3, 2, 1).reshape(L, 3, 1024))
    p_lru = st("o_plru", lambda a: a.transpose(0, 2, 1).reshape(L, 1024))
    p_ret = st("o_pret", lambda a: a.reshape(L, 128, 4, 2, 256).transpose(0, 2, 3, 1, 4).reshape(L, 4, 256, 256))
    p_mk = st("o_pmk", lambda a: a.reshape(L, 256, 4, 128))
    p_mv = st("o_pmv", lambda a: a.reshape(L, 256, 4, 128))
    cat = lambda name, f: np.concatenate([f(r[c][name]) for c in range(8)], axis=1)
    s_wk = cat("o_swk", lambda a: a.reshape(L, NSEQ, 128, 4, 64))
    s_wv = cat("o_swv", lambda a: a.reshape(L, NSEQ, 128, 4, 64))
    s_conv = cat("o_sconv", lambda a: a.transpose(0, 3, 4, 2, 1).reshape(L, NSEQ, 3, 1024))
    s_lru = cat("o_slru", lambda a: a.transpose(0, 3, 2, 1).reshape(L, NSEQ, 1024))
    s_ret = cat("o_sret", lambda a: a)
    outs = (yp, ys, p_wk, p_wv, p_conv, p_lru, p_ret, p_mk, p_mv, s_wk, s_wv, s_conv, s_lru, s_ret)
    return tuple(np.ascontiguousarray(o, dtype=f32) for o in outs)
```

```python
import contextlib
import numpy as np
import ml_dtypes
import concourse.bass as bass
import concourse.mybir as mybir
from concourse.bass_utils import run_bass_kernel_spmd

F32 = mybir.dt.float32
BF16 = mybir.dt.bfloat16
AF = mybir.ActivationFunctionType
ALU = mybir.AluOpType
AX = mybir.AxisListType

ENGS = ("pe", "dve", "act", "pool", "sp")
DMA_SLOTS = {"sp": 30, "pool": 24, "act": 8}
SAME_ENGINE_SYNC = ("dve", "act", "pool")


class Tr:
    __slots__ = ("name", "w", "r", "prev_r")

    def __init__(self, name):
        self.name = name
        self.w = {}
        self.r = {}
        self.prev_r = {}


def _flat(d):
    out = []
    for k, v in d.items():
        if k == "dma":
            out.extend(v)
        else:
            out.append(v)
    return out


def _add(d, ins):
    if ins.is_dma:
        d.setdefault("dma", []).append(ins)
    else:
        d[ins.eng] = ins


class Ins:
    __slots__ = ("eng", "fn", "deps", "is_dma", "waited", "semval", "slot", "dval", "inc")

    def __init__(self, eng, fn, is_dma):
        self.eng = eng
        self.fn = fn
        self.deps = []
        self.is_dma = is_dma
        self.waited = False
        self.semval = 0
        self.slot = None
        self.dval = 0
        self.inc = 16


class K:
    def __init__(self, nc):
        self.nc = nc
        self.q = {e: [] for e in ENGS}
        self.stack = contextlib.ExitStack()
        self.dma_count = {e: 0 for e in DMA_SLOTS}
        self.slot_last = {e: [None] * n for e, n in DMA_SLOTS.items()}
        self.sb_off = 16512
        self.n_t = 0
        self.ps_rr = 0

    def sbuf(self, name, shape, dtype, off=None):
        esz = 2 if dtype == BF16 else 4
        nbytes = int(np.prod(shape[1:])) * esz
        if off is None:
            off = self.sb_off
            self.sb_off = (off + nbytes + 31) // 32 * 32
            assert self.sb_off <= SB_END, ("SBUF overflow", name, self.sb_off)
        else:
            assert off + nbytes <= SB_END, ("SBUF overflow", name)
        self.n_t += 1
        h = self.nc.alloc_sbuf_tensor_at(f"{name}_{self.n_t}", list(shape), dtype, offset=off)
        return h, Tr(name)

    def psum_banks(self):
        self.banks = []
        for i in range(8):
            h = self.nc.alloc_psum_tensor(f"psb{i}", [128, 512], F32)
            self.banks.append((h, Tr(f"psb{i}")))
        return self.banks

    def ps(self):
        b = self.banks[self.ps_rr % 6]
        self.ps_rr += 1
        return b

    def _record(self, ins, reads, writes, join):
        deps = ins.deps
        for t in reads:
            deps.extend(_flat(t.w))
        for t in writes:
            if join and not t.r:
                deps.extend(_flat(t.prev_r))
            else:
                deps.extend(_flat(t.w))
                deps.extend(_flat(t.r))
        for t in reads:
            _add(t.r, ins)
        for t in writes:
            if join and not t.r:
                _add(t.w, ins)
            else:
                t.prev_r = t.r
                t.r = {}
                t.w = {}
                _add(t.w, ins)
        self.q[ins.eng].append(ins)
        return ins

    def op(self, eng, fn, reads=(), writes=(), join=False):
        return self._record(Ins(eng, fn, False), reads, writes, join)

    def dma(self, q, out, in_, reads=(), writes=(), join=False, **kw):
        ins = Ins(q, (lambda e: e.dma_start(out=out, in_=in_, **kw)), True)
        n = self.dma_count[q]
        self.dma_count[q] = n + 1
        ns = DMA_SLOTS[q]
        slot = n % ns
        ins.slot = (q, slot)
        ins.dval = 16 * (n // ns + 1)
        prev = self.slot_last[q][slot]
        if prev is not None:
            ins.deps.append(prev)
        self.slot_last[q][slot] = ins
        return self._record(ins, reads, writes, join)

    def barrier(self):
        lastc = []
        for e in ENGS:
            for ins in reversed(self.q[e]):
                if not ins.is_dma and ins.fn is not None:
                    lastc.append(ins)
                    break
        dmas = [s for q in self.slot_last for s in self.slot_last[q] if s is not None]
        for e in ENGS:
            b = Ins(e, None, False)
            b.deps = list(lastc) + list(dmas)
            self.q[e].append(b)

    def emit(self):
        nc = self.nc
        st = self.stack
        esem = {e: st.enter_context(nc.semaphore(f"es_{e}")) for e in ENGS}
        dsem = {(q, i): st.enter_context(nc.semaphore(f"ds_{q}{i}")) for q, n in DMA_SLOTS.items() for i in range(n)}
        fin = Ins("sp", None, False)
        fin.deps = [s for q in self.slot_last for s in self.slot_last[q] if s is not None]
        self.q["sp"].append(fin)
        for e in ENGS:
            for ins in self.q[e]:
                for d in ins.deps:
                    if d.is_dma or d.fn is None:
                        continue
                    if d.eng == e and e not in SAME_ENGINE_SYNC:
                        continue
                    d.waited = True
        for e in ENGS:
            c = 0
            for ins in self.q[e]:
                if ins.is_dma or ins.fn is None:
                    continue
                if ins.waited:
                    c += 1
                    ins.semval = c
        stats = {}

        def run(e, eh):
            seen = {}
            nw = 0
            for ins in self.q[e]:
                for d in ins.deps:
                    if d.is_dma:
                        key = d.slot
                        sem = dsem[key]
                        val = d.dval
                    else:
                        if d.fn is None:
                            continue
                        if d.eng == e and e not in SAME_ENGINE_SYNC:
                            continue
                        key = d.eng
                        sem = esem[d.eng]
                        val = d.semval
                    if seen.get(key, 0) >= val:
                        continue
                    seen[key] = val
                    eh.wait_ge(sem, val)
                    nw += 1
                if ins.fn is None:
                    continue
                bi = ins.fn(eh)
                if ins.is_dma:
                    bi.then_inc(dsem[ins.slot], 16)
                elif ins.waited:
                    bi.then_inc(esem[e], 1)
            stats[e] = (len(self.q[e]), nw)

        with nc.Block() as block:
            @block.tensor
            def _(eh):
                run("pe", eh)

            @block.vector
            def _(eh):
                run("dve", eh)

            @block.scalar
            def _(eh):
                run("act", eh)

            @block.gpsimd
            def _(eh):
                run("pool", eh)

            @block.sync
            def _(eh):
                run("sp", eh)
        self.stats = stats
        st.close()


def OPF(name, *a, **kw):
    return lambda e: getattr(e, name)(*a, **kw)


def bcast(ap, axis, n):
    dims = [list(d) for d in ap.ap]
    dims.insert(axis, [0, n])
    return bass.AP(ap.tensor, ap.offset, dims)


SB_END = 229312
D = 2048
KD = 16
L = 2
SEQ = 4096
NGRP = 4
TP = 1024
TS = 128
NSEQ = 16
TTOT = SEQ + TS
IN_W = 13824
C_QA, C_KA, C_VA, C_XR, C_YR, C_QC, C_KC, C_VC, C_GC, C_G = 0, 1024, 1280, 1536, 2560, 3584, 4608, 5632, 6656, 7680
EPS = 1e-6
NEG = -1e30
GAM = [1.0 - 2.0 ** (-5.0 - h) for h in range(4)]


def grp_cols(g):
    if g == 0:
        return 0, TP + TS
    return TP + TS + (g - 1) * TP, TP


def tgroups(T):
    out = []
    t = 0
    while t < T:
        n = min(512, T - t)
        out.append((t, n))
        t += n
    return out


def weight_plan():
    plan = {}
    off = [0]

    def add(key, src, r0, nr, cols):
        cols = np.asarray(cols, dtype=np.int64)
        assert (nr // 128) * len(cols) <= 8192
        plan[key] = (src, r0, nr, cols, off[0])
        off[0] += (nr // 128) * len(cols)

    ar = np.arange
    kd = []
    for kv in range(4):
        c = C_KA + kv * 64 + ar(64)
        kd += [c, c]
    add("a_kdup", "w_in", 0, D, np.concatenate(kd))
    add("a_vk", "w_in", 0, D, np.concatenate([C_VA + ar(256), C_KA + ar(256)]))
    for i in range(2):
        add(f"a_q{i}", "w_in", 0, D, C_QA + i * 512 + ar(512))
    for i in range(4):
        cols = np.concatenate([C_XR + (2 * i) * 128 + ar(128), C_YR + (2 * i) * 128 + ar(128),
                               C_XR + (2 * i + 1) * 128 + ar(128), C_YR + (2 * i + 1) * 128 + ar(128)])
        add(f"b_xy{i}", "w_in", 0, D, cols)
    for h in range(4):
        add(f"c_qk{h}", "w_in", 0, D, np.concatenate([C_QC + h * 256 + ar(256), C_KC + h * 256 + ar(256)]))
        add(f"c_vg{h}", "w_in", 0, D, np.concatenate([C_VC + h * 256 + ar(256), C_GC + h * 256 + ar(256)]))
    for b in range(3):
        for mb in range(4):
            pass
        for mb in range(8):
            add(f"d_g{b}_{mb}", "w_in", 0, D, C_G + b * D + mb * 256 + ar(256))
            add(f"d_w{b}_{mb}", f"w_branch{b}", 0, 1024, mb * 256 + ar(256))
    for mb in range(4):
        add(f"e_o{mb}", "w_out", 0, D, mb * 512 + ar(512))
    add("x_q", "w_xq", 0, D, ar(512))
    add("x_k", "w_xk", 0, D, ar(512))
    add("x_v", "w_xv", 0, D, ar(512))
    add("x_o", "w_xo", 0, 512, ar(2048))
    for e in range(8):
        for i in range(2):
            add(f"f_u{e}_{i}", "w_up", 0, D, e * 1024 + i * 512 + ar(512))
        for i in range(2):
            add(f"f_d{e}_{i}", "w_down", e * 1024, 1024, i * 1024 + ar(1024))
    return plan, off[0]


def const_tables():
    c = {}
    i = np.arange(128)[:, None]
    j = np.arange(256)[None, :]
    full = (j > i) & (j <= i + 128)
    c["maskA_full"] = np.where(full, 0.0, NEG).astype(np.float32)
    c["maskA_first"] = np.where(full & (j >= 128), 0.0, NEG).astype(np.float32)
    s = np.arange(128) // 8
    t = np.arange(128) % 8
    ms = np.zeros((128, 256), bool)
    ms[:, :128] = np.arange(128)[None, :] >= (t[:, None] + 1)
    ms[:, 128:] = (s[:, None] == s[None, :]) & (t[None, :] <= t[:, None])
    c["maskA_samp"] = np.where(ms, 0.0, NEG).astype(np.float32)
    kk = np.arange(128)[:, None]
    qq = np.arange(128)[None, :]
    dtp = np.zeros((128, 4, 128), np.float64)
    dts = np.zeros((128, 4, 128), np.float64)
    for h in range(4):
        lg = np.log(GAM[h])
        dtp[:, h, :] = np.where(qq >= kk, np.exp(np.maximum(qq - kk, 0) * lg), 0.0)
        same = (s[:, None] == s[None, :]) & (t[None, :] >= t[:, None])
        dts[:, h, :] = np.where(same, np.exp(np.maximum(t[None, :] - t[:, None], 0) * lg), 0.0)
    c["DTp"] = dtp.astype(np.float32)
    c["DTs"] = dts.astype(np.float32)
    dec = np.zeros((128, 16), np.float64)
    n = np.arange(128)
    for h in range(4):
        lg = np.log(GAM[h])
        dec[:, h] = np.exp((n + 1.0) * lg)
        dec[:, 4 + h] = np.exp((127.0 - n) * lg)
        dec[:, 8 + h] = np.exp((t + 1.0) * lg)
        dec[:, 12 + h] = np.exp((7.0 - t) * lg)
    c["dec"] = dec.astype(np.float32)
    inv = (1.0 / (10000.0 ** np.linspace(0.0, 1.0, 128, dtype=np.float32))).astype(np.float32)
    pos = np.zeros((33, 128), np.float32)
    pos[:32] = np.arange(4096, dtype=np.float32).reshape(32, 128)
    pos[32] = 8192.0 + t
    ang = (pos[:, :, None] * inv[None, None, :]).astype(np.float32)
    cs, sn = np.cos(ang).astype(np.float32), np.sin(ang).astype(np.float32)
    c["rot"] = np.stack([cs, cs / 16.0, sn, sn / 16.0], axis=2).astype(np.float32)
    c["ident"] = np.eye(128, dtype=np.float32)
    bm = (s[None, :] == np.arange(16)[:, None]).astype(np.float32)
    c["bm"] = np.broadcast_to(bm[None], (128, 16, 128)).copy()
    c["bmv"] = (s[:, None] == np.arange(16)[None, :]).astype(np.float32)
    return c


class Cfg:
    NL = 2
    NG = 4
    DBG = False
    STOP = None


def build_program(cfg):
    nc = bass.Bass("TRN2", target_bir_lowering=False)
    k = K(nc)
    plan, wtot = weight_plan()
    NL, NG = cfg.NL, cfg.NG
    dbg_out = {}

    def din(name, shape, dt=F32):
        return nc.dram_tensor(name, list(shape), dt, kind="ExternalInput").ap()

    def dout(name, shape):
        return nc.dram_tensor(name, list(shape), F32, kind="ExternalOutput").ap()

    xT = din("xT", [128, KD, TTOT])
    memT = din("memT", [128, KD, 256])
    wl = [din(f"wl{l}", [128, wtot]) for l in range(cfg.NL)]
    small = din("small", [L, 128, 16 * 4 + 8 * 4 + 8 * 4 + 16])
    bdw = din("bdw", [L, 2, 128, 8, 128])
    gnb = din("gnb", [L, 128, 1024])
    ctab = {n: din("c_" + n, v.shape) for n, v in const_tables().items()}
    kcT = din("kcT", [L, 64, NSEQ, 4, 128])
    vc = din("vc", [L, 128, NSEQ, 256])
    kc_tm = din("kc_tm", [L, NSEQ, 128, 256])
    vc_tm = din("vc_tm", [L, NSEQ, 128, 256])
    sconv = din("sconv", [L, 128, 8, NSEQ, 3])
    slru = din("slru", [L, 128, 8, NSEQ])
    sret = din("sret", [L, NSEQ, 4, 256, 256])
    mkT = din("mkT", [L, 128, NSEQ, 4, 256])
    mv = din("mv", [L, 128, NSEQ, 2, 512])

    yT = dout("yT", [128, KD, TTOT])
    o_pwk = dout("o_pwk", [L, 128, 256])
    o_pwv = dout("o_pwv", [L, 128, 256])
    o_pconv = dout("o_pconv", [L, 128, 8, 3])
    o_plru = dout("o_plru", [L, 128, 8])
    o_pret = dout("o_pret", [L, 128, 4 * 2 * 256])
    o_pmk = dout("o_pmk", [L, 256, 512])
    o_pmv = dout("o_pmv", [L, 256, 512])
    o_swk = dout("o_swk", [L, NSEQ, 128, 256])
    o_swv = dout("o_swv", [L, NSEQ, 128, 256])
    o_sconv = dout("o_sconv", [L, 128, 8, NSEQ, 3])
    o_slru = dout("o_slru", [L, 128, 8, NSEQ])
    o_sret = dout("o_sret", [L, NSEQ, 4, 256, 256])
    hscr = nc.dram_tensor("hscr", [128, KD, TTOT], F32, kind="Internal").ap()
    hscr_tr = [[Tr(f"hscr{g}_{kc}") for kc in range(KD)] for g in range(NGRP)]
    sscr = nc.dram_tensor("sscr", [128, 2048], F32, kind="Internal").ap()
    sscr_t = Tr("sscr")

    def dbg(name, ap_sb, tr, shape):
        if not cfg.DBG:
            return
        o = dout("dbg_" + name, shape)
        dbg_out[name] = shape
        k.dma("pool", o, ap_sb, reads=tr)

    banks = k.psum_banks()
    TM = TP + TS
    ident_f, ident_f_t = k.sbuf("ident_f", [128, 128], F32)
    ident_b, ident_b_t = k.sbuf("ident_b", [128, 128], BF16)
    ones_b, ones_b_t = k.sbuf("ones_b", [128, 128], BF16)
    maskA = {n: k.sbuf(n, [128, 256], F32) for n in ("maskA_full", "maskA_first", "maskA_samp")}
    DTp, DTp_t = k.sbuf("DTp", [128, 4, 128], F32)
    DTs, DTs_t = k.sbuf("DTs", [128, 4, 128], F32)
    dec, dec_t = k.sbuf("dec", [128, 16], F32)
    bm, bm_t = k.sbuf("bm", [128, 16, 128], BF16)
    bmv, bmv_t = k.sbuf("bmv", [128, 16], F32)
    eps_t, eps_tt = k.sbuf("eps", [128, 1], F32)
    smallt, small_t = k.sbuf("small", [128, 144], F32)
    cneg, cneg_t = k.sbuf("cneg", [128, 8], F32)
    bdw_t = [k.sbuf(f"bdw{i}", [128, 8, 128], BF16) for i in range(2)]
    kcar, kcar_t = k.sbuf("kcar", [128, 4, 128], BF16)
    vcar, vcar_t = k.sbuf("vcar", [128, 256], BF16)
    convcar, convcar_t = k.sbuf("convcar", [128, 8, 3], F32)
    hcar, hcar_t = k.sbuf("hcar", [128, 8], F32)
    memK, memK_t = k.sbuf("memK", [128, 4, 256], BF16)
    memV, memV_t = k.sbuf("memV", [128, 2, 512], BF16)
    NRING = 2
    ring = [k.sbuf(f"wring{i}", [128, 8192], BF16) for i in range(NRING)]
    ring_i = [0]
    uT, _ = k.sbuf("uT", [128, KD, TM], BF16)
    uT_t = [Tr(f"uT{kc}") for kc in range(KD)]
    R0 = k.sb_off
    hT, _ = k.sbuf("hT", [128, KD, TM], F32)
    hT_t = [Tr(f"hT{kc}") for kc in range(KD)]
    X0 = k.sb_off
    XSZ = SB_END - X0
    print("SBUF: R0", R0, "X0", X0, "X size", XSZ)

    Sq, Id, Cp = AF.Square, AF.Identity, AF.Copy

    def wload(l, key):
        src, r0, nr, cols, off = plan[key]
        KC, NCc = nr // 128, len(cols)
        h, tr = ring[ring_i[0] % NRING]
        ring_i[0] += 1
        k.dma("pool", h[:, 0:KC * NCc], wl[l][:, off:off + KC * NCc], writes=[tr])
        return h[:, 0:KC * NCc].rearrange("p (k n) -> p k n", k=KC), tr, KC, NCc

    def wload2(l, keyA, keyB):
        sa, ra_, nra, ca, offa = plan[keyA]
        sb_, rb_, nrb, cb, offb = plan[keyB]
        KA, NA, KB, NB = nra // 128, len(ca), nrb // 128, len(cb)
        assert offb == offa + KA * NA and KA * NA + KB * NB <= 8192
        h, tr = ring[ring_i[0] % NRING]
        ring_i[0] += 1
        tot = KA * NA + KB * NB
        k.dma("pool", h[:, 0:tot], wl[l][:, offa:offa + tot], writes=[tr])
        va = h[:, 0:KA * NA].rearrange("p (k n) -> p k n", k=KA)
        vb = h[:, KA * NA:tot].rearrange("p (k n) -> p k n", k=KB)
        return va, vb, tr, KA, KB

    def mm(out, lhsT, rhs, start, stop, reads, pst):
        k.op("pe", OPF("matmul", out, lhsT=lhsT, rhs=rhs, start=start, stop=stop), reads=reads, writes=[pst], join=not start)

    def dense_fm(w, wtr, KC, m_list, xsrc, xtrs, T, evac):
        for m in m_list:
            for (t0, tn) in tgroups(T):
                ps, pst = k.ps()
                for kc in range(KC):
                    mm(ps[:, 0:tn], w[:, kc, m * 128:(m + 1) * 128], xsrc(kc, t0, tn), kc == 0, kc == KC - 1, [wtr, xtrs[kc]], pst)
                evac(m, t0, tn, ps, pst)

    def dense_tm(w, wtr, KC, NCc, xsrc, xtrs, tiles, evac):
        for tt in tiles:
            ps, pst = k.ps()
            for kc in range(KC):
                mm(ps[:, 0:NCc], xsrc(kc, tt * 128, 128), w[:, kc, 0:NCc], kc == 0, kc == KC - 1, [wtr, xtrs[kc]], pst)
            evac(tt, ps, pst)

    uT_src = lambda kc, t0, tn: uT[:, kc, t0:t0 + tn]

    def rmsnorm_to_uT(l, T, gcol, src_h, src_tr, final_out=None):
        sq, sq_t = k.sbuf("sq", [128, 2, TM], BF16, off=X0)
        sq_tr = [Tr("sq0"), Tr("sq1")]
        rstd, rstd_t = k.sbuf("rstd", [128, TM], F32, off=X0 + 2 * TM * 2)
        tg = tgroups(T)
        pss = [banks[6], banks[7], k.ps()][:len(tg)]
        for kc in range(KD):
            j = kc % 2
            k.op("act", OPF("activation", out=sq[:, j, 0:T], in_=src_h[:, kc, 0:T], func=Sq), reads=[src_tr[kc]], writes=[sq_tr[j]])
            for i, (t0, tn) in enumerate(tg):
                ps, pst = pss[i]
                mm(ps[:, 0:tn], ones_b[:, :], sq[:, j, t0:t0 + tn], kc == 0, kc == KD - 1, [sq_tr[j], ones_b_t], pst)
        for i, (t0, tn) in enumerate(tg):
            ps, pst = pss[i]
            k.op("act", OPF("activation", out=rstd[:, t0:t0 + tn], in_=ps[:, 0:tn], func=AF.Sqrt, scale=1.0 / D, bias=eps_t[:, 0:1]), reads=[pst, eps_tt], writes=[rstd_t], join=(i > 0))
        k.op("dve", OPF("reciprocal", out=rstd[:, 0:T], in_=rstd[:, 0:T]), reads=[rstd_t], writes=[rstd_t])
        for kc in range(KD):
            if final_out is None:
                k.op("dve", OPF("scalar_tensor_tensor", out=uT[:, kc, 0:T], in0=src_h[:, kc, 0:T], scalar=smallt[:, gcol + kc:gcol + kc + 1], in1=rstd[:, 0:T], op0=ALU.mult, op1=ALU.mult),
                     reads=[src_tr[kc], rstd_t, small_t], writes=[uT_t[kc]])
            else:
                yo, yc0 = final_out
                k.op("dve", OPF("scalar_tensor_tensor", out=src_h[:, kc, 0:T], in0=src_h[:, kc, 0:T], scalar=smallt[:, gcol + kc:gcol + kc + 1], in1=rstd[:, 0:T], op0=ALU.mult, op1=ALU.mult),
                     reads=[src_tr[kc], rstd_t, small_t], writes=[src_tr[kc]])
                k.dma("sp", yo[:, kc, yc0:yc0 + T], src_h[:, kc, 0:T], reads=[src_tr[kc]])

    k.op("dve", OPF("memset", eps_t[:], EPS), writes=[eps_tt])
    k.dma("sp", ident_f[:], ctab["ident"], writes=[ident_f_t])
    k.dma("pool", ident_b[:], ctab["ident"], writes=[ident_b_t])
    k.op("dve", OPF("memset", ones_b[:], 1.0), writes=[ones_b_t])
    for n in maskA:
        k.dma("sp", maskA[n][0][:], ctab[n], writes=[maskA[n][1]])
    k.dma("sp", DTp[:], ctab["DTp"], writes=[DTp_t])
    k.dma("sp", DTs[:], ctab["DTs"], writes=[DTs_t])
    k.dma("sp", dec[:], ctab["dec"], writes=[dec_t])
    k.dma("pool", bm[:], ctab["bm"], writes=[bm_t])
    k.dma("sp", bmv[:], ctab["bmv"], writes=[bmv_t])

    for l in range(NL):
        k.dma("sp", smallt[:], small[l], writes=[small_t])
        for i in range(2):
            k.dma("pool", bdw_t[i][0][:], bdw[l, i], writes=[bdw_t[i][1]])
        G1, G2, G3, GF, CW, CB, BA, BX, LAM, SINK = 0, 16, 32, 48, 64, 96, 104, 112, 120, 128
        k.op("act", OPF("activation", out=cneg[:], in_=smallt[:, LAM:LAM + 8], func=AF.Exp, scale=-1.0), reads=[small_t], writes=[cneg_t])
        k.op("act", OPF("activation", out=cneg[:], in_=cneg[:], func=AF.Ln, bias=1.0), reads=[cneg_t], writes=[cneg_t])
        k.op("dve", OPF("tensor_scalar", out=cneg[:], in0=cneg[:], scalar1=-8.0, scalar2=None, op0=ALU.mult), reads=[cneg_t], writes=[cneg_t])
        k.op("dve", OPF("memset", kcar[:], 0.0), writes=[kcar_t])
        k.op("dve", OPF("memset", vcar[:], 0.0), writes=[vcar_t])
        k.op("dve", OPF("memset", convcar[:], 0.0), writes=[convcar_t])
        k.op("dve", OPF("memset", hcar[:], 0.0), writes=[hcar_t])

        for g in range(NG):
            c0, T = grp_cols(g)
            has_s = (g == 0)
            ntile = T // 128
            src = xT if l == 0 else hscr
            k.barrier()
            for kc in range(KD):
                rd = [hscr_tr[g][kc]] if l > 0 else []
                k.dma("sp", hT[:, kc, 0:T], src[:, kc, c0:c0 + T], reads=rd, writes=[hT_t[kc]])
            rmsnorm_to_uT(l, T, G1, hT, hT_t)
            dbg(f"u_{l}_{g}", uT[:, :, 0:T], uT_t, [128, KD, T])
            if cfg.STOP == "u":
                k.emit()
                return nc, k, dbg_out
            k.barrier()
            MRG = SB_END - KD * TM * 2
            assert MRG >= X0, (MRG, X0)
            mergedT, _ = k.sbuf("mergedT", [128, KD, TM], BF16, off=MRG)
            mg_t = [Tr(f"mg{m}") for m in range(KD)]
            obT, _ = k.sbuf("obT", [128, 8, TM], BF16, off=R0)
            ob_t = [Tr(f"ob{c}") for c in range(8)]
            wa = [R0 + 8 * TM * 2, MRG]

            def walloc(name, shape, dt, region=None):
                r = wa if region is None else region
                esz = 2 if dt == BF16 else 4
                nb = (int(np.prod(shape[1:])) * esz + 31) // 32 * 32
                assert r[0] + nb <= r[1], ("work area overflow", name, r[0] + nb - r[1])
                h, t = k.sbuf(name, shape, dt, off=r[0])
                r[0] += nb
                return h, t

            SCL = 0.125
            sg2 = [walloc(f"sg_{i}", [128, 512], F32) for i in range(2)]
            wa_save = wa[0]

            def softmax_rows(ps, pst, mask, mask_t, sinkcol, scale, tmp):
                Sm, Sm_t, P, P_t, Pn, Pn_t, PnT, PnT_t, stt, stt_t = tmp
                if mask is not None:
                    k.op("dve", OPF("scalar_tensor_tensor", out=Sm[:, :], in0=ps[:, 0:256], scalar=scale, in1=mask[:, :], op0=ALU.mult, op1=ALU.add), reads=[pst, mask_t], writes=[Sm_t])
                else:
                    k.op("dve", OPF("tensor_scalar", out=Sm[:, :], in0=ps[:, 0:256], scalar1=scale, scalar2=None, op0=ALU.mult), reads=[pst], writes=[Sm_t])
                yield
                k.op("dve", OPF("reduce_max", out=stt[:, 0:1], in_=Sm[:, :], axis=AX.X), reads=[Sm_t], writes=[stt_t])
                yield
                if sinkcol is not None:
                    k.op("dve", OPF("tensor_scalar", out=stt[:, 1:2], in0=stt[:, 0:1], scalar1=sinkcol, scalar2=-1.0, op0=ALU.max, op1=ALU.mult), reads=[stt_t, small_t], writes=[stt_t])
                else:
                    k.op("dve", OPF("tensor_scalar", out=stt[:, 1:2], in0=stt[:, 0:1], scalar1=-1.0, scalar2=None, op0=ALU.mult), reads=[stt_t], writes=[stt_t])
                yield
                k.op("act", OPF("activation", out=P[:, :], in_=Sm[:, :], func=AF.Exp, bias=stt[:, 1:2], scale=1.0, accum_out=stt[:, 2:3]), reads=[Sm_t, stt_t], writes=[P_t, stt_t])
                yield
                if sinkcol is not None:
                    k.op("act", OPF("activation", out=stt[:, 3:4], in_=sinkcol, func=AF.Exp, bias=stt[:, 1:2], scale=1.0), reads=[stt_t, small_t], writes=[stt_t])
                    yield
                    k.op("dve", OPF("tensor_tensor", out=stt[:, 2:3], in0=stt[:, 2:3], in1=stt[:, 3:4], op=ALU.add), reads=[stt_t], writes=[stt_t])
                    yield
                k.op("dve", OPF("reciprocal", out=stt[:, 4:5], in_=stt[:, 2:3]), reads=[stt_t], writes=[stt_t])
                yield
                k.op("dve", OPF("tensor_scalar", out=Pn[:, :], in0=P[:, :], scalar1=stt[:, 4:5], scalar2=None, op0=ALU.mult), reads=[P_t, stt_t], writes=[Pn_t])
                yield
                ptb = ps[:, :].bitcast(BF16)
                for c in range(2):
                    k.op("pe", OPF("transpose", out=ptb[:, 512 + c * 128:512 + (c + 1) * 128], in_=Pn[:, c * 128:(c + 1) * 128], identity=ident_b[:, :]), reads=[Pn_t, ident_b_t], writes=[pst], join=(c > 0))
                yield
                k.op("act", OPF("activation", out=PnT[:, :], in_=ptb[:, 512:768], func=Cp), reads=[pst], writes=[PnT_t])
                yield

            def lockstep(gens):
                gens = list(gens)
                while gens:
                    nxt = []
                    for g_ in gens:
                        try:
                            next(g_)
                            nxt.append(g_)
                        except StopIteration:
                            pass
                    gens = nxt

            def sm_tmp(tag, region=None):
                Sm, Sm_t = walloc("Sm" + tag, [128, 256], F32, region)
                P, P_t = walloc("P" + tag, [128, 256], F32, region)
                Pn, Pn_t = walloc("Pn" + tag, [128, 256], BF16, region)
                PnT, PnT_t = walloc("PnT" + tag, [128, 256], BF16, region)
                stt, stt_t = walloc("stt" + tag, [128, 8], F32, region)
                return (Sm, Sm_t, P, P_t, Pn, Pn_t, PnT, PnT_t, stt, stt_t)

            ra = [MRG, SB_END]
            kT, kT_t = walloc("kT", [128, 4, 128 + TP], BF16, ra)
            V, V_t = walloc("V", [128, 9, 256], BF16, ra)
            ksT, ksT_t = walloc("ksT", [128, 4, 128], BF16, ra)
            vs, vs_t = walloc("vs", [128, 256], BF16, ra)
            if has_s:
                KcT, KcT_t = walloc("KcT", [128, NSEQ, 4, 128], BF16, ra)
                Vc, Vc_t = walloc("Vc", [128, NSEQ, 256], BF16)
                qz, qz_t = walloc("qz", [128, NSEQ, 128], BF16)
                for hf in range(2):
                    k.dma("pool", KcT[hf * 64:(hf + 1) * 64], kcT[l], writes=[KcT_t], join=(hf > 0))
                k.dma("pool", Vc[:], vc[l], writes=[Vc_t])
                k.dma("sp", o_swk[l][:, 0:120, :], kc_tm[l][:, 8:128, :])
                k.dma("sp", o_swv[l][:, 0:120, :], vc_tm[l][:, 8:128, :])
            qTb = [walloc(f"qTb{i}", [128, TM], BF16) for i in range(2)]
            smt = [sm_tmp(f"a{i}") for i in range(4)]
            kvf, kvf_t = walloc("kvf", [128, 512], F32)
            k.op("act", OPF("activation", out=kT[:, :, 0:128], in_=kcar[:, :, :], func=Cp), reads=[kcar_t], writes=[kT_t])
            k.op("act", OPF("activation", out=V[:, 0, :], in_=vcar[:, :], func=Cp), reads=[vcar_t], writes=[V_t])
            w, wtr, KC, NCc = wload(l, "a_kdup")

            def ev_k(m, t0, tn, ps, pst):
                if t0 < TP:
                    k.op("act", OPF("activation", out=kT[:, m, 128 + t0:128 + t0 + tn], in_=ps[:, 0:tn], func=Cp), reads=[pst], writes=[kT_t], join=True)
                else:
                    k.op("act", OPF("activation", out=ksT[:, m, :], in_=ps[:, 0:128], func=Cp), reads=[pst], writes=[ksT_t], join=True)
            dense_fm(w, wtr, KC, range(4), uT_src, uT_t, T, ev_k)
            if cfg.STOP == "A1":
                k.emit()
                return nc, k, dbg_out
            w, wtr, KC, NCc = wload(l, "a_vk")

            def ev_vk(tt, ps, pst):
                if tt < 8:
                    k.op("act", OPF("activation", out=V[:, tt + 1, :], in_=ps[:, 0:256], func=Cp), reads=[pst], writes=[V_t], join=True)
                    if g == NG - 1 and tt == 7 and cfg.STOP != "A2m":
                        k.op("act", OPF("activation", out=kvf[:, :], in_=ps[:, 0:512], func=Cp), reads=[pst], writes=[kvf_t])
                        if cfg.STOP != "A2v1":
                            k.dma("sp", o_pwv[l], kvf[:, 0:256], reads=[kvf_t])
                            k.dma("sp", o_pwk[l], kvf[:, 256:512], reads=[kvf_t])
                else:
                    k.op("act", OPF("activation", out=vs[:, :], in_=ps[:, 0:256], func=Cp), reads=[pst], writes=[vs_t])
                    if cfg.STOP != "A2m":
                        k.op("act", OPF("activation", out=kvf[:, :], in_=ps[:, 0:512], func=Cp), reads=[pst], writes=[kvf_t])
                    if cfg.STOP not in ("A2x", "A2m", "A2v1"):
                        for s_ in range(NSEQ):
                            k.dma("sp", o_swv[l][s_, 120:128, :], kvf[s_ * 8:(s_ + 1) * 8, 0:256], reads=[kvf_t])
                            k.dma("sp", o_swk[l][s_, 120:128, :], kvf[s_ * 8:(s_ + 1) * 8, 256:512], reads=[kvf_t])
            dense_tm(w, wtr, KC, 512, uT_src, uT_t, range(ntile), ev_vk)
            if cfg.STOP in ("A2", "A2x", "A2m", "A2v1"):
                k.emit()
                return nc, k, dbg_out
            k.op("act", OPF("activation", out=kcar[:, :, :], in_=kT[:, :, TP:TP + 128], func=Cp), reads=[kT_t], writes=[kcar_t])
            k.op("act", OPF("activation", out=vcar[:, :], in_=V[:, 8, :], func=Cp), reads=[V_t], writes=[vcar_t])
            for qi in range(2):
                w, wtr, KC, NCc = wload(l, f"a_q{qi}")
                for mloc in range(4):
                    hp = qi * 4 + mloc
                    kvh = hp // 2
                    qb, qb_t = qTb[hp % 2]
                    for (t0, tn) in tgroups(T):
                        ps, pst = k.ps()
                        for kc in range(KC):
                            mm(ps[:, 0:tn], w[:, kc, mloc * 128:(mloc + 1) * 128], uT[:, kc, t0:t0 + tn], kc == 0, kc == KC - 1, [wtr, uT_t[kc]], pst)
                        k.op("act", OPF("activation", out=qb[:, t0:t0 + tn], in_=ps[:, 0:tn], func=Cp), reads=[pst], writes=[qb_t], join=True)
                    def a_unit(blk, hh, pso, pso_t, tmp, first):
                        head = 2 * hp + hh
                        po_ = hh * 64
                        ps, pst = k.ps()
                        mm(ps[:, 0:256], qb[po_:po_ + 64, blk * 128:(blk + 1) * 128], kT[po_:po_ + 64, kvh, blk * 128:blk * 128 + 256], True, True, [qb_t, kT_t], pst)
                        yield
                        mk_ = maskA["maskA_first"] if (g == 0 and blk == 0) else maskA["maskA_full"]
                        yield from softmax_rows(ps, pst, mk_[0], mk_[1], smallt[:, SINK + head:SINK + head + 1], SCL, tmp)
                        PnT, PnT_t = tmp[6], tmp[7]
                        for c in range(2):
                            k.op("pe", OPF("matmul", pso[po_:po_ + 64, 0:128], lhsT=V[:, blk + c, kvh * 64:(kvh + 1) * 64], rhs=PnT[:, c * 128:(c + 1) * 128], start=(c == 0), stop=(c == 1)),
                                 reads=[V_t, PnT_t], writes=[pso_t], join=not (first and c == 0))
                        yield

                    for b2 in range(0, 8, 2):
                        psos = [k.ps(), k.ps()]
                        units = []
                        for bi_, blk in enumerate((b2, b2 + 1)):
                            for hh in range(2):
                                units.append(a_unit(blk, hh, psos[bi_][0], psos[bi_][1], smt[bi_ * 2 + hh], hh == 0))
                        lockstep(units)
                        for bi_, blk in enumerate((b2, b2 + 1)):
                            k.op("act", OPF("activation", out=obT[:, hp, blk * 128:(blk + 1) * 128], in_=psos[bi_][0][:, 0:128], func=Cp), reads=[psos[bi_][1]], writes=[ob_t[hp]], join=True)
                    if has_s:
                        k.op("dve", OPF("tensor_tensor", out=qz[:, :, :], in0=bcast(qb[:, TP:TM], 1, NSEQ), in1=bm[:, :, :], op=ALU.mult), reads=[qb_t, bm_t], writes=[qz_t])
                        pso, pso_t = k.ps()

                        def s_unit(hh, tmp):
                            head = 2 * hp + hh
                            po_ = hh * 64
                            ps, pst = k.ps()
                            for s_ in range(NSEQ):
                                mm(ps[:, 0:128], qz[po_:po_ + 64, s_, :], KcT[po_:po_ + 64, s_, kvh, :], s_ == 0, s_ == NSEQ - 1, [qz_t, KcT_t], pst)
                            k.op("pe", OPF("matmul", ps[:, 128:256], lhsT=qb[po_:po_ + 64, TP:TM], rhs=ksT[po_:po_ + 64, kvh, :], start=True, stop=True), reads=[qb_t, ksT_t], writes=[pst], join=True)
                            yield
                            mk_ = maskA["maskA_samp"]
                            yield from softmax_rows(ps, pst, mk_[0], mk_[1], smallt[:, SINK + head:SINK + head + 1], SCL, tmp)
                            PnT, PnT_t = tmp[6], tmp[7]
                            k.op("pe", OPF("matmul", pso[po_:po_ + 64, 0:128], lhsT=vs[:, kvh * 64:(kvh + 1) * 64], rhs=PnT[:, 128:256], start=True, stop=False),
                                 reads=[vs_t, PnT_t], writes=[pso_t], join=(hh > 0))
                            for s_ in range(NSEQ):
                                k.op("pe", OPF("matmul", pso[po_:po_ + 64, s_ * 8:(s_ + 1) * 8], lhsT=Vc[:, s_, kvh * 64:(kvh + 1) * 64], rhs=PnT[:, s_ * 8:(s_ + 1) * 8], start=False, stop=(s_ == NSEQ - 1)),
                                     reads=[Vc_t, PnT_t], writes=[pso_t], join=True)
                            yield
                        lockstep([s_unit(0, smt[0]), s_unit(1, smt[1])])
                        k.op("act", OPF("activation", out=obT[:, hp, TP:TM], in_=pso[:, 0:128], func=Cp), reads=[pso_t], writes=[ob_t[hp]], join=True)
            dbg(f"oaT_{l}_{g}", obT[:, :, 0:T], ob_t, [128, 8, T])
            if cfg.STOP == "A":
                k.emit()
                return nc, k, dbg_out
            k.barrier()

            def branch_merge(b, first):
                cnt = 0
                for mb in range(8):
                    wg, ww, wgt, KCg, KCw = wload2(l, f"d_g{b}_{mb}", f"d_w{b}_{mb}")
                    wwt = wgt
                    for mloc in range(2):
                        m = mb * 2 + mloc
                        for (t0, tn) in tgroups(T):
                            sg, sg_t = sg2[cnt % 2]
                            cnt += 1
                            psg, psg_t = k.ps()
                            for kc in range(KCg):
                                mm(psg[:, 0:tn], wg[:, kc, mloc * 128:(mloc + 1) * 128], uT[:, kc, t0:t0 + tn], kc == 0, kc == KCg - 1, [wgt, uT_t[kc]], psg_t)
                            psw, psw_t = k.ps()
                            for kc in range(KCw):
                                mm(psw[:, 0:tn], ww[:, kc, mloc * 128:(mloc + 1) * 128], obT[:, kc, t0:t0 + tn], kc == 0, kc == KCw - 1, [wwt, ob_t[kc]], psw_t)
                            k.op("act", OPF("activation", out=sg[:, 0:tn], in_=psg[:, 0:tn], func=AF.Sigmoid), reads=[psg_t], writes=[sg_t])
                            if first:
                                k.op("dve", OPF("tensor_tensor", out=mergedT[:, m, t0:t0 + tn], in0=psw[:, 0:tn], in1=sg[:, 0:tn], op=ALU.mult),
                                     reads=[psw_t, sg_t], writes=[mg_t[m]], join=True)
                            else:
                                k.op("dve", OPF("tensor_tensor", out=sg[:, 0:tn], in0=psw[:, 0:tn], in1=sg[:, 0:tn], op=ALU.mult), reads=[psw_t, sg_t], writes=[sg_t])
                                k.op("dve", OPF("tensor_tensor", out=mergedT[:, m, t0:t0 + tn], in0=mergedT[:, m, t0:t0 + tn], in1=sg[:, 0:tn], op=ALU.add),
                                     reads=[sg_t, mg_t[m]], writes=[mg_t[m]])

            branch_merge(0, True)
            k.barrier()

            wa[0] = wa_save
            xp, xp_t = walloc("xp", [128, 3 + TP], F32)
            xc, xc_t = walloc("xc", [128, TM], F32)
            xcb, xcb_t = walloc("xcb", [128, TM], BF16)
            rr, rr_t = walloc("rr", [128, TM], F32)
            ig, ig_t = walloc("ig", [128, TM], F32)
            aa, aa_t = walloc("aa", [128, TM], F32)
            m2, m2_t = walloc("m2", [128, TM], F32)
            hh_, hh_t = walloc("hh", [128, TM], F32)
            yv, yv_t = walloc("yv", [128, TM], F32)
            gt, gt_t = walloc("gt", [128, TM], F32)
            if has_s:
                xps, xps_t = walloc("xps", [128, NSEQ, 11], F32)
                h0s, h0s_t = walloc("h0s", [128, 8, NSEQ], F32)
                slo, slo_t = walloc("slo", [128, 8, NSEQ], F32)
                tm16, tm16_t = walloc("tm16", [128, NSEQ], F32)
                k.dma("sp", h0s[:], slru[l], writes=[h0s_t])

            def v3(ap):
                return ap.rearrange("p (s t) -> p s t", t=8)

            for bi in range(4):
                w, wtr, KC, NCc = wload(l, f"b_xy{bi}")
                for j in range(2):
                    cc = 2 * bi + j
                    k.op("dve", OPF("tensor_copy", out=xp[:, 0:3], in_=convcar[:, cc, :]), reads=[convcar_t], writes=[xp_t])
                    if has_s:
                        k.dma("sp", xps[:, :, 0:3], sconv[l][:, cc, :, :], writes=[xps_t])
                    for (t0, tn) in tgroups(T):
                        ps, pst = k.ps()
                        for kc in range(KC):
                            mm(ps[:, 0:tn], w[:, kc, (2 * j) * 128:(2 * j + 1) * 128], uT[:, kc, t0:t0 + tn], kc == 0, kc == KC - 1, [wtr, uT_t[kc]], pst)
                        if t0 < TP:
                            k.op("act", OPF("activation", out=xp[:, 3 + t0:3 + t0 + tn], in_=ps[:, 0:tn], func=Cp), reads=[pst], writes=[xp_t], join=True)
                        else:
                            k.op("act", OPF("activation", out=xps[:, :, 3:11], in_=v3(ps[:, 0:128]), func=Cp), reads=[pst], writes=[xps_t], join=True)
                    cw = lambda jj, cc=cc: smallt[:, CW + cc * 4 + jj:CW + cc * 4 + jj + 1]
                    k.op("dve", OPF("tensor_scalar", out=xc[:, 0:TP], in0=xp[:, 0:TP], scalar1=cw(0), scalar2=smallt[:, CB + cc:CB + cc + 1], op0=ALU.mult, op1=ALU.add), reads=[xp_t, small_t], writes=[xc_t])
                    for jj in range(1, 4):
                        k.op("dve", OPF("scalar_tensor_tensor", out=xc[:, 0:TP], in0=xp[:, jj:jj + TP], scalar=cw(jj), in1=xc[:, 0:TP], op0=ALU.mult, op1=ALU.add), reads=[xp_t, xc_t, small_t], writes=[xc_t])
                    k.op("dve", OPF("tensor_copy", out=convcar[:, cc, :], in_=xp[:, TP:TP + 3]), reads=[xp_t], writes=[convcar_t])
                    if has_s:
                        xcs = v3(xc[:, TP:TM])
                        k.op("dve", OPF("tensor_scalar", out=xcs, in0=xps[:, :, 0:8], scalar1=cw(0), scalar2=smallt[:, CB + cc:CB + cc + 1], op0=ALU.mult, op1=ALU.add), reads=[xps_t, small_t], writes=[xc_t])
                        for jj in range(1, 4):
                            k.op("dve", OPF("scalar_tensor_tensor", out=xcs, in0=xps[:, :, jj:jj + 8], scalar=cw(jj), in1=xcs, op0=ALU.mult, op1=ALU.add), reads=[xps_t, xc_t, small_t], writes=[xc_t])
                        k.dma("sp", o_sconv[l][:, cc, :, :], xps[:, :, 8:11], reads=[xps_t])
                    k.op("act", OPF("activation", out=xcb[:, 0:T], in_=xc[:, 0:T], func=Cp), reads=[xc_t], writes=[xcb_t])
                    for gi, (dst, dst_t, bcol) in enumerate(((rr, rr_t, BA), (ig, ig_t, BX))):
                        for (t0, tn) in tgroups(T):
                            ps, pst = k.ps()
                            mm(ps[:, 0:tn], bdw_t[gi][0][:, cc, :], xcb[:, t0:t0 + tn], True, True, [bdw_t[gi][1], xcb_t], pst)
                            k.op("act", OPF("activation", out=dst[:, t0:t0 + tn], in_=ps[:, 0:tn], func=AF.Sigmoid, bias=smallt[:, bcol + cc:bcol + cc + 1], scale=1.0),
                                 reads=[pst, small_t], writes=[dst_t], join=True)
                    k.op("act", OPF("activation", out=aa[:, 0:T], in_=rr[:, 0:T], func=AF.Exp, scale=cneg[:, cc:cc + 1]), reads=[rr_t, cneg_t], writes=[aa_t])
                    k.op("dve", OPF("tensor_tensor", out=m2[:, 0:T], in0=aa[:, 0:T], in1=aa[:, 0:T], op=ALU.mult), reads=[aa_t], writes=[m2_t])
                    k.op("dve", OPF("tensor_scalar", out=m2[:, 0:T], in0=m2[:, 0:T], scalar1=-1.0, scalar2=1.0, op0=ALU.mult, op1=ALU.add), reads=[m2_t], writes=[m2_t])
                    k.op("act", OPF("activation", out=m2[:, 0:T], in_=m2[:, 0:T], func=AF.Sqrt), reads=[m2_t], writes=[m2_t])
                    if g == 0:
                        k.op("dve", OPF("memset", m2[:, 0:1], 1.0), reads=[], writes=[m2_t])
                    k.op("dve", OPF("tensor_tensor", out=ig[:, 0:T], in0=ig[:, 0:T], in1=xc[:, 0:T], op=ALU.mult), reads=[ig_t, xc_t], writes=[ig_t])
                    k.op("dve", OPF("tensor_tensor", out=ig[:, 0:T], in0=ig[:, 0:T], in1=m2[:, 0:T], op=ALU.mult), reads=[ig_t, m2_t], writes=[ig_t])
                    if has_s:
                        a0 = v3(aa[:, TP:TM])[:, :, 0]
                        u0 = v3(ig[:, TP:TM])[:, :, 0]
                        k.op("dve", OPF("tensor_tensor", out=tm16[:, :], in0=a0, in1=h0s[:, cc, :], op=ALU.mult), reads=[aa_t, h0s_t], writes=[tm16_t])
                        k.op("dve", OPF("tensor_tensor", out=u0, in0=u0, in1=tm16[:, :], op=ALU.add), reads=[ig_t, tm16_t], writes=[ig_t])
                        k.op("dve", OPF("memset", a0, 0.0), reads=[tm16_t], writes=[aa_t])
                    k.op("dve", OPF("tensor_tensor_scan", out=hh_[:, 0:TP], data0=aa[:, 0:TP], data1=ig[:, 0:TP], initial=hcar[:, cc:cc + 1], op0=ALU.mult, op1=ALU.add), reads=[aa_t, ig_t, hcar_t], writes=[hh_t])
                    if has_s:
                        k.op("dve", OPF("tensor_tensor_scan", out=hh_[:, TP:TM], data0=aa[:, TP:TM], data1=ig[:, TP:TM], initial=0.0, op0=ALU.mult, op1=ALU.add), reads=[aa_t, ig_t], writes=[hh_t], join=True)
                        k.op("dve", OPF("tensor_copy", out=slo[:, cc, :], in_=v3(hh_[:, TP:TM])[:, :, 7]), reads=[hh_t], writes=[slo_t])
                    k.op("dve", OPF("tensor_copy", out=hcar[:, cc:cc + 1], in_=hh_[:, TP - 1:TP]), reads=[hh_t], writes=[hcar_t])
                    for (t0, tn) in tgroups(T):
                        ps, pst = k.ps()
                        for kc in range(KC):
                            mm(ps[:, 0:tn], w[:, kc, (2 * j + 1) * 128:(2 * j + 2) * 128], uT[:, kc, t0:t0 + tn], kc == 0, kc == KC - 1, [wtr, uT_t[kc]], pst)
                        k.op("act", OPF("activation", out=yv[:, t0:t0 + tn], in_=ps[:, 0:tn], func=Cp), reads=[pst], writes=[yv_t], join=True)
                    k.op("dve", OPF("tensor_tensor", out=gt[:, 0:T], in0=yv[:, 0:T], in1=yv[:, 0:T], op=ALU.mult), reads=[yv_t], writes=[gt_t])
                    k.op("dve", OPF("tensor_scalar", out=gt[:, 0:T], in0=gt[:, 0:T], scalar1=0.044715, scalar2=1.0, op0=ALU.mult, op1=ALU.add), reads=[gt_t], writes=[gt_t])
                    k.op("dve", OPF("tensor_tensor", out=gt[:, 0:T], in0=gt[:, 0:T], in1=yv[:, 0:T], op=ALU.mult), reads=[gt_t, yv_t], writes=[gt_t])
                    k.op("act", OPF("activation", out=gt[:, 0:T], in_=gt[:, 0:T], func=AF.Sigmoid, scale=1.5957691216057308), reads=[gt_t], writes=[gt_t])
                    k.op("dve", OPF("tensor_tensor", out=gt[:, 0:T], in0=gt[:, 0:T], in1=yv[:, 0:T], op=ALU.mult), reads=[gt_t, yv_t], writes=[gt_t])
                    k.op("dve", OPF("tensor_tensor", out=obT[:, cc, 0:T], in0=gt[:, 0:T], in1=hh_[:, 0:T], op=ALU.mult), reads=[gt_t, hh_t], writes=[ob_t[cc]])
            if has_s:
                k.dma("sp", o_slru[l], slo[:], reads=[slo_t])
            dbg(f"obT_{l}_{g}", obT[:, :, 0:T], ob_t, [128, 8, T])
            if cfg.STOP == "B":
                k.emit()
                return nc, k, dbg_out
            branch_merge(1, False)
            k.barrier()

            wa[0] = wa_save
            qT_all, qT_all_t = walloc("qT_all", [128, 2, TM], BF16)
            kT_all, kT_all_t = walloc("kT_all", [128, 2, TM], BF16)
            qdT_all, qdT_all_t = walloc("qdT_all", [128, 2, TM], BF16)
            kd_all, kd_all_t = walloc("kd_all", [128, 9, 256], BF16)
            rtb = [walloc(f"rt{i}", [128, 2, 2, 128], F32) for i in range(2)]
            t14 = [walloc(f"t14_{i}", [128, 2, 128], F32) for i in range(4)]
            rot, rot_t = walloc("rot", [128, 2, 256], BF16)
            qd, qd_t = walloc("qd", [128, 256], BF16)
            vb2 = [walloc(f"vb{i}", [128, 256], BF16) for i in range(2)]
            sgl2 = [walloc(f"sgl{i}", [128, 256], F32) for i in range(2)]
            itm2 = [walloc(f"itm{i}", [128, 128], BF16) for i in range(2)]
            yy, yy_t = walloc("yy", [128, 256], F32)
            oc, oc_t = walloc("oc", [128, 256], BF16)
            gst, gst_t = walloc("gst", [128, 16], F32)
            gn_sb, gn_sb_t = walloc("gn_sb", [128, 256], F32)
            Sf, Sf_t = walloc("Sf", [128, 4, 2, 256], F32)
            Sb, Sb_t = walloc("Sb", [128, 4, 2, 256], BF16)
            if g == 0:
                k.op("dve", OPF("memset", Sf[:], 0.0), writes=[Sf_t])
            else:
                k.dma("sp", Sf[:, :, :, :].rearrange("p h c e -> p (h c e)"), sscr, reads=[sscr_t], writes=[Sf_t])
            k.op("act", OPF("activation", out=Sb[:, :, :, :], in_=Sf[:, :, :, :], func=Cp), reads=[Sf_t], writes=[Sb_t])
            if has_s:
                Sst, Sst_t = walloc("Sst", [128, 4, 2, 256], BF16)
                Sfs, Sfs_t = walloc("Sfs", [128, 2, 2, 256], F32)
                vexp, vexp_t = walloc("vexp", [128, 4, 256], BF16)
                oTs, oTs_t = walloc("oTs", [128, 2, 128], F32)
                Sn2 = [walloc(f"Sn{i}", [128, 2, 256], F32) for i in range(1)]

            def gn_gate(po, po_t, h, sgl, sgl_t, dst_cols):
                k.op("dve", OPF("bn_stats", out=gst[:, 0:6], in_=po[:, 0:256]), reads=[po_t], writes=[gst_t])
                k.op("dve", OPF("bn_aggr", out=gst[:, 8:10], in_=gst[:, 0:6]), reads=[gst_t], writes=[gst_t])
                k.op("act", OPF("activation", out=gst[:, 10:11], in_=gst[:, 9:10], func=AF.Sqrt, bias=eps_t[:, 0:1], scale=1.0), reads=[gst_t, eps_tt], writes=[gst_t])
                k.op("dve", OPF("reciprocal", out=gst[:, 10:11], in_=gst[:, 10:11]), reads=[gst_t], writes=[gst_t])
                k.op("dve", OPF("tensor_scalar", out=yy[:, :], in0=po[:, 0:256], scalar1=gst[:, 8:9], scalar2=gst[:, 10:11], op0=ALU.subtract, op1=ALU.mult), reads=[po_t, gst_t], writes=[yy_t])
                k.op("dve", OPF("tensor_tensor", out=yy[:, :], in0=yy[:, :], in1=gn_sb[:, :], op=ALU.mult), reads=[yy_t, gn_sb_t], writes=[yy_t])
                k.op("dve", OPF("tensor_tensor", out=oc[:, :], in0=yy[:, :], in1=sgl[:, :], op=ALU.mult), reads=[yy_t, sgl_t], writes=[oc_t])
                pt, ptt = k.ps()
                ptb = pt[:, :].bitcast(BF16)
                for ec in range(2):
                    k.op("pe", OPF("transpose", out=ptb[:, ec * 128:(ec + 1) * 128], in_=oc[:, ec * 128:(ec + 1) * 128], identity=ident_b[:, :]), reads=[oc_t, ident_b_t], writes=[ptt], join=(ec > 0))
                for ec in range(2):
                    k.op("act", OPF("activation", out=obT[:, 2 * h + ec, dst_cols[0]:dst_cols[1]], in_=ptb[:, ec * 128:(ec + 1) * 128], func=Cp), reads=[ptt], writes=[ob_t[2 * h + ec]], join=True)

            NU = 1 if has_s else 2
            c1bufs = [(rtb[0], t14, (rot, rot_t), (qd, qd_t))]
            if NU == 2:
                t14b = [walloc(f"t14b_{i}", [128, 2, 128], F32) for i in range(4)]
                rotb = walloc("rotb", [128, 2, 256], BF16)
                qdb = walloc("qdb", [128, 256], BF16)
                c1bufs.append((rtb[1], t14b, rotb, qdb))
            for h in range(4):
                k.dma("sp", gn_sb[:], gnb[l][:, h * 256:(h + 1) * 256], writes=[gn_sb_t])
                w1, w1t, KC1, _ = wload(l, f"c_qk{h}")

                def c1_unit(tt, bufs):
                    (rt, rt_t), t14_, (rot_, rot_t_), (qd_, qd_t_) = bufs
                    is_s = (tt == 8)
                    cols = (tt * 128, (tt + 1) * 128)
                    k.dma("sp", rt[:], ctab["rot"][32 if is_s else g * 8 + tt], writes=[rt_t])
                    ps, pst = k.ps()
                    for kc in range(KC1):
                        mm(ps[:, 0:512], uT[:, kc, cols[0]:cols[1]], w1[:, kc, 0:512], kc == 0, kc == KC1 - 1, [w1t, uT_t[kc]], pst)
                    yield
                    psv = ps[:, 0:512].rearrange("p (a d two) -> p a d two", a=2, two=2)
                    x1, x2 = psv[:, :, :, 0], psv[:, :, :, 1]
                    cs, sn = rt[:, 0, :, :], rt[:, 1, :, :]
                    for i_, (xa_, tb) in enumerate(((x1, cs), (x2, sn), (x2, cs), (x1, sn))):
                        k.op("dve", OPF("tensor_tensor", out=t14_[i_][0][:, :, :], in0=xa_, in1=tb, op=ALU.mult), reads=[pst, rt_t], writes=[t14_[i_][1]])
                        yield
                    rv = rot_[:, :, :].rearrange("p a (d two) -> p a d two", two=2)
                    k.op("dve", OPF("tensor_tensor", out=rv[:, :, :, 0], in0=t14_[0][0][:, :, :], in1=t14_[1][0][:, :, :], op=ALU.subtract), reads=[t14_[0][1], t14_[1][1]], writes=[rot_t_])
                    yield
                    k.op("dve", OPF("tensor_tensor", out=rv[:, :, :, 1], in0=t14_[2][0][:, :, :], in1=t14_[3][0][:, :, :], op=ALU.add), reads=[t14_[2][1], t14_[3][1]], writes=[rot_t_], join=True)
                    yield
                    dq = (8 if is_s else 0) + h
                    dk = (12 if is_s else 4) + h
                    k.op("dve", OPF("tensor_scalar", out=qd_[:, :], in0=rot_[:, 0, :], scalar1=dec[:, dq:dq + 1], scalar2=None, op0=ALU.mult), reads=[rot_t_, dec_t], writes=[qd_t_])
                    yield
                    k.op("dve", OPF("tensor_scalar", out=kd_all[:, tt, :], in0=rot_[:, 1, :], scalar1=dec[:, dk:dk + 1], scalar2=None, op0=ALU.mult), reads=[rot_t_, dec_t], writes=[kd_all_t], join=True)
                    yield
                    pt, ptt = k.ps()
                    ptb = pt[:, :].bitcast(BF16)
                    for i_, (a_, dc) in enumerate(((0, 0), (0, 1), (1, 0), (1, 1))):
                        k.op("pe", OPF("transpose", out=ptb[:, i_ * 128:(i_ + 1) * 128], in_=rot_[:, a_, dc * 128:(dc + 1) * 128], identity=ident_b[:, :]), reads=[rot_t_, ident_b_t], writes=[ptt], join=(i_ > 0))
                    for dc in range(2):
                        k.op("pe", OPF("transpose", out=ptb[:, (4 + dc) * 128:(5 + dc) * 128], in_=qd_[:, dc * 128:(dc + 1) * 128], identity=ident_b[:, :]), reads=[qd_t_, ident_b_t], writes=[ptt], join=True)
                    yield
                    p3 = lambda lo: ptb[:, lo * 128:(lo + 2) * 128].rearrange("p (c n) -> p c n", c=2)
                    k.op("act", OPF("activation", out=qT_all[:, :, cols[0]:cols[1]], in_=p3(0), func=Cp), reads=[ptt], writes=[qT_all_t], join=True)
                    k.op("act", OPF("activation", out=kT_all[:, :, cols[0]:cols[1]], in_=p3(2), func=Cp), reads=[ptt], writes=[kT_all_t], join=True)
                    k.op("act", OPF("activation", out=qdT_all[:, :, cols[0]:cols[1]], in_=p3(4), func=Cp), reads=[ptt], writes=[qdT_all_t], join=True)
                    yield

                for t0_ in range(0, ntile, NU):
                    lockstep([c1_unit(t0_ + i_, c1bufs[i_]) for i_ in range(NU) if t0_ + i_ < ntile])
                w2, w2t, KC2, _ = wload(l, f"c_vg{h}")

                def c2_front(tt):
                    is_s = (tt == 8)
                    cols = (tt * 128, (tt + 1) * 128)
                    vb, vb_t = vb2[tt % 2]
                    sgl, sgl_t = sgl2[tt % 2]
                    itm, itm_t = itm2[tt % 2]
                    ps, pst = k.ps()
                    for kc in range(KC2):
                        mm(ps[:, 0:512], uT[:, kc, cols[0]:cols[1]], w2[:, kc, 0:512], kc == 0, kc == KC2 - 1, [w2t, uT_t[kc]], pst)
                    yield
                    k.op("act", OPF("activation", out=vb[:, :], in_=ps[:, 0:256], func=Cp), reads=[pst], writes=[vb_t])
                    k.op("act", OPF("activation", out=sgl[:, :], in_=ps[:, 256:512], func=AF.Silu), reads=[pst], writes=[sgl_t])
                    yield
                    pi, pi_t = k.ps()
                    for dc in range(2):
                        mm(pi[:, 0:128], kT_all[:, dc, cols[0]:cols[1]], qT_all[:, dc, cols[0]:cols[1]], dc == 0, dc == 1, [kT_all_t, qT_all_t], pi_t)
                    yield
                    DT_, DT_t = (DTs, DTs_t) if is_s else (DTp, DTp_t)
                    k.op("dve", OPF("tensor_tensor", out=itm[:, :], in0=pi[:, 0:128], in1=DT_[:, h, :], op=ALU.mult), reads=[pi_t, DT_t], writes=[itm_t])
                    yield

                def c2_back(tt):
                    cols = (tt * 128, (tt + 1) * 128)
                    vb, vb_t = vb2[tt % 2]
                    sgl, sgl_t = sgl2[tt % 2]
                    itm, itm_t = itm2[tt % 2]
                    po, po_t = k.ps()
                    mm(po[:, 0:256], itm[:, :], vb[:, :], True, False, [itm_t, vb_t], po_t)
                    for dc in range(2):
                        mm(po[:, 0:256], qdT_all[:, dc, cols[0]:cols[1]], Sb[:, h, dc, :], False, dc == 1, [qdT_all_t, Sb_t], po_t)
                    yield
                    gn_gate(po, po_t, h, sgl, sgl_t, cols)
                    yield
                    for dc in range(2):
                        pu, pu_t = k.ps()
                        mm(pu[:, 0:256], kd_all[:, tt, dc * 128:(dc + 1) * 128], vb[:, :], True, True, [kd_all_t, vb_t], pu_t)
                        k.op("dve", OPF("scalar_tensor_tensor", out=Sf[:, h, dc, :], in0=Sf[:, h, dc, :], scalar=float(GAM[h] ** 128), in1=pu[:, 0:256], op0=ALU.mult, op1=ALU.add), reads=[pu_t, Sf_t], writes=[Sf_t])
                        yield
                    k.op("act", OPF("activation", out=Sb[:, h, :, :], in_=Sf[:, h, :, :], func=Cp), reads=[Sf_t], writes=[Sb_t])
                    yield

                def c2_back_sample(tt):
                    cols = (tt * 128, (tt + 1) * 128)
                    vb, vb_t = vb2[tt % 2]
                    sgl, sgl_t = sgl2[tt % 2]
                    itm, itm_t = itm2[tt % 2]
                    pots = [k.ps(), k.ps()]
                    for ec in range(2):
                        k.op("pe", OPF("matmul", pots[ec][0][:, 0:128], lhsT=vb[:, ec * 128:(ec + 1) * 128], rhs=itm[:, :], start=True, stop=False), reads=[vb_t, itm_t], writes=[pots[ec][1]])
                    for sq in range(4):
                        for c_ in range(2):
                            k.dma("pool", Sst[:, :, c_, :], sret[l][4 * sq:4 * sq + 4, h, c_ * 128:(c_ + 1) * 128, :].rearrange("s d e -> d s e"), writes=[Sst_t], join=(c_ > 0))
                        for s4 in range(4):
                            s_ = 4 * sq + s4
                            for ec in range(2):
                                for dc in range(2):
                                    k.op("pe", OPF("matmul", pots[ec][0][:, s_ * 8:s_ * 8 + 8], lhsT=Sst[:, s4, dc, ec * 128:(ec + 1) * 128], rhs=qdT_all[:, dc, TP + s_ * 8:TP + s_ * 8 + 8], start=False, stop=(dc == 1 and s_ == NSEQ - 1)),
                                         reads=[Sst_t, qdT_all_t], writes=[pots[ec][1]], join=True)
                    for ec in range(2):
                        k.op("act", OPF("activation", out=oTs[:, ec, :], in_=pots[ec][0][:, 0:128], func=Cp), reads=[pots[ec][1]], writes=[oTs_t], join=(ec > 0))
                    po, po_t = k.ps()
                    for ec in range(2):
                        k.op("pe", OPF("transpose", out=po[:, ec * 128:(ec + 1) * 128], in_=oTs[:, ec, :], identity=ident_f[:, :]), reads=[oTs_t, ident_f_t], writes=[po_t], join=(ec > 0))
                    gn_gate(po, po_t, h, sgl, sgl_t, cols)
                    for sq in range(4):
                        k.op("dve", OPF("tensor_tensor", out=vexp[:, :, :], in0=bcast(vb[:, :], 1, 4), in1=bcast(bmv[:, 4 * sq:4 * sq + 4], 2, 256), op=ALU.mult), reads=[vb_t, bmv_t], writes=[vexp_t])
                        for pr in range(2):
                            for c_ in range(2):
                                k.dma("sp", Sfs[:, :, c_, :], sret[l][4 * sq + 2 * pr:4 * sq + 2 * pr + 2, h, c_ * 128:(c_ + 1) * 128, :].rearrange("s d e -> d s e"), writes=[Sfs_t], join=(c_ > 0))
                            for dc in range(2):
                                Sn, Sn_t = Sn2[0]
                                pu, pu_t = k.ps()
                                mm(pu[:, 0:512], kd_all[:, 8, dc * 128:(dc + 1) * 128], vexp[:, 2 * pr:2 * pr + 2, :].rearrange("p s e -> p (s e)"), True, True, [kd_all_t, vexp_t], pu_t)
                                k.op("dve", OPF("scalar_tensor_tensor", out=Sn[:, :, :], in0=Sfs[:, :, dc, :], scalar=float(GAM[h] ** 8), in1=pu[:, 0:512].rearrange("p (s e) -> p s e", s=2), op0=ALU.mult, op1=ALU.add),
                                     reads=[pu_t, Sfs_t], writes=[Sn_t])
                                s0 = 4 * sq + 2 * pr
                                k.dma("sp", o_sret[l][s0:s0 + 2, h, dc * 128:(dc + 1) * 128, :].rearrange("s d e -> d s e"), Sn[:, :, :], reads=[Sn_t])

                lockstep([c2_front(0)])
                for tt in range(8):
                    gens = [c2_back(tt)]
                    if tt + 1 < ntile:
                        gens.append(c2_front(tt + 1))
                    lockstep(gens)
                if has_s:
                    c2_back_sample(8)
            dbg(f"ocT_{l}_{g}", obT[:, :, 0:T], ob_t, [128, 8, T])
            if cfg.STOP == "C":
                k.emit()
                return nc, k, dbg_out
            k.dma("sp", sscr, Sf[:, :, :, :].rearrange("p h c e -> p (h c e)"), reads=[Sf_t], writes=[sscr_t])
            if g == NG - 1:
                k.dma("sp", o_pret[l], Sf[:, :, :, :].rearrange("p h c e -> p (h c e)"), reads=[Sf_t])
            branch_merge(2, False)
            dbg(f"mgT_{l}_{g}", mergedT[:, :, 0:T], mg_t, [128, KD, T])
            if cfg.STOP == "D":
                k.emit()
                return nc, k, dbg_out
            k.barrier()

            for mb in range(4):
                w, wtr, KC, _ = wload(l, f"e_o{mb}")
                for mloc in range(4):
                    m = mb * 4 + mloc
                    rd = [hscr_tr[g][m]] if l > 0 else []
                    k.dma("sp", hT[:, m, 0:T], src[:, m, c0:c0 + T], reads=rd, writes=[hT_t[m]])
                    for (t0, tn) in tgroups(T):
                        ps, pst = k.ps()
                        for kc in range(KC):
                            mm(ps[:, 0:tn], w[:, kc, mloc * 128:(mloc + 1) * 128], mergedT[:, kc, t0:t0 + tn], kc == 0, kc == KC - 1, [wtr, mg_t[kc]], pst)
                        k.op("dve", OPF("tensor_tensor", out=hT[:, m, t0:t0 + tn], in0=ps[:, 0:tn], in1=hT[:, m, t0:t0 + tn], op=ALU.add), reads=[pst, hT_t[m]], writes=[hT_t[m]])
            dbg(f"h1T_{l}_{g}", hT[:, :, 0:T], hT_t, [128, KD, T])
            if cfg.STOP == "E":
                k.emit()
                return nc, k, dbg_out
            k.barrier()

            rmsnorm_to_uT(l, T, G2, hT, hT_t)
            k.barrier()
            xa = [X0, SB_END]
            qxT, qxT_t = walloc("qxT", [128, 4, TM], BF16, xa)
            oxT, oxT_t = walloc("oxT", [128, 4, TM], BF16, xa)
            xsm = [sm_tmp(f"x{i}", xa) for i in range(4)]
            if g == 0:
                xa_save = xa[0]
                mf, mf_t = walloc("mf", [128, 512], F32, xa)
                memTb, memTb_t = walloc("memTb", [128, KD, 256], BF16, xa)
                k.dma("pool", memTb[:], memT, writes=[memTb_t])
                wk_, wkt, KC, _ = wload(l, "x_k")
                msrc = lambda kc, t0, tn: memTb[:, kc, t0:t0 + tn]
                mtr = [memTb_t] * KD

                def ev_mk(m, t0, tn, ps, pst):
                    k.op("act", OPF("activation", out=memK[:, m, :], in_=ps[:, 0:256], func=Cp), reads=[pst], writes=[memK_t], join=True)
                dense_fm(wk_, wkt, KC, range(4), msrc, mtr, 256, ev_mk)

                def ev_mkt(tt, ps, pst):
                    k.op("act", OPF("activation", out=mf[:, :], in_=ps[:, 0:512], func=Cp), reads=[pst], writes=[mf_t])
                    k.dma("sp", o_pmk[l][tt * 128:(tt + 1) * 128, :], mf[:, :], reads=[mf_t])
                dense_tm(wk_, wkt, KC, 512, msrc, mtr, range(2), ev_mkt)
                wv_, wvt, KC, _ = wload(l, "x_v")

                def ev_mvt(tt, ps, pst):
                    k.op("act", OPF("activation", out=memV[:, tt, :], in_=ps[:, 0:512], func=Cp), reads=[pst], writes=[memV_t], join=True)
                    k.op("act", OPF("activation", out=mf[:, :], in_=ps[:, 0:512], func=Cp), reads=[pst], writes=[mf_t])
                    k.dma("sp", o_pmv[l][tt * 128:(tt + 1) * 128, :], mf[:, :], reads=[mf_t])
                dense_tm(wv_, wvt, KC, 512, msrc, mtr, range(2), ev_mvt)
                k.barrier()
                xa[0] = xa_save
                qzx, qzx_t = walloc("qzx", [128, NSEQ, 128], BF16, xa)
                mkh, mkh_t = walloc("mkh", [128, 8, 256], BF16, xa)
                mvh, mvh_t = walloc("mvh", [128, 8, 2, 128], BF16, xa)
            wq_, wqt, KC, _ = wload(l, "x_q")

            def ev_q(m, t0, tn, ps, pst):
                k.op("act", OPF("activation", out=qxT[:, m, t0:t0 + tn], in_=ps[:, 0:tn], func=Cp), reads=[pst], writes=[qxT_t], join=True)
            dense_fm(wq_, wqt, KC, range(4), uT_src, uT_t, T, ev_q)
            XS = 128.0 ** -0.5
            def x_unit(hd, tt, tmp):
                ps, pst = k.ps()
                mm(ps[:, 0:256], qxT[:, hd, tt * 128:(tt + 1) * 128], memK[:, hd, :], True, True, [qxT_t, memK_t], pst)
                yield
                yield from softmax_rows(ps, pst, None, None, None, XS, tmp)
                PnT, PnT_t = tmp[6], tmp[7]
                for mc in range(2):
                    mm(ps[:, 384:512], memV[:, mc, hd * 128:(hd + 1) * 128], PnT[:, mc * 128:(mc + 1) * 128], mc == 0, mc == 1, [memV_t, PnT_t], pst)
                yield
                k.op("act", OPF("activation", out=oxT[:, hd, tt * 128:(tt + 1) * 128], in_=ps[:, 384:512], func=Cp), reads=[pst], writes=[oxT_t], join=True)
                yield

            for hd in range(4):
                for t4 in range(0, 8, 4):
                    lockstep([x_unit(hd, t4 + i_, xsm[i_]) for i_ in range(4)])
                if has_s:
                    tmp = xsm[0]
                    k.op("dve", OPF("tensor_tensor", out=qzx[:, :, :], in0=bcast(qxT[:, hd, TP:TM], 1, NSEQ), in1=bm[:, :, :], op=ALU.mult), reads=[qxT_t, bm_t], writes=[qzx_t])
                    ps, pst = k.ps()
                    for s_ in range(NSEQ):
                        if s_ % 8 == 0:
                            k.dma("pool", mkh[:], mkT[l][:, s_:s_ + 8, hd, :], writes=[mkh_t])
                        mm(ps[:, 0:256], qzx[:, s_, :], mkh[:, s_ % 8, :], s_ == 0, s_ == NSEQ - 1, [qzx_t, mkh_t], pst)
                    for _ in softmax_rows(ps, pst, None, None, None, XS, tmp):
                        pass
                    PnT, PnT_t = tmp[6], tmp[7]
                    pso, pso_t = k.ps()
                    for s_ in range(NSEQ):
                        if s_ % 8 == 0:
                            k.dma("pool", mvh[:], mv[l][:, s_:s_ + 8, :, hd * 128:(hd + 1) * 128], writes=[mvh_t])
                        for mc in range(2):
                            k.op("pe", OPF("matmul", pso[:, s_ * 8:(s_ + 1) * 8], lhsT=mvh[:, s_ % 8, mc, :], rhs=PnT[:, mc * 128 + s_ * 8:mc * 128 + s_ * 8 + 8], start=(mc == 0), stop=(mc == 1)),
                                 reads=[mvh_t, PnT_t], writes=[pso_t], join=not (s_ == 0 and mc == 0))
                    k.op("act", OPF("activation", out=oxT[:, hd, TP:TM], in_=pso[:, 0:128], func=Cp), reads=[pso_t], writes=[oxT_t], join=True)
            wo_, wot, KC, _ = wload(l, "x_o")
            for m in range(KD):
                for (t0, tn) in tgroups(T):
                    ps, pst = k.ps()
                    for kc in range(KC):
                        mm(ps[:, 0:tn], wo_[:, kc, m * 128:(m + 1) * 128], oxT[:, kc, t0:t0 + tn], kc == 0, kc == KC - 1, [wot, oxT_t], pst)
                    k.op("dve", OPF("tensor_tensor", out=hT[:, m, t0:t0 + tn], in0=ps[:, 0:tn], in1=hT[:, m, t0:t0 + tn], op=ALU.add), reads=[pst, hT_t[m]], writes=[hT_t[m]])
            dbg(f"h2T_{l}_{g}", hT[:, :, 0:T], hT_t, [128, KD, T])
            if cfg.STOP == "X":
                k.emit()
                return nc, k, dbg_out
            k.barrier()

            rmsnorm_to_uT(l, T, G3, hT, hT_t)
            k.barrier()
            xa = [X0, SB_END]
            fT2 = [walloc(f"fT{i}", [128, 8, TM], BF16, xa) for i in range(2)]
            fr2 = [walloc(f"fr{i}", [128, 512], F32, xa) for i in range(2)]
            frc = [0]

            def ffn_up(e_):
                fT, fT_t = fT2[e_ % 2]
                for i in range(2):
                    w, wtr, KC, _ = wload(l, f"f_u{e_}_{i}")
                    for mloc in range(4):
                        fc = i * 4 + mloc
                        for (t0, tn) in tgroups(T):
                            fr, fr_t = fr2[frc[0] % 2]
                            frc[0] += 1
                            ps, pst = k.ps()
                            for kc in range(KC):
                                mm(ps[:, 0:tn], w[:, kc, mloc * 128:(mloc + 1) * 128], uT[:, kc, t0:t0 + tn], kc == 0, kc == KC - 1, [wtr, uT_t[kc]], pst)
                            k.op("act", OPF("activation", out=fr[:, 0:tn], in_=ps[:, 0:tn], func=AF.Relu), reads=[pst], writes=[fr_t])
                            k.op("pool", OPF("tensor_tensor", out=fT[:, fc, t0:t0 + tn], in0=fr[:, 0:tn], in1=fr[:, 0:tn], op=ALU.mult), reads=[fr_t], writes=[fT_t], join=True)

            def ffn_down(e_):
                fT, fT_t = fT2[e_ % 2]
                for i in range(2):
                    w, wtr, KC, _ = wload(l, f"f_d{e_}_{i}")
                    for mloc in range(8):
                        m = i * 8 + mloc
                        for (t0, tn) in tgroups(T):
                            ps, pst = k.ps()
                            for kc in range(KC):
                                mm(ps[:, 0:tn], w[:, kc, mloc * 128:(mloc + 1) * 128], fT[:, kc, t0:t0 + tn], kc == 0, kc == KC - 1, [wtr, fT_t], pst)
                            k.op("dve", OPF("tensor_tensor", out=hT[:, m, t0:t0 + tn], in0=ps[:, 0:tn], in1=hT[:, m, t0:t0 + tn], op=ALU.add), reads=[pst, hT_t[m]], writes=[hT_t[m]])

            ffn_up(0)
            for e_ in range(8):
                if e_ < 7:
                    ffn_up(e_ + 1)
                ffn_down(e_)
            dbg(f"h3T_{l}_{g}", hT[:, :, 0:T], hT_t, [128, KD, T])
            if cfg.STOP == "F":
                k.emit()
                return nc, k, dbg_out
            k.barrier()
            if l < L - 1:
                for kc in range(KD):
                    k.dma("sp", hscr[:, kc, c0:c0 + T], hT[:, kc, 0:T], reads=[hT_t[kc]], writes=[hscr_tr[g][kc]])
            else:
                rmsnorm_to_uT(l, T, GF, hT, hT_t, final_out=(yT, c0))
        k.dma("sp", o_pconv[l], convcar[:], reads=[convcar_t])
        k.dma("sp", o_plru[l], hcar[:], reads=[hcar_t])
    t_emit = k.emit()
    return nc, k, dbg_out


def _fm(a):
    t = a.shape[0]
    return np.ascontiguousarray(a.T.reshape(KD, 128, t).transpose(1, 0, 2))


def pack_weights(inp, l):
    plan, wtot = weight_plan()
    srcs = {"w_in": inp["w_in"][l], "w_out": inp["w_out"][l], "w_xq": inp["w_xq"][l], "w_xk": inp["w_xk"][l],
            "w_xv": inp["w_xv"][l], "w_xo": inp["w_xo"][l], "w_up": inp["w_up"][l], "w_down": inp["w_down"][l]}
    for b in range(3):
        srcs[f"w_branch{b}"] = inp["w_branch"][l, b]
    arr = np.empty((128, wtot), np.float32)
    for key, (src, r0, nr, cols, off) in plan.items():
        W = srcs[src][r0:r0 + nr]
        c0, c1 = int(cols[0]), int(cols[-1]) + 1
        if c1 - c0 == len(cols) and np.all(np.diff(cols) == 1):
            W = W[:, c0:c1]
        else:
            W = W[:, cols]
        KC = nr // 128
        arr[:, off:off + KC * len(cols)] = W.reshape(KC, 128, len(cols)).transpose(1, 0, 2).reshape(128, -1)
    return arr


def prep_inputs(inp, cfg, n_cores=8):
    f32 = np.float32
    inp = {k_: np.asarray(v) for k_, v in inp.items()}
    common = {}
    for l in range(cfg.NL):
        common[f"wl{l}"] = pack_weights(inp, l)
    small = np.zeros((L, 128, 144), f32)
    bdw = np.zeros((L, 2, 128, 8, 128), f32)
    gnb = np.zeros((L, 128, 1024), f32)
    pp = lambda v: v.reshape(-1, 128).T
    for l in range(L):
        small[l, :, 0:16] = pp(inp["norm_mix"][l])
        small[l, :, 16:32] = pp(inp["norm_cross"][l])
        small[l, :, 32:48] = pp(inp["norm_ffn"][l])
        small[l, :, 48:64] = pp(inp["norm_final"])
        small[l, :, 64:96] = inp["conv_w"][l].reshape(4, 8, 128).transpose(2, 1, 0).reshape(128, 32)
        small[l, :, 96:104] = pp(inp["conv_b"][l])
        small[l, :, 104:112] = pp(inp["lru_ba"][l])
        small[l, :, 112:120] = pp(inp["lru_bx"][l])
        small[l, :, 120:128] = pp(inp["lru_lambda"][l])
        small[l, :, 128:144] = np.broadcast_to(inp["attn_sink"][l][None, :], (128, 16))
        for i, nm in enumerate(("lru_wa", "lru_wx")):
            w = inp[nm][l]
            for cc in range(8):
                for bn in range(2):
                    bdw[l, i, bn * 64:(bn + 1) * 64, cc, bn * 64:(bn + 1) * 64] = w[2 * cc + bn]
        gnb[l] = np.broadcast_to(inp["ret_gn"][l].reshape(1, 1024), (128, 1024))
    common["small"] = small
    common["bdw"] = bdw
    common["gnb"] = gnb
    for n, v in const_tables().items():
        common["c_" + n] = np.ascontiguousarray(v, dtype=f32)
    maps = []
    for c in range(n_cores):
        m = dict(common)
        xs = inp["x_sample"][c * NSEQ:(c + 1) * NSEQ].reshape(TS, D)
        if c < 2:
            xp = inp["x_prompt"][c]
            mem = inp["mem_prompt"][c]
        else:
            xp = np.zeros((SEQ, D), f32)
            mem = np.zeros((256, D), f32)
        m["xT"] = _fm(np.concatenate([xp[0:TP], xs, xp[TP:]], axis=0))
        m["memT"] = _fm(mem)
        sl = slice(c * NSEQ, (c + 1) * NSEQ)
        ck, cv = inp["cache_win_k"][:, sl], inp["cache_win_v"][:, sl]
        m["kcT"] = np.ascontiguousarray(ck.transpose(0, 4, 1, 3, 2))
        m["vc"] = np.ascontiguousarray(cv.reshape(L, NSEQ, 128, 256).transpose(0, 2, 1, 3))
        m["kc_tm"] = np.ascontiguousarray(ck.reshape(L, NSEQ, 128, 256))
        m["vc_tm"] = np.ascontiguousarray(cv.reshape(L, NSEQ, 128, 256))
        m["sconv"] = np.ascontiguousarray(inp["state_conv"][:, sl].reshape(L, NSEQ, 3, 8, 128).transpose(0, 4, 3, 1, 2))
        m["slru"] = np.ascontiguousarray(inp["state_lru"][:, sl].reshape(L, NSEQ, 8, 128).transpose(0, 3, 2, 1))
        m["sret"] = np.ascontiguousarray(inp["state_ret"][:, sl])
        m["mkT"] = np.ascontiguousarray(inp["cache_mem_k"][:, sl].transpose(0, 4, 1, 3, 2))
        m["mv"] = np.ascontiguousarray(inp["cache_mem_v"][:, sl].reshape(L, NSEQ, 2, 128, 512).transpose(0, 3, 1, 2, 4))
        maps.append(m)
    return maps


_PROG = {}


def run(inp, cfg=None, n_cores=8):
    cfg = cfg or Cfg()
    key = (cfg.NL, cfg.NG, cfg.DBG, cfg.STOP)
    if key not in _PROG:
        _PROG[key] = build_program(cfg)
    nc, k, dbg_out = _PROG[key]
    maps = prep_inputs(inp, cfg, n_cores)
    res = run_bass_kernel_spmd(nc, maps, core_ids=list(range(n_cores)))
    return res.results


def kernel(**inputs):
    r = run(inputs, Cfg(), 8)
    f32 = np.float32
    yp = np.zeros((2, SEQ, D), f32)
    ys = np.zeros((128, 8, D), f32)
    for c in range(8):
        y = r[c]["yT"].transpose(1, 0, 2).reshape(D, TTOT).T
        ys[c * NSEQ:(c + 1) * NSEQ] = y[TP:TP + TS].reshape(NSEQ, 8, D)
        if c < 2:
            yp[c, 0:TP] = y[0:TP]
            yp[c, TP:] = y[TP + TS:]
    st = lambda name, f: np.stack([f(r[c][name]) for c in range(2)], axis=1)
    p_wk = st("o_pwk", lambda a: a.reshape(L, 128, 4, 64))
    p_wv = st("o_pwv", lambda a: a.reshape(L, 128, 4, 64))
    p_conv = st("o_pconv", lambda a: a.transpose(0, 3, 2, 1).reshape(L, 3, 1024))
    p_lru = st("o_plru", lambda a: a.transpose(0, 2, 1).reshape(L, 1024))
    p_ret = st("o_pret", lambda a: a.reshape(L, 128, 4, 2, 256).transpose(0, 2, 3, 1, 4).reshape(L, 4, 256, 256))
    p_mk = st("o_pmk", lambda a: a.reshape(L, 256, 4, 128))
    p_mv = st("o_pmv", lambda a: a.reshape(L, 256, 4, 128))
    cat = lambda name, f: np.concatenate([f(r[c][name]) for c in range(8)], axis=1)
    s_wk = cat("o_swk", lambda a: a.reshape(L, NSEQ, 128, 4, 64))
    s_wv = cat("o_swv", lambda a: a.reshape(L, NSEQ, 128, 4, 64))
    s_conv = cat("o_sconv", lambda a: a.transpose(0, 3, 4, 2, 1).reshape(L, NSEQ, 3, 1024))
    s_lru = cat("o_slru", lambda a: a.transpose(0, 3, 2, 1).reshape(L, NSEQ, 1024))
    s_ret = cat("o_sret", lambda a: a)
    outs = (yp, ys, p_wk, p_wv, p_conv, p_lru, p_ret, p_mk, p_mv, s_wk, s_wv, s_conv, s_lru, s_ret)
    return tuple(np.ascontiguousarray(o, dtype=f32) for o in outs)
```

```python
import contextlib
import numpy as np
import ml_dtypes
import concourse.bass as bass
import concourse.mybir as mybir
from concourse.bass_utils import run_bass_kernel_spmd

F32 = mybir.dt.float32
BF16 = mybir.dt.bfloat16
AF = mybir.ActivationFunctionType
ALU = mybir.AluOpType
AX = mybir.AxisListType

ENGS = ("pe", "dve", "act", "pool", "sp")
DMA_SLOTS = {"sp": 30, "pool": 24, "act": 8}
SAME_ENGINE_SYNC = ("dve", "act", "pool")


class Tr:
    __slots__ = ("name", "w", "r", "prev_r")

    def __init__(self, name):
        self.name = name
        self.w = {}
        self.r = {}
        self.prev_r = {}


def _flat(d):
    out = []
    for k, v in d.items():
        if k == "dma":
            out.extend(v)
        else:
            out.append(v)
    return out


def _add(d, ins):
    if ins.is_dma:
        d.setdefault("dma", []).append(ins)
    else:
        d[ins.eng] = ins


class Ins:
    __slots__ = ("eng", "fn", "deps", "is_dma", "waited", "semval", "slot", "dval", "inc")

    def __init__(self, eng, fn, is_dma):
        self.eng = eng
        self.fn = fn
        self.deps = []
        self.is_dma = is_dma
        self.waited = False
        self.semval = 0
        self.slot = None
        self.dval = 0
        self.inc = 16


class K:
    def __init__(self, nc):
        self.nc = nc
        self.q = {e: [] for e in ENGS}
        self.stack = contextlib.ExitStack()
        self.dma_count = {e: 0 for e in DMA_SLOTS}
        self.slot_last = {e: [None] * n for e, n in DMA_SLOTS.items()}
        self.sb_off = 16512
        self.n_t = 0
        self.ps_rr = 0

    def sbuf(self, name, shape, dtype, off=None):
        esz = 2 if dtype == BF16 else 4
        nbytes = int(np.prod(shape[1:])) * esz
        if off is None:
            off = self.sb_off
            self.sb_off = (off + nbytes + 31) // 32 * 32
            assert self.sb_off <= SB_END, ("SBUF overflow", name, self.sb_off)
        else:
            assert off + nbytes <= SB_END, ("SBUF overflow", name)
        self.n_t += 1
        h = self.nc.alloc_sbuf_tensor_at(f"{name}_{self.n_t}", list(shape), dtype, offset=off)
        return h, Tr(name)

    def psum_banks(self):
        self.banks = []
        for i in range(8):
            h = self.nc.alloc_psum_tensor(f"psb{i}", [128, 512], F32)
            self.banks.append((h, Tr(f"psb{i}")))
        return self.banks

    def ps(self):
        b = self.banks[self.ps_rr % 6]
        self.ps_rr += 1
        return b

    def _record(self, ins, reads, writes, join):
        deps = ins.deps
        for t in reads:
            deps.extend(_flat(t.w))
        for t in writes:
            if join and not t.r:
                deps.extend(_flat(t.prev_r))
            else:
                deps.extend(_flat(t.w))
                deps.extend(_flat(t.r))
        for t in reads:
            _add(t.r, ins)
        for t in writes:
            if join and not t.r:
                _add(t.w, ins)
            else:
                t.prev_r = t.r
                t.r = {}
                t.w = {}
                _add(t.w, ins)
        self.q[ins.eng].append(ins)
        return ins

    def op(self, eng, fn, reads=(), writes=(), join=False):
        return self._record(Ins(eng, fn, False), reads, writes, join)

    def dma(self, q, out, in_, reads=(), writes=(), join=False, **kw):
        ins = Ins(q, (lambda e: e.dma_start(out=out, in_=in_, **kw)), True)
        n = self.dma_count[q]
        self.dma_count[q] = n + 1
        ns = DMA_SLOTS[q]
        slot = n % ns
        ins.slot = (q, slot)
        ins.dval = 16 * (n // ns + 1)
        prev = self.slot_last[q][slot]
        if prev is not None:
            ins.deps.append(prev)
        self.slot_last[q][slot] = ins
        return self._record(ins, reads, writes, join)

    def barrier(self):
        lastc = []
        for e in ENGS:
            for ins in reversed(self.q[e]):
                if not ins.is_dma and ins.fn is not None:
                    lastc.append(ins)
                    break
        dmas = [s for q in self.slot_last for s in self.slot_last[q] if s is not None]
        for e in ENGS:
            b = Ins(e, None, False)
            b.deps = list(lastc) + list(dmas)
            self.q[e].append(b)

    def emit(self):
        nc = self.nc
        st = self.stack
        esem = {e: st.enter_context(nc.semaphore(f"es_{e}")) for e in ENGS}
        dsem = {(q, i): st.enter_context(nc.semaphore(f"ds_{q}{i}")) for q, n in DMA_SLOTS.items() for i in range(n)}
        fin = Ins("sp", None, False)
        fin.deps = [s for q in self.slot_last for s in self.slot_last[q] if s is not None]
        self.q["sp"].append(fin)
        for e in ENGS:
            for ins in self.q[e]:
                for d in ins.deps:
                    if d.is_dma or d.fn is None:
                        continue
                    if d.eng == e and e not in SAME_ENGINE_SYNC:
                        continue
                    d.waited = True
        for e in ENGS:
            c = 0
            for ins in self.q[e]:
                if ins.is_dma or ins.fn is None:
                    continue
                if ins.waited:
                    c += 1
                    ins.semval = c
        stats = {}

        def run(e, eh):
            seen = {}
            nw = 0
            for ins in self.q[e]:
                for d in ins.deps:
                    if d.is_dma:
                        key = d.slot
                        sem = dsem[key]
                        val = d.dval
                    else:
                        if d.fn is None:
                            continue
                        if d.eng == e and e not in SAME_ENGINE_SYNC:
                            continue
                        key = d.eng
                        sem = esem[d.eng]
                        val = d.semval
                    if seen.get(key, 0) >= val:
                        continue
                    seen[key] = val
                    eh.wait_ge(sem, val)
                    nw += 1
                if ins.fn is None:
                    continue
                bi = ins.fn(eh)
                if ins.is_dma:
                    bi.then_inc(dsem[ins.slot], 16)
                elif ins.waited:
                    bi.then_inc(esem[e], 1)
            stats[e] = (len(self.q[e]), nw)

        with nc.Block() as block:
            @block.tensor
            def _(eh):
                run("pe", eh)

            @block.vector
            def _(eh):
                run("dve", eh)

            @block.scalar
            def _(eh):
                run("act", eh)

            @block.gpsimd
            def _(eh):
                run("pool", eh)

            @block.sync
            def _(eh):
                run("sp", eh)
        self.stats = stats
        st.close()


def OPF(name, *a, **kw):
    return lambda e: getattr(e, name)(*a, **kw)


def bcast(ap, axis, n):
    dims = [list(d) for d in ap.ap]
    dims.insert(axis, [0, n])
    return bass.AP(ap.tensor, ap.offset, dims)


SB_END = 229312
D = 2048
KD = 16
L = 2
SEQ = 4096
NGRP = 4
TP = 1024
TS = 128
NSEQ = 16
TTOT = SEQ + TS
IN_W = 13824
C_QA, C_KA, C_VA, C_XR, C_YR, C_QC, C_KC, C_VC, C_GC, C_G = 0, 1024, 1280, 1536, 2560, 3584, 4608, 5632, 6656, 7680
EPS = 1e-6
NEG = -1e30
GAM = [1.0 - 2.0 ** (-5.0 - h) for h in range(4)]


def grp_cols(g):
    if g == 0:
        return 0, TP + TS
    return TP + TS + (g - 1) * TP, TP


def tgroups(T):
    out = []
    t = 0
    while t < T:
        n = min(512, T - t)
        out.append((t, n))
        t += n
    return out


def weight_plan():
    plan = {}
    off = [0]

    def add(key, src, r0, nr, cols):
        cols = np.asarray(cols, dtype=np.int64)
        assert (nr // 128) * len(cols) <= 8192
        plan[key] = (src, r0, nr, cols, off[0])
        off[0] += (nr // 128) * len(cols)

    ar = np.arange
    kd = []
    for kv in range(4):
        c = C_KA + kv * 64 + ar(64)
        kd += [c, c]
    add("a_kdup", "w_in", 0, D, np.concatenate(kd))
    add("a_vk", "w_in", 0, D, np.concatenate([C_VA + ar(256), C_KA + ar(256)]))
    for i in range(2):
        add(f"a_q{i}", "w_in", 0, D, C_QA + i * 512 + ar(512))
    for i in range(4):
        cols = np.concatenate([C_XR + (2 * i) * 128 + ar(128), C_YR + (2 * i) * 128 + ar(128),
                               C_XR + (2 * i + 1) * 128 + ar(128), C_YR + (2 * i + 1) * 128 + ar(128)])
        add(f"b_xy{i}", "w_in", 0, D, cols)
    for h in range(4):
        add(f"c_qk{h}", "w_in", 0, D, np.concatenate([C_QC + h * 256 + ar(256), C_KC + h * 256 + ar(256)]))
        add(f"c_vg{h}", "w_in", 0, D, np.concatenate([C_VC + h * 256 + ar(256), C_GC + h * 256 + ar(256)]))
    for b in range(3):
        for mb in range(4):
            pass
        for mb in range(8):
            add(f"d_g{b}_{mb}", "w_in", 0, D, C_G + b * D + mb * 256 + ar(256))
            add(f"d_w{b}_{mb}", f"w_branch{b}", 0, 1024, mb * 256 + ar(256))
    for mb in range(4):
        add(f"e_o{mb}", "w_out", 0, D, mb * 512 + ar(512))
    add("x_q", "w_xq", 0, D, ar(512))
    add("x_k", "w_xk", 0, D, ar(512))
    add("x_v", "w_xv", 0, D, ar(512))
    add("x_o", "w_xo", 0, 512, ar(2048))
    for e in range(8):
        for i in range(2):
            add(f"f_u{e}_{i}", "w_up", 0, D, e * 1024 + i * 512 + ar(512))
        for i in range(2):
            add(f"f_d{e}_{i}", "w_down", e * 1024, 1024, i * 1024 + ar(1024))
    return plan, off[0]


def const_tables():
    c = {}
    i = np.arange(128)[:, None]
    j = np.arange(256)[None, :]
    full = (j > i) & (j <= i + 128)
    c["maskA_full"] = np.where(full, 0.0, NEG).astype(np.float32)
    c["maskA_first"] = np.where(full & (j >= 128), 0.0, NEG).astype(np.float32)
    s = np.arange(128) // 8
    t = np.arange(128) % 8
    ms = np.zeros((128, 256), bool)
    ms[:, :128] = np.arange(128)[None, :] >= (t[:, None] + 1)
    ms[:, 128:] = (s[:, None] == s[None, :]) & (t[None, :] <= t[:, None])
    c["maskA_samp"] = np.where(ms, 0.0, NEG).astype(np.float32)
    kk = np.arange(128)[:, None]
    qq = np.arange(128)[None, :]
    dtp = np.zeros((128, 4, 128), np.float64)
    dts = np.zeros((128, 4, 128), np.float64)
    for h in range(4):
        lg = np.log(GAM[h])
        dtp[:, h, :] = np.where(qq >= kk, np.exp(np.maximum(qq - kk, 0) * lg), 0.0)
        same = (s[:, None] == s[None, :]) & (t[None, :] >= t[:, None])
        dts[:, h, :] = np.where(same, np.exp(np.maximum(t[None, :] - t[:, None], 0) * lg), 0.0)
    c["DTp"] = dtp.astype(np.float32)
    c["DTs"] = dts.astype(np.float32)
    dec = np.zeros((128, 16), np.float64)
    n = np.arange(128)
    for h in range(4):
        lg = np.log(GAM[h])
        dec[:, h] = np.exp((n + 1.0) * lg)
        dec[:, 4 + h] = np.exp((127.0 - n) * lg)
        dec[:, 8 + h] = np.exp((t + 1.0) * lg)
        dec[:, 12 + h] = np.exp((7.0 - t) * lg)
    c["dec"] = dec.astype(np.float32)
    inv = (1.0 / (10000.0 ** np.linspace(0.0, 1.0, 128, dtype=np.float32))).astype(np.float32)
    pos = np.zeros((33, 128), np.float32)
    pos[:32] = np.arange(4096, dtype=np.float32).reshape(32, 128)
    pos[32] = 8192.0 + t
    ang = (pos[:, :, None] * inv[None, None, :]).astype(np.float32)
    cs, sn = np.cos(ang).astype(np.float32), np.sin(ang).astype(np.float32)
    c["rot"] = np.stack([cs, cs / 16.0, sn, sn / 16.0], axis=2).astype(np.float32)
    c["ident"] = np.eye(128, dtype=np.float32)
    bm = (s[None, :] == np.arange(16)[:, None]).astype(np.float32)
    c["bm"] = np.broadcast_to(bm[None], (128, 16, 128)).copy()
    c["bmv"] = (s[:, None] == np.arange(16)[None, :]).astype(np.float32)
    return c


class Cfg:
    NL = 2
    NG = 4
    DBG = False
    STOP = None


def build_program(cfg):
    nc = bass.Bass("TRN2", target_bir_lowering=False)
    k = K(nc)
    plan, wtot = weight_plan()
    NL, NG = cfg.NL, cfg.NG
    dbg_out = {}

    def din(name, shape, dt=F32):
        return nc.dram_tensor(name, list(shape), dt, kind="ExternalInput").ap()

    def dout(name, shape):
        return nc.dram_tensor(name, list(shape), F32, kind="ExternalOutput").ap()

    xT = din("xT", [128, KD, TTOT])
    memT = din("memT", [128, KD, 256])
    wl = [din(f"wl{l}", [128, wtot]) for l in range(cfg.NL)]
    small = din("small", [L, 128, 16 * 4 + 8 * 4 + 8 * 4 + 16])
    bdw = din("bdw", [L, 2, 128, 8, 128])
    gnb = din("gnb", [L, 128, 1024])
    ctab = {n: din("c_" + n, v.shape) for n, v in const_tables().items()}
    kcT = din("kcT", [L, 64, NSEQ, 4, 128])
    vc = din("vc", [L, 128, NSEQ, 256])
    kc_tm = din("kc_tm", [L, NSEQ, 128, 256])
    vc_tm = din("vc_tm", [L, NSEQ, 128, 256])
    sconv = din("sconv", [L, 128, 8, NSEQ, 3])
    slru = din("slru", [L, 128, 8, NSEQ])
    sret = din("sret", [L, NSEQ, 4, 256, 256])
    mkT = din("mkT", [L, 128, NSEQ, 4, 256])
    mv = din("mv", [L, 128, NSEQ, 2, 512])

    yT = dout("yT", [128, KD, TTOT])
    o_pwk = dout("o_pwk", [L, 128, 256])
    o_pwv = dout("o_pwv", [L, 128, 256])
    o_pconv = dout("o_pconv", [L, 128, 8, 3])
    o_plru = dout("o_plru", [L, 128, 8])
    o_pret = dout("o_pret", [L, 128, 4 * 2 * 256])
    o_pmk = dout("o_pmk", [L, 256, 512])
    o_pmv = dout("o_pmv", [L, 256, 512])
    o_swk = dout("o_swk", [L, NSEQ, 128, 256])
    o_swv = dout("o_swv", [L, NSEQ, 128, 256])
    o_sconv = dout("o_sconv", [L, 128, 8, NSEQ, 3])
    o_slru = dout("o_slru", [L, 128, 8, NSEQ])
    o_sret = dout("o_sret", [L, NSEQ, 4, 256, 256])
    hscr = nc.dram_tensor("hscr", [128, KD, TTOT], F32, kind="Internal").ap()
    hscr_tr = [[Tr(f"hscr{g}_{kc}") for kc in range(KD)] for g in range(NGRP)]
    sscr = nc.dram_tensor("sscr", [128, 2048], F32, kind="Internal").ap()
    sscr_t = Tr("sscr")

    def dbg(name, ap_sb, tr, shape):
        if not cfg.DBG:
            return
        o = dout("dbg_" + name, shape)
        dbg_out[name] = shape
        k.dma("pool", o, ap_sb, reads=tr)

    banks = k.psum_banks()
    TM = TP + TS
    ident_f, ident_f_t = k.sbuf("ident_f", [128, 128], F32)
    ident_b, ident_b_t = k.sbuf("ident_b", [128, 128], BF16)
    ones_b, ones_b_t = k.sbuf("ones_b", [128, 128], BF16)
    maskA = {n: k.sbuf(n, [128, 256], F32) for n in ("maskA_full", "maskA_first", "maskA_samp")}
    DTp, DTp_t = k.sbuf("DTp", [128, 4, 128], F32)
    DTs, DTs_t = k.sbuf("DTs", [128, 4, 128], F32)
    dec, dec_t = k.sbuf("dec", [128, 16], F32)
    bm, bm_t = k.sbuf("bm", [128, 16, 128], BF16)
    bmv, bmv_t = k.sbuf("bmv", [128, 16], F32)
    eps_t, eps_tt = k.sbuf("eps", [128, 1], F32)
    smallt, small_t = k.sbuf("small", [128, 144], F32)
    cneg, cneg_t = k.sbuf("cneg", [128, 8], F32)
    bdw_t = [k.sbuf(f"bdw{i}", [128, 8, 128], BF16) for i in range(2)]
    kcar, kcar_t = k.sbuf("kcar", [128, 4, 128], BF16)
    vcar, vcar_t = k.sbuf("vcar", [128, 256], BF16)
    convcar, convcar_t = k.sbuf("convcar", [128, 8, 3], F32)
    hcar, hcar_t = k.sbuf("hcar", [128, 8], F32)
    memK, memK_t = k.sbuf("memK", [128, 4, 256], BF16)
    memV, memV_t = k.sbuf("memV", [128, 2, 512], BF16)
    NRING = 2
    ring = [k.sbuf(f"wring{i}", [128, 8192], BF16) for i in range(NRING)]
    ring_i = [0]
    uT, _ = k.sbuf("uT", [128, KD, TM], BF16)
    uT_t = [Tr(f"uT{kc}") for kc in range(KD)]
    R0 = k.sb_off
    hT, _ = k.sbuf("hT", [128, KD, TM], F32)
    hT_t = [Tr(f"hT{kc}") for kc in range(KD)]
    X0 = k.sb_off
    XSZ = SB_END - X0
    print("SBUF: R0", R0, "X0", X0, "X size", XSZ)

    Sq, Id, Cp = AF.Square, AF.Identity, AF.Copy

    def wload(l, key):
        src, r0, nr, cols, off = plan[key]
        KC, NCc = nr // 128, len(cols)
        h, tr = ring[ring_i[0] % NRING]
        ring_i[0] += 1
        k.dma("pool", h[:, 0:KC * NCc], wl[l][:, off:off + KC * NCc], writes=[tr])
        return h[:, 0:KC * NCc].rearrange("p (k n) -> p k n", k=KC), tr, KC, NCc

    def wload2(l, keyA, keyB):
        sa, ra_, nra, ca, offa = plan[keyA]
        sb_, rb_, nrb, cb, offb = plan[keyB]
        KA, NA, KB, NB = nra // 128, len(ca), nrb // 128, len(cb)
        assert offb == offa + KA * NA and KA * NA + KB * NB <= 8192
        h, tr = ring[ring_i[0] % NRING]
        ring_i[0] += 1
        tot = KA * NA + KB * NB
        k.dma("pool", h[:, 0:tot], wl[l][:, offa:offa + tot], writes=[tr])
        va = h[:, 0:KA * NA].rearrange("p (k n) -> p k n", k=KA)
        vb = h[:, KA * NA:tot].rearrange("p (k n) -> p k n", k=KB)
        return va, vb, tr, KA, KB

    def mm(out, lhsT, rhs, start, stop, reads, pst):
        k.op("pe", OPF("matmul", out, lhsT=lhsT, rhs=rhs, start=start, stop=stop), reads=reads, writes=[pst], join=not start)

    def dense_fm(w, wtr, KC, m_list, xsrc, xtrs, T, evac):
        for m in m_list:
            for (t0, tn) in tgroups(T):
                ps, pst = k.ps()
                for kc in range(KC):
                    mm(ps[:, 0:tn], w[:, kc, m * 128:(m + 1) * 128], xsrc(kc, t0, tn), kc == 0, kc == KC - 1, [wtr, xtrs[kc]], pst)
                evac(m, t0, tn, ps, pst)

    def dense_tm(w, wtr, KC, NCc, xsrc, xtrs, tiles, evac):
        for tt in tiles:
            ps, pst = k.ps()
            for kc in range(KC):
                mm(ps[:, 0:NCc], xsrc(kc, tt * 128, 128), w[:, kc, 0:NCc], kc == 0, kc == KC - 1, [wtr, xtrs[kc]], pst)
            evac(tt, ps, pst)

    uT_src = lambda kc, t0, tn: uT[:, kc, t0:t0 + tn]

    def rmsnorm_to_uT(l, T, gcol, src_h, src_tr, final_out=None):
        sq, sq_t = k.sbuf("sq", [128, 2, TM], BF16, off=X0)
        sq_tr = [Tr("sq0"), Tr("sq1")]
        rstd, rstd_t = k.sbuf("rstd", [128, TM], F32, off=X0 + 2 * TM * 2)
        tg = tgroups(T)
        pss = [banks[6], banks[7], k.ps()][:len(tg)]
        for kc in range(KD):
            j = kc % 2
            k.op("act", OPF("activation", out=sq[:, j, 0:T], in_=src_h[:, kc, 0:T], func=Sq), reads=[src_tr[kc]], writes=[sq_tr[j]])
            for i, (t0, tn) in enumerate(tg):
                ps, pst = pss[i]
                mm(ps[:, 0:tn], ones_b[:, :], sq[:, j, t0:t0 + tn], kc == 0, kc == KD - 1, [sq_tr[j], ones_b_t], pst)
        for i, (t0, tn) in enumerate(tg):
            ps, pst = pss[i]
            k.op("act", OPF("activation", out=rstd[:, t0:t0 + tn], in_=ps[:, 0:tn], func=AF.Sqrt, scale=1.0 / D, bias=eps_t[:, 0:1]), reads=[pst, eps_tt], writes=[rstd_t], join=(i > 0))
        k.op("dve", OPF("reciprocal", out=rstd[:, 0:T], in_=rstd[:, 0:T]), reads=[rstd_t], writes=[rstd_t])
        for kc in range(KD):
            if final_out is None:
                k.op("dve", OPF("scalar_tensor_tensor", out=uT[:, kc, 0:T], in0=src_h[:, kc, 0:T], scalar=smallt[:, gcol + kc:gcol + kc + 1], in1=rstd[:, 0:T], op0=ALU.mult, op1=ALU.mult),
                     reads=[src_tr[kc], rstd_t, small_t], writes=[uT_t[kc]])
            else:
                yo, yc0 = final_out
                k.op("dve", OPF("scalar_tensor_tensor", out=src_h[:, kc, 0:T], in0=src_h[:, kc, 0:T], scalar=smallt[:, gcol + kc:gcol + kc + 1], in1=rstd[:, 0:T], op0=ALU.mult, op1=ALU.mult),
                     reads=[src_tr[kc], rstd_t, small_t], writes=[src_tr[kc]])
                k.dma("sp", yo[:, kc, yc0:yc0 + T], src_h[:, kc, 0:T], reads=[src_tr[kc]])

    k.op("dve", OPF("memset", eps_t[:], EPS), writes=[eps_tt])
    k.dma("sp", ident_f[:], ctab["ident"], writes=[ident_f_t])
    k.dma("pool", ident_b[:], ctab["ident"], writes=[ident_b_t])
    k.op("dve", OPF("memset", ones_b[:], 1.0), writes=[ones_b_t])
    for n in maskA:
        k.dma("sp", maskA[n][0][:], ctab[n], writes=[maskA[n][1]])
    k.dma("sp", DTp[:], ctab["DTp"], writes=[DTp_t])
    k.dma("sp", DTs[:], ctab["DTs"], writes=[DTs_t])
    k.dma("sp", dec[:], ctab["dec"], writes=[dec_t])
    k.dma("pool", bm[:], ctab["bm"], writes=[bm_t])
    k.dma("sp", bmv[:], ctab["bmv"], writes=[bmv_t])

    for l in range(NL):
        k.dma("sp", smallt[:], small[l], writes=[small_t])
        for i in range(2):
            k.dma("pool", bdw_t[i][0][:], bdw[l, i], writes=[bdw_t[i][1]])
        G1, G2, G3, GF, CW, CB, BA, BX, LAM, SINK = 0, 16, 32, 48, 64, 96, 104, 112, 120, 128
        k.op("act", OPF("activation", out=cneg[:], in_=smallt[:, LAM:LAM + 8], func=AF.Exp, scale=-1.0), reads=[small_t], writes=[cneg_t])
        k.op("act", OPF("activation", out=cneg[:], in_=cneg[:], func=AF.Ln, bias=1.0), reads=[cneg_t], writes=[cneg_t])
        k.op("dve", OPF("tensor_scalar", out=cneg[:], in0=cneg[:], scalar1=-8.0, scalar2=None, op0=ALU.mult), reads=[cneg_t], writes=[cneg_t])
        k.op("dve", OPF("memset", kcar[:], 0.0), writes=[kcar_t])
        k.op("dve", OPF("memset", vcar[:], 0.0), writes=[vcar_t])
        k.op("dve", OPF("memset", convcar[:], 0.0), writes=[convcar_t])
        k.op("dve", OPF("memset", hcar[:], 0.0), writes=[hcar_t])

        for g in range(NG):
            c0, T = grp_cols(g)
            has_s = (g == 0)
            ntile = T // 128
            src = xT if l == 0 else hscr
            k.barrier()
            for kc in range(KD):
                rd = [hscr_tr[g][kc]] if l > 0 else []
                k.dma("sp", hT[:, kc, 0:T], src[:, kc, c0:c0 + T], reads=rd, writes=[hT_t[kc]])
            rmsnorm_to_uT(l, T, G1, hT, hT_t)
            dbg(f"u_{l}_{g}", uT[:, :, 0:T], uT_t, [128, KD, T])
            if cfg.STOP == "u":
                k.emit()
                return nc, k, dbg_out
            k.barrier()
            MRG = SB_END - KD * TM * 2
            assert MRG >= X0, (MRG, X0)
            mergedT, _ = k.sbuf("mergedT", [128, KD, TM], BF16, off=MRG)
            mg_t = [Tr(f"mg{m}") for m in range(KD)]
            obT, _ = k.sbuf("obT", [128, 8, TM], BF16, off=R0)
            ob_t = [Tr(f"ob{c}") for c in range(8)]
            wa = [R0 + 8 * TM * 2, MRG]

            def walloc(name, shape, dt, region=None):
                r = wa if region is None else region
                esz = 2 if dt == BF16 else 4
                nb = (int(np.prod(shape[1:])) * esz + 31) // 32 * 32
                assert r[0] + nb <= r[1], ("work area overflow", name, r[0] + nb - r[1])
                h, t = k.sbuf(name, shape, dt, off=r[0])
                r[0] += nb
                return h, t

            SCL = 0.125
            sg2 = [walloc(f"sg_{i}", [128, 512], F32) for i in range(2)]
            wa_save = wa[0]

            def softmax_rows(ps, pst, mask, mask_t, sinkcol, scale, tmp):
                Sm, Sm_t, P, P_t, Pn, Pn_t, PnT, PnT_t, stt, stt_t = tmp
                if mask is not None:
                    k.op("dve", OPF("scalar_tensor_tensor", out=Sm[:, :], in0=ps[:, 0:256], scalar=scale, in1=mask[:, :], op0=ALU.mult, op1=ALU.add), reads=[pst, mask_t], writes=[Sm_t])
                else:
                    k.op("dve", OPF("tensor_scalar", out=Sm[:, :], in0=ps[:, 0:256], scalar1=scale, scalar2=None, op0=ALU.mult), reads=[pst], writes=[Sm_t])
                yield
                k.op("dve", OPF("reduce_max", out=stt[:, 0:1], in_=Sm[:, :], axis=AX.X), reads=[Sm_t], writes=[stt_t])
                yield
                if sinkcol is not None:
                    k.op("dve", OPF("tensor_scalar", out=stt[:, 1:2], in0=stt[:, 0:1], scalar1=sinkcol, scalar2=-1.0, op0=ALU.max, op1=ALU.mult), reads=[stt_t, small_t], writes=[stt_t])
                else:
                    k.op("dve", OPF("tensor_scalar", out=stt[:, 1:2], in0=stt[:, 0:1], scalar1=-1.0, scalar2=None, op0=ALU.mult), reads=[stt_t], writes=[stt_t])
                yield
                k.op("act", OPF("activation", out=P[:, :], in_=Sm[:, :], func=AF.Exp, bias=stt[:, 1:2], scale=1.0, accum_out=stt[:, 2:3]), reads=[Sm_t, stt_t], writes=[P_t, stt_t])
                yield
                if sinkcol is not None:
                    k.op("act", OPF("activation", out=stt[:, 3:4], in_=sinkcol, func=AF.Exp, bias=stt[:, 1:2], scale=1.0), reads=[stt_t, small_t], writes=[stt_t])
                    yield
                    k.op("dve", OPF("tensor_tensor", out=stt[:, 2:3], in0=stt[:, 2:3], in1=stt[:, 3:4], op=ALU.add), reads=[stt_t], writes=[stt_t])
                    yield
                k.op("dve", OPF("reciprocal", out=stt[:, 4:5], in_=stt[:, 2:3]), reads=[stt_t], writes=[stt_t])
                yield
                k.op("dve", OPF("tensor_scalar", out=Pn[:, :], in0=P[:, :], scalar1=stt[:, 4:5], scalar2=None, op0=ALU.mult), reads=[P_t, stt_t], writes=[Pn_t])
                yield
                ptb = ps[:, :].bitcast(BF16)
                for c in range(2):
                    k.op("pe", OPF("transpose", out=ptb[:, 512 + c * 128:512 + (c + 1) * 128], in_=Pn[:, c * 128:(c + 1) * 128], identity=ident_b[:, :]), reads=[Pn_t, ident_b_t], writes=[pst], join=(c > 0))
                yield
                k.op("act", OPF("activation", out=PnT[:, :], in_=ptb[:, 512:768], func=Cp), reads=[pst], writes=[PnT_t])
                yield

            def lockstep(gens):
                gens = list(gens)
                while gens:
                    nxt = []
                    for g_ in gens:
                        try:
                            next(g_)
                            nxt.append(g_)
                        except StopIteration:
                            pass
                    gens = nxt

            def sm_tmp(tag, region=None):
                Sm, Sm_t = walloc("Sm" + tag, [128, 256], F32, region)
                P, P_t = walloc("P" + tag, [128, 256], F32, region)
                Pn, Pn_t = walloc("Pn" + tag, [128, 256], BF16, region)
                PnT, PnT_t = walloc("PnT" + tag, [128, 256], BF16, region)
                stt, stt_t = walloc("stt" + tag, [128, 8], F32, region)
                return (Sm, Sm_t, P, P_t, Pn, Pn_t, PnT, PnT_t, stt, stt_t)

            ra = [MRG, SB_END]
            kT, kT_t = walloc("kT", [128, 4, 128 + TP], BF16, ra)
            V, V_t = walloc("V", [128, 9, 256], BF16, ra)
            ksT, ksT_t = walloc("ksT", [128, 4, 128], BF16, ra)
            vs, vs_t = walloc("vs", [128, 256], BF16, ra)
            if has_s:
                KcT, KcT_t = walloc("KcT", [128, NSEQ, 4, 128], BF16, ra)
                Vc, Vc_t = walloc("Vc", [128, NSEQ, 256], BF16)
                qz, qz_t = walloc("qz", [128, NSEQ, 128], BF16)
                for hf in range(2):
                    k.dma("pool", KcT[hf * 64:(hf + 1) * 64], kcT[l], writes=[KcT_t], join=(hf > 0))
                k.dma("pool", Vc[:], vc[l], writes=[Vc_t])
                k.dma("sp", o_swk[l][:, 0:120, :], kc_tm[l][:, 8:128, :])
                k.dma("sp", o_swv[l][:, 0:120, :], vc_tm[l][:, 8:128, :])
            qTb = [walloc(f"qTb{i}", [128, TM], BF16) for i in range(2)]
            smt = [sm_tmp(f"a{i}") for i in range(4)]
            kvf, kvf_t = walloc("kvf", [128, 512], F32)
            k.op("act", OPF("activation", out=kT[:, :, 0:128], in_=kcar[:, :, :], func=Cp), reads=[kcar_t], writes=[kT_t])
            k.op("act", OPF("activation", out=V[:, 0, :], in_=vcar[:, :], func=Cp), reads=[vcar_t], writes=[V_t])
            w, wtr, KC, NCc = wload(l, "a_kdup")

            def ev_k(m, t0, tn, ps, pst):
                if t0 < TP:
                    k.op("act", OPF("activation", out=kT[:, m, 128 + t0:128 + t0 + tn], in_=ps[:, 0:tn], func=Cp), reads=[pst], writes=[kT_t], join=True)
                else:
                    k.op("act", OPF("activation", out=ksT[:, m, :], in_=ps[:, 0:128], func=Cp), reads=[pst], writes=[ksT_t], join=True)
            dense_fm(w, wtr, KC, range(4), uT_src, uT_t, T, ev_k)
            if cfg.STOP == "A1":
                k.emit()
                return nc, k, dbg_out
            w, wtr, KC, NCc = wload(l, "a_vk")

            def ev_vk(tt, ps, pst):
                if tt < 8:
                    k.op("act", OPF("activation", out=V[:, tt + 1, :], in_=ps[:, 0:256], func=Cp), reads=[pst], writes=[V_t], join=True)
                    if g == NG - 1 and tt == 7 and cfg.STOP != "A2m":
                        k.op("act", OPF("activation", out=kvf[:, :], in_=ps[:, 0:512], func=Cp), reads=[pst], writes=[kvf_t])
                        if cfg.STOP != "A2v1":
                            k.dma("sp", o_pwv[l], kvf[:, 0:256], reads=[kvf_t])
                            k.dma("sp", o_pwk[l], kvf[:, 256:512], reads=[kvf_t])
                else:
                    k.op("act", OPF("activation", out=vs[:, :], in_=ps[:, 0:256], func=Cp), reads=[pst], writes=[vs_t])
                    if cfg.STOP != "A2m":
                        k.op("act", OPF("activation", out=kvf[:, :], in_=ps[:, 0:512], func=Cp), reads=[pst], writes=[kvf_t])
                    if cfg.STOP not in ("A2x", "A2m", "A2v1"):
                        for s_ in range(NSEQ):
                            k.dma("sp", o_swv[l][s_, 120:128, :], kvf[s_ * 8:(s_ + 1) * 8, 0:256], reads=[kvf_t])
                            k.dma("sp", o_swk[l][s_, 120:128, :], kvf[s_ * 8:(s_ + 1) * 8, 256:512], reads=[kvf_t])
            dense_tm(w, wtr, KC, 512, uT_src, uT_t, range(ntile), ev_vk)
            if cfg.STOP in ("A2", "A2x", "A2m", "A2v1"):
                k.emit()
                return nc, k, dbg_out
            k.op("act", OPF("activation", out=kcar[:, :, :], in_=kT[:, :, TP:TP + 128], func=Cp), reads=[kT_t], writes=[kcar_t])
            k.op("act", OPF("activation", out=vcar[:, :], in_=V[:, 8, :], func=Cp), reads=[V_t], writes=[vcar_t])
            for qi in range(2):
                w, wtr, KC, NCc = wload(l, f"a_q{qi}")
                for mloc in range(4):
                    hp = qi * 4 + mloc
                    kvh = hp // 2
                    qb, qb_t = qTb[hp % 2]
                    for (t0, tn) in tgroups(T):
                        ps, pst = k.ps()
                        for kc in range(KC):
                            mm(ps[:, 0:tn], w[:, kc, mloc * 128:(mloc + 1) * 128], uT[:, kc, t0:t0 + tn], kc == 0, kc == KC - 1, [wtr, uT_t[kc]], pst)
                        k.op("act", OPF("activation", out=qb[:, t0:t0 + tn], in_=ps[:, 0:tn], func=Cp), reads=[pst], writes=[qb_t], join=True)
                    def a_unit(blk, hh, pso, pso_t, tmp, first):
                        head = 2 * hp + hh
                        po_ = hh * 64
                        ps, pst = k.ps()
                        mm(ps[:, 0:256], qb[po_:po_ + 64, blk * 128:(blk + 1) * 128], kT[po_:po_ + 64, kvh, blk * 128:blk * 128 + 256], True, True, [qb_t, kT_t], pst)
                        yield
                        mk_ = maskA["maskA_first"] if (g == 0 and blk == 0) else maskA["maskA_full"]
                        yield from softmax_rows(ps, pst, mk_[0], mk_[1], smallt[:, SINK + head:SINK + head + 1], SCL, tmp)
                        PnT, PnT_t = tmp[6], tmp[7]
                        for c in range(2):
                            k.op("pe", OPF("matmul", pso[po_:po_ + 64, 0:128], lhsT=V[:, blk + c, kvh * 64:(kvh + 1) * 64], rhs=PnT[:, c * 128:(c + 1) * 128], start=(c == 0), stop=(c == 1)),
                                 reads=[V_t, PnT_t], writes=[pso_t], join=not (first and c == 0))
                        yield

                    for b2 in range(0, 8, 2):
                        psos = [k.ps(), k.ps()]
                        units = []
                        for bi_, blk in enumerate((b2, b2 + 1)):
                            for hh in range(2):
                                units.append(a_unit(blk, hh, psos[bi_][0], psos[bi_][1], smt[bi_ * 2 + hh], hh == 0))
                        lockstep(units)
                        for bi_, blk in enumerate((b2, b2 + 1)):
                            k.op("act", OPF("activation", out=obT[:, hp, blk * 128:(blk + 1) * 128], in_=psos[bi_][0][:, 0:128], func=Cp), reads=[psos[bi_][1]], writes=[ob_t[hp]], join=True)
                    if has_s:
                        k.op("dve", OPF("tensor_tensor", out=qz[:, :, :], in0=bcast(qb[:, TP:TM], 1, NSEQ), in1=bm[:, :, :], op=ALU.mult), reads=[qb_t, bm_t], writes=[qz_t])
                        pso, pso_t = k.ps()

                        def s_unit(hh, tmp):
                            head = 2 * hp + hh
                            po_ = hh * 64
                            ps, pst = k.ps()
                            for s_ in range(NSEQ):
                                mm(ps[:, 0:128], qz[po_:po_ + 64, s_, :], KcT[po_:po_ + 64, s_, kvh, :], s_ == 0, s_ == NSEQ - 1, [qz_t, KcT_t], pst)
                            k.op("pe", OPF("matmul", ps[:, 128:256], lhsT=qb[po_:po_ + 64, TP:TM], rhs=ksT[po_:po_ + 64, kvh, :], start=True, stop=True), reads=[qb_t, ksT_t], writes=[pst], join=True)
                            yield
                            mk_ = maskA["maskA_samp"]
                            yield from softmax_rows(ps, pst, mk_[0], mk_[1], smallt[:, SINK + head:SINK + head + 1], SCL, tmp)
                            PnT, PnT_t = tmp[6], tmp[7]
                            k.op("pe", OPF("matmul", pso[po_:po_ + 64, 0:128], lhsT=vs[:, kvh * 64:(kvh + 1) * 64], rhs=PnT[:, 128:256], start=True, stop=False),
                                 reads=[vs_t, PnT_t], writes=[pso_t], join=(hh > 0))
                            for s_ in range(NSEQ):
                                k.op("pe", OPF("matmul", pso[po_:po_ + 64, s_ * 8:(s_ + 1) * 8], lhsT=Vc[:, s_, kvh * 64:(kvh + 1) * 64], rhs=PnT[:, s_ * 8:(s_ + 1) * 8], start=False, stop=(s_ == NSEQ - 1)),
                                     reads=[Vc_t, PnT_t], writes=[pso_t], join=True)
                            yield
                        lockstep([s_unit(0, smt[0]), s_unit(1, smt[1])])
                        k.op("act", OPF("activation", out=obT[:, hp, TP:TM], in_=pso[:, 0:128], func=Cp), reads=[pso_t], writes=[ob_t[hp]], join=True)
            dbg(f"oaT_{l}_{g}", obT[:, :, 0:T], ob_t, [128, 8, T])
            if cfg.STOP == "A":
                k.emit()
                return nc, k, dbg_out
            k.barrier()

            def branch_merge(b, first):
                cnt = 0
                for mb in range(8):
                    wg, ww, wgt, KCg, KCw = wload2(l, f"d_g{b}_{mb}", f"d_w{b}_{mb}")
                    wwt = wgt
                    for mloc in range(2):
                        m = mb * 2 + mloc
                        for (t0, tn) in tgroups(T):
                            sg, sg_t = sg2[cnt % 2]
                            cnt += 1
                            psg, psg_t = k.ps()
                            for kc in range(KCg):
                                mm(psg[:, 0:tn], wg[:, kc, mloc * 128:(mloc + 1) * 128], uT[:, kc, t0:t0 + tn], kc == 0, kc == KCg - 1, [wgt, uT_t[kc]], psg_t)
                            psw, psw_t = k.ps()
                            for kc in range(KCw):
                                mm(psw[:, 0:tn], ww[:, kc, mloc * 128:(mloc + 1) * 128], obT[:, kc, t0:t0 + tn], kc == 0, kc == KCw - 1, [wwt, ob_t[kc]], psw_t)
                            k.op("act", OPF("activation", out=sg[:, 0:tn], in_=psg[:, 0:tn], func=AF.Sigmoid), reads=[psg_t], writes=[sg_t])
                            if first:
                                k.op("dve", OPF("tensor_tensor", out=mergedT[:, m, t0:t0 + tn], in0=psw[:, 0:tn], in1=sg[:, 0:tn], op=ALU.mult),
                                     reads=[psw_t, sg_t], writes=[mg_t[m]], join=True)
                            else:
                                k.op("dve", OPF("tensor_tensor", out=sg[:, 0:tn], in0=psw[:, 0:tn], in1=sg[:, 0:tn], op=ALU.mult), reads=[psw_t, sg_t], writes=[sg_t])
                                k.op("dve", OPF("tensor_tensor", out=mergedT[:, m, t0:t0 + tn], in0=mergedT[:, m, t0:t0 + tn], in1=sg[:, 0:tn], op=ALU.add),
                                     reads=[sg_t, mg_t[m]], writes=[mg_t[m]])

            branch_merge(0, True)
            k.barrier()

            wa[0] = wa_save
            xp, xp_t = walloc("xp", [128, 3 + TP], F32)
            xc, xc_t = walloc("xc", [128, TM], F32)
            xcb, xcb_t = walloc("xcb", [128, TM], BF16)
            rr, rr_t = walloc("rr", [128, TM], F32)
            ig, ig_t = walloc("ig", [128, TM], F32)
            aa, aa_t = walloc("aa", [128, TM], F32)
            m2, m2_t = walloc("m2", [128, TM], F32)
            hh_, hh_t = walloc("hh", [128, TM], F32)
            yv, yv_t = walloc("yv", [128, TM], F32)
            gt, gt_t = walloc("gt", [128, TM], F32)
            if has_s:
                xps, xps_t = walloc("xps", [128, NSEQ, 11], F32)
                h0s, h0s_t = walloc("h0s", [128, 8, NSEQ], F32)
                slo, slo_t = walloc("slo", [128, 8, NSEQ], F32)
                tm16, tm16_t = walloc("tm16", [128, NSEQ], F32)
                k.dma("sp", h0s[:], slru[l], writes=[h0s_t])

            def v3(ap):
                return ap.rearrange("p (s t) -> p s t", t=8)

            for bi in range(4):
                w, wtr, KC, NCc = wload(l, f"b_xy{bi}")
                for j in range(2):
                    cc = 2 * bi + j
                    k.op("dve", OPF("tensor_copy", out=xp[:, 0:3], in_=convcar[:, cc, :]), reads=[convcar_t], writes=[xp_t])
                    if has_s:
                        k.dma("sp", xps[:, :, 0:3], sconv[l][:, cc, :, :], writes=[xps_t])
                    for (t0, tn) in tgroups(T):
                        ps, pst = k.ps()
                        for kc in range(KC):
                            mm(ps[:, 0:tn], w[:, kc, (2 * j) * 128:(2 * j + 1) * 128], uT[:, kc, t0:t0 + tn], kc == 0, kc == KC - 1, [wtr, uT_t[kc]], pst)
                        if t0 < TP:
                            k.op("act", OPF("activation", out=xp[:, 3 + t0:3 + t0 + tn], in_=ps[:, 0:tn], func=Cp), reads=[pst], writes=[xp_t], join=True)
                        else:
                            k.op("act", OPF("activation", out=xps[:, :, 3:11], in_=v3(ps[:, 0:128]), func=Cp), reads=[pst], writes=[xps_t], join=True)
                    cw = lambda jj, cc=cc: smallt[:, CW + cc * 4 + jj:CW + cc * 4 + jj + 1]
                    k.op("dve", OPF("tensor_scalar", out=xc[:, 0:TP], in0=xp[:, 0:TP], scalar1=cw(0), scalar2=smallt[:, CB + cc:CB + cc + 1], op0=ALU.mult, op1=ALU.add), reads=[xp_t, small_t], writes=[xc_t])
                    for jj in range(1, 4):
                        k.op("dve", OPF("scalar_tensor_tensor", out=xc[:, 0:TP], in0=xp[:, jj:jj + TP], scalar=cw(jj), in1=xc[:, 0:TP], op0=ALU.mult, op1=ALU.add), reads=[xp_t, xc_t, small_t], writes=[xc_t])
                    k.op("dve", OPF("tensor_copy", out=convcar[:, cc, :], in_=xp[:, TP:TP + 3]), reads=[xp_t], writes=[convcar_t])
                    if has_s:
                        xcs = v3(xc[:, TP:TM])
                        k.op("dve", OPF("tensor_scalar", out=xcs, in0=xps[:, :, 0:8], scalar1=cw(0), scalar2=smallt[:, CB + cc:CB + cc + 1], op0=ALU.mult, op1=ALU.add), reads=[xps_t, small_t], writes=[xc_t])
                        for jj in range(1, 4):
                            k.op("dve", OPF("scalar_tensor_tensor", out=xcs, in0=xps[:, :, jj:jj + 8], scalar=cw(jj), in1=xcs, op0=ALU.mult, op1=ALU.add), reads=[xps_t, xc_t, small_t], writes=[xc_t])
                        k.dma("sp", o_sconv[l][:, cc, :, :], xps[:, :, 8:11], reads=[xps_t])
                    k.op("act", OPF("activation", out=xcb[:, 0:T], in_=xc[:, 0:T], func=Cp), reads=[xc_t], writes=[xcb_t])
                    for gi, (dst, dst_t, bcol) in enumerate(((rr, rr_t, BA), (ig, ig_t, BX))):
                        for (t0, tn) in tgroups(T):
                            ps, pst = k.ps()
                            mm(ps[:, 0:tn], bdw_t[gi][0][:, cc, :], xcb[:, t0:t0 + tn], True, True, [bdw_t[gi][1], xcb_t], pst)
                            k.op("act", OPF("activation", out=dst[:, t0:t0 + tn], in_=ps[:, 0:tn], func=AF.Sigmoid, bias=smallt[:, bcol + cc:bcol + cc + 1], scale=1.0),
                                 reads=[pst, small_t], writes=[dst_t], join=True)
                    k.op("act", OPF("activation", out=aa[:, 0:T], in_=rr[:, 0:T], func=AF.Exp, scale=cneg[:, cc:cc + 1]), reads=[rr_t, cneg_t], writes=[aa_t])
                    k.op("dve", OPF("tensor_tensor", out=m2[:, 0:T], in0=aa[:, 0:T], in1=aa[:, 0:T], op=ALU.mult), reads=[aa_t], writes=[m2_t])
                    k.op("dve", OPF("tensor_scalar", out=m2[:, 0:T], in0=m2[:, 0:T], scalar1=-1.0, scalar2=1.0, op0=ALU.mult, op1=ALU.add), reads=[m2_t], writes=[m2_t])
                    k.op("act", OPF("activation", out=m2[:, 0:T], in_=m2[:, 0:T], func=AF.Sqrt), reads=[m2_t], writes=[m2_t])
                    if g == 0:
                        k.op("dve", OPF("memset", m2[:, 0:1], 1.0), reads=[], writes=[m2_t])
                    k.op("dve", OPF("tensor_tensor", out=ig[:, 0:T], in0=ig[:, 0:T], in1=xc[:, 0:T], op=ALU.mult), reads=[ig_t, xc_t], writes=[ig_t])
                    k.op("dve", OPF("tensor_tensor", out=ig[:, 0:T], in0=ig[:, 0:T], in1=m2[:, 0:T], op=ALU.mult), reads=[ig_t, m2_t], writes=[ig_t])
                    if has_s:
                        a0 = v3(aa[:, TP:TM])[:, :, 0]
                        u0 = v3(ig[:, TP:TM])[:, :, 0]
                        k.op("dve", OPF("tensor_tensor", out=tm16[:, :], in0=a0, in1=h0s[:, cc, :], op=ALU.mult), reads=[aa_t, h0s_t], writes=[tm16_t])
                        k.op("dve", OPF("tensor_tensor", out=u0, in0=u0, in1=tm16[:, :], op=ALU.add), reads=[ig_t, tm16_t], writes=[ig_t])
                        k.op("dve", OPF("memset", a0, 0.0), reads=[tm16_t], writes=[aa_t])
                    k.op("dve", OPF("tensor_tensor_scan", out=hh_[:, 0:TP], data0=aa[:, 0:TP], data1=ig[:, 0:TP], initial=hcar[:, cc:cc + 1], op0=ALU.mult, op1=ALU.add), reads=[aa_t, ig_t, hcar_t], writes=[hh_t])
                    if has_s:
                        k.op("dve", OPF("tensor_tensor_scan", out=hh_[:, TP:TM], data0=aa[:, TP:TM], data1=ig[:, TP:TM], initial=0.0, op0=ALU.mult, op1=ALU.add), reads=[aa_t, ig_t], writes=[hh_t], join=True)
                        k.op("dve", OPF("tensor_copy", out=slo[:, cc, :], in_=v3(hh_[:, TP:TM])[:, :, 7]), reads=[hh_t], writes=[slo_t])
                    k.op("dve", OPF("tensor_copy", out=hcar[:, cc:cc + 1], in_=hh_[:, TP - 1:TP]), reads=[hh_t], writes=[hcar_t])
                    for (t0, tn) in tgroups(T):
                        ps, pst = k.ps()
                        for kc in range(KC):
                            mm(ps[:, 0:tn], w[:, kc, (2 * j + 1) * 128:(2 * j + 2) * 128], uT[:, kc, t0:t0 + tn], kc == 0, kc == KC - 1, [wtr, uT_t[kc]], pst)
                        k.op("act", OPF("activation", out=yv[:, t0:t0 + tn], in_=ps[:, 0:tn], func=Cp), reads=[pst], writes=[yv_t], join=True)
                    k.op("dve", OPF("tensor_tensor", out=gt[:, 0:T], in0=yv[:, 0:T], in1=yv[:, 0:T], op=ALU.mult), reads=[yv_t], writes=[gt_t])
                    k.op("dve", OPF("tensor_scalar", out=gt[:, 0:T], in0=gt[:, 0:T], scalar1=0.044715, scalar2=1.0, op0=ALU.mult, op1=ALU.add), reads=[gt_t], writes=[gt_t])
                    k.op("dve", OPF("tensor_tensor", out=gt[:, 0:T], in0=gt[:, 0:T], in1=yv[:, 0:T], op=ALU.mult), reads=[gt_t, yv_t], writes=[gt_t])
                    k.op("act", OPF("activation", out=gt[:, 0:T], in_=gt[:, 0:T], func=AF.Sigmoid, scale=1.5957691216057308), reads=[gt_t], writes=[gt_t])
                    k.op("dve", OPF("tensor_tensor", out=gt[:, 0:T], in0=gt[:, 0:T], in1=yv[:, 0:T], op=ALU.mult), reads=[gt_t, yv_t], writes=[gt_t])
                    k.op("dve", OPF("tensor_tensor", out=obT[:, cc, 0:T], in0=gt[:, 0:T], in1=hh_[:, 0:T], op=ALU.mult), reads=[gt_t, hh_t], writes=[ob_t[cc]])
            if has_s:
                k.dma("sp", o_slru[l], slo[:], reads=[slo_t])
            dbg(f"obT_{l}_{g}", obT[:, :, 0:T], ob_t, [128, 8, T])
            if cfg.STOP == "B":
                k.emit()
                return nc, k, dbg_out
            branch_merge(1, False)
            k.barrier()

            wa[0] = wa_save
            qT_all, qT_all_t = walloc("qT_all", [128, 2, TM], BF16)
            kT_all, kT_all_t = walloc("kT_all", [128, 2, TM], BF16)
            qdT_all, qdT_all_t = walloc("qdT_all", [128, 2, TM], BF16)
            kd_all, kd_all_t = walloc("kd_all", [128, 9, 256], BF16)
            rtb = [walloc(f"rt{i}", [128, 2, 2, 128], F32) for i in range(2)]
            t14 = [walloc(f"t14_{i}", [128, 2, 128], F32) for i in range(4)]
            rot, rot_t = walloc("rot", [128, 2, 256], BF16)
            qd, qd_t = walloc("qd", [128, 256], BF16)
            vb2 = [walloc(f"vb{i}", [128, 256], BF16) for i in range(2)]
            sgl2 = [walloc(f"sgl{i}", [128, 256], F32) for i in range(2)]
            itm2 = [walloc(f"itm{i}", [128, 128], BF16) for i in range(2)]
            yy, yy_t = walloc("yy", [128, 256], F32)
            oc, oc_t = walloc("oc", [128, 256], BF16)
            gst, gst_t = walloc("gst", [128, 16], F32)
            gn_sb, gn_sb_t = walloc("gn_sb", [128, 256], F32)
            Sf, Sf_t = walloc("Sf", [128, 4, 2, 256], F32)
            Sb, Sb_t = walloc("Sb", [128, 4, 2, 256], BF16)
            if g == 0:
                k.op("dve", OPF("memset", Sf[:], 0.0), writes=[Sf_t])
            else:
                k.dma("sp", Sf[:, :, :, :].rearrange("p h c e -> p (h c e)"), sscr, reads=[sscr_t], writes=[Sf_t])
            k.op("act", OPF("activation", out=Sb[:, :, :, :], in_=Sf[:, :, :, :], func=Cp), reads=[Sf_t], writes=[Sb_t])
            if has_s:
                Sst, Sst_t = walloc("Sst", [128, 4, 2, 256], BF16)
                Sfs, Sfs_t = walloc("Sfs", [128, 2, 2, 256], F32)
                vexp, vexp_t = walloc("vexp", [128, 4, 256], BF16)
                oTs, oTs_t = walloc("oTs", [128, 2, 128], F32)
                Sn2 = [walloc(f"Sn{i}", [128, 2, 256], F32) for i in range(1)]

            def gn_gate(po, po_t, h, sgl, sgl_t, dst_cols):
                k.op("dve", OPF("bn_stats", out=gst[:, 0:6], in_=po[:, 0:256]), reads=[po_t], writes=[gst_t])
                k.op("dve", OPF("bn_aggr", out=gst[:, 8:10], in_=gst[:, 0:6]), reads=[gst_t], writes=[gst_t])
                k.op("act", OPF("activation", out=gst[:, 10:11], in_=gst[:, 9:10], func=AF.Sqrt, bias=eps_t[:, 0:1], scale=1.0), reads=[gst_t, eps_tt], writes=[gst_t])
                k.op("dve", OPF("reciprocal", out=gst[:, 10:11], in_=gst[:, 10:11]), reads=[gst_t], writes=[gst_t])
                k.op("dve", OPF("tensor_scalar", out=yy[:, :], in0=po[:, 0:256], scalar1=gst[:, 8:9], scalar2=gst[:, 10:11], op0=ALU.subtract, op1=ALU.mult), reads=[po_t, gst_t], writes=[yy_t])
                k.op("dve", OPF("tensor_tensor", out=yy[:, :], in0=yy[:, :], in1=gn_sb[:, :], op=ALU.mult), reads=[yy_t, gn_sb_t], writes=[yy_t])
                k.op("dve", OPF("tensor_tensor", out=oc[:, :], in0=yy[:, :], in1=sgl[:, :], op=ALU.mult), reads=[yy_t, sgl_t], writes=[oc_t])
                pt, ptt = k.ps()
                ptb = pt[:, :].bitcast(BF16)
                for ec in range(2):
                    k.op("pe", OPF("transpose", out=ptb[:, ec * 128:(ec + 1) * 128], in_=oc[:, ec * 128:(ec + 1) * 128], identity=ident_b[:, :]), reads=[oc_t, ident_b_t], writes=[ptt], join=(ec > 0))
                for ec in range(2):
                    k.op("act", OPF("activation", out=obT[:, 2 * h + ec, dst_cols[0]:dst_cols[1]], in_=ptb[:, ec * 128:(ec + 1) * 128], func=Cp), reads=[ptt], writes=[ob_t[2 * h + ec]], join=True)

            NU = 1 if has_s else 2
            c1bufs = [(rtb[0], t14, (rot, rot_t), (qd, qd_t))]
            if NU == 2:
                t14b = [walloc(f"t14b_{i}", [128, 2, 128], F32) for i in range(4)]
                rotb = walloc("rotb", [128, 2, 256], BF16)
                qdb = walloc("qdb", [128, 256], BF16)
                c1bufs.append((rtb[1], t14b, rotb, qdb))
            for h in range(4):
                k.dma("sp", gn_sb[:], gnb[l][:, h * 256:(h + 1) * 256], writes=[gn_sb_t])
                w1, w1t, KC1, _ = wload(l, f"c_qk{h}")

                def c1_unit(tt, bufs):
                    (rt, rt_t), t14_, (rot_, rot_t_), (qd_, qd_t_) = bufs
                    is_s = (tt == 8)
                    cols = (tt * 128, (tt + 1) * 128)
                    k.dma("sp", rt[:], ctab["rot"][32 if is_s else g * 8 + tt], writes=[rt_t])
                    ps, pst = k.ps()
                    for kc in range(KC1):
                        mm(ps[:, 0:512], uT[:, kc, cols[0]:cols[1]], w1[:, kc, 0:512], kc == 0, kc == KC1 - 1, [w1t, uT_t[kc]], pst)
                    yield
                    psv = ps[:, 0:512].rearrange("p (a d two) -> p a d two", a=2, two=2)
                    x1, x2 = psv[:, :, :, 0], psv[:, :, :, 1]
                    cs, sn = rt[:, 0, :, :], rt[:, 1, :, :]
                    for i_, (xa_, tb) in enumerate(((x1, cs), (x2, sn), (x2, cs), (x1, sn))):
                        k.op("dve", OPF("tensor_tensor", out=t14_[i_][0][:, :, :], in0=xa_, in1=tb, op=ALU.mult), reads=[pst, rt_t], writes=[t14_[i_][1]])
                        yield
                    rv = rot_[:, :, :].rearrange("p a (d two) -> p a d two", two=2)
                    k.op("dve", OPF("tensor_tensor", out=rv[:, :, :, 0], in0=t14_[0][0][:, :, :], in1=t14_[1][0][:, :, :], op=ALU.subtract), reads=[t14_[0][1], t14_[1][1]], writes=[rot_t_])
                    yield
                    k.op("dve", OPF("tensor_tensor", out=rv[:, :, :, 1], in0=t14_[2][0][:, :, :], in1=t14_[3][0][:, :, :], op=ALU.add), reads=[t14_[2][1], t14_[3][1]], writes=[rot_t_], join=True)
                    yield
                    dq = (8 if is_s else 0) + h
                    dk = (12 if is_s else 4) + h
                    k.op("dve", OPF("tensor_scalar", out=qd_[:, :], in0=rot_[:, 0, :], scalar1=dec[:, dq:dq + 1], scalar2=None, op0=ALU.mult), reads=[rot_t_, dec_t], writes=[qd_t_])
                    yield
                    k.op("dve", OPF("tensor_scalar", out=kd_all[:, tt, :], in0=rot_[:, 1, :], scalar1=dec[:, dk:dk + 1], scalar2=None, op0=ALU.mult), reads=[rot_t_, dec_t], writes=[kd_all_t], join=True)
                    yield
                    pt, ptt = k.ps()
                    ptb = pt[:, :].bitcast(BF16)
                    for i_, (a_, dc) in enumerate(((0, 0), (0, 1), (1, 0), (1, 1))):
                        k.op("pe", OPF("transpose", out=ptb[:, i_ * 128:(i_ + 1) * 128], in_=rot_[:, a_, dc * 128:(dc + 1) * 128], identity=ident_b[:, :]), reads=[rot_t_, ident_b_t], writes=[ptt], join=(i_ > 0))
                    for dc in range(2):
                        k.op("pe", OPF("transpose", out=ptb[:, (4 + dc) * 128:(5 + dc) * 128], in_=qd_[:, dc * 128:(dc + 1) * 128], identity=ident_b[:, :]), reads=[qd_t_, ident_b_t], writes=[ptt], join=True)
                    yield
                    p3 = lambda lo: ptb[:, lo * 128:(lo + 2) * 128].rearrange("p (c n) -> p c n", c=2)
                    k.op("act", OPF("activation", out=qT_all[:, :, cols[0]:cols[1]], in_=p3(0), func=Cp), reads=[ptt], writes=[qT_all_t], join=True)
                    k.op("act", OPF("activation", out=kT_all[:, :, cols[0]:cols[1]], in_=p3(2), func=Cp), reads=[ptt], writes=[kT_all_t], join=True)
                    k.op("act", OPF("activation", out=qdT_all[:, :, cols[0]:cols[1]], in_=p3(4), func=Cp), reads=[ptt], writes=[qdT_all_t], join=True)
                    yield

                for t0_ in range(0, ntile, NU):
                    lockstep([c1_unit(t0_ + i_, c1bufs[i_]) for i_ in range(NU) if t0_ + i_ < ntile])
                w2, w2t, KC2, _ = wload(l, f"c_vg{h}")

                def c2_front(tt):
                    is_s = (tt == 8)
                    cols = (tt * 128, (tt + 1) * 128)
                    vb, vb_t = vb2[tt % 2]
                    sgl, sgl_t = sgl2[tt % 2]
                    itm, itm_t = itm2[tt % 2]
                    ps, pst = k.ps()
                    for kc in range(KC2):
                        mm(ps[:, 0:512], uT[:, kc, cols[0]:cols[1]], w2[:, kc, 0:512], kc == 0, kc == KC2 - 1, [w2t, uT_t[kc]], pst)
                    yield
                    k.op("act", OPF("activation", out=vb[:, :], in_=ps[:, 0:256], func=Cp), reads=[pst], writes=[vb_t])
                    k.op("act", OPF("activation", out=sgl[:, :], in_=ps[:, 256:512], func=AF.Silu), reads=[pst], writes=[sgl_t])
                    yield
                    pi, pi_t = k.ps()
                    for dc in range(2):
                        mm(pi[:, 0:128], kT_all[:, dc, cols[0]:cols[1]], qT_all[:, dc, cols[0]:cols[1]], dc == 0, dc == 1, [kT_all_t, qT_all_t], pi_t)
                    yield
                    DT_, DT_t = (DTs, DTs_t) if is_s else (DTp, DTp_t)
                    k.op("dve", OPF("tensor_tensor", out=itm[:, :], in0=pi[:, 0:128], in1=DT_[:, h, :], op=ALU.mult), reads=[pi_t, DT_t], writes=[itm_t])
                    yield

                def c2_back(tt):
                    cols = (tt * 128, (tt + 1) * 128)
                    vb, vb_t = vb2[tt % 2]
                    sgl, sgl_t = sgl2[tt % 2]
                    itm, itm_t = itm2[tt % 2]
                    po, po_t = k.ps()
                    mm(po[:, 0:256], itm[:, :], vb[:, :], True, False, [itm_t, vb_t], po_t)
                    for dc in range(2):
                        mm(po[:, 0:256], qdT_all[:, dc, cols[0]:cols[1]], Sb[:, h, dc, :], False, dc == 1, [qdT_all_t, Sb_t], po_t)
                    yield
                    gn_gate(po, po_t, h, sgl, sgl_t, cols)
                    yield
                    for dc in range(2):
                        pu, pu_t = k.ps()
                        mm(pu[:, 0:256], kd_all[:, tt, dc * 128:(dc + 1) * 128], vb[:, :], True, True, [kd_all_t, vb_t], pu_t)
                        k.op("dve", OPF("scalar_tensor_tensor", out=Sf[:, h, dc, :], in0=Sf[:, h, dc, :], scalar=float(GAM[h] ** 128), in1=pu[:, 0:256], op0=ALU.mult, op1=ALU.add), reads=[pu_t, Sf_t], writes=[Sf_t])
                        yield
                    k.op("act", OPF("activation", out=Sb[:, h, :, :], in_=Sf[:, h, :, :], func=Cp), reads=[Sf_t], writes=[Sb_t])
                    yield

                def c2_back_sample(tt):
                    cols = (tt * 128, (tt + 1) * 128)
                    vb, vb_t = vb2[tt % 2]
                    sgl, sgl_t = sgl2[tt % 2]
                    itm, itm_t = itm2[tt % 2]
                    pots = [k.ps(), k.ps()]
                    for ec in range(2):
                        k.op("pe", OPF("matmul", pots[ec][0][:, 0:128], lhsT=vb[:, ec * 128:(ec + 1) * 128], rhs=itm[:, :], start=True, stop=False), reads=[vb_t, itm_t], writes=[pots[ec][1]])
                    for sq in range(4):
                        for c_ in range(2):
                            k.dma("pool", Sst[:, :, c_, :], sret[l][4 * sq:4 * sq + 4, h, c_ * 128:(c_ + 1) * 128, :].rearrange("s d e -> d s e"), writes=[Sst_t], join=(c_ > 0))
                        for s4 in range(4):
                            s_ = 4 * sq + s4
                            for ec in range(2):
                                for dc in range(2):
                                    k.op("pe", OPF("matmul", pots[ec][0][:, s_ * 8:s_ * 8 + 8], lhsT=Sst[:, s4, dc, ec * 128:(ec + 1) * 128], rhs=qdT_all[:, dc, TP + s_ * 8:TP + s_ * 8 + 8], start=False, stop=(dc == 1 and s_ == NSEQ - 1)),
                                         reads=[Sst_t, qdT_all_t], writes=[pots[ec][1]], join=True)
                    for ec in range(2):
                        k.op("act", OPF("activation", out=oTs[:, ec, :], in_=pots[ec][0][:, 0:128], func=Cp), reads=[pots[ec][1]], writes=[oTs_t], join=(ec > 0))
                    po, po_t = k.ps()
                    for ec in range(2):
                        k.op("pe", OPF("transpose", out=po[:, ec * 128:(ec + 1) * 128], in_=oTs[:, ec, :], identity=ident_f[:, :]), reads=[oTs_t, ident_f_t], writes=[po_t], join=(ec > 0))
                    gn_gate(po, po_t, h, sgl, sgl_t, cols)
                    for sq in range(4):
                        k.op("dve", OPF("tensor_tensor", out=vexp[:, :, :], in0=bcast(vb[:, :], 1, 4), in1=bcast(bmv[:, 4 * sq:4 * sq + 4], 2, 256), op=ALU.mult), reads=[vb_t, bmv_t], writes=[vexp_t])
                        for pr in range(2):
                            for c_ in range(2):
                                k.dma("sp", Sfs[:, :, c_, :], sret[l][4 * sq + 2 * pr:4 * sq + 2 * pr + 2, h, c_ * 128:(c_ + 1) * 128, :].rearrange("s d e -> d s e"), writes=[Sfs_t], join=(c_ > 0))
                            for dc in range(2):
                                Sn, Sn_t = Sn2[0]
                                pu, pu_t = k.ps()
                                mm(pu[:, 0:512], kd_all[:, 8, dc * 128:(dc + 1) * 128], vexp[:, 2 * pr:2 * pr + 2, :].rearrange("p s e -> p (s e)"), True, True, [kd_all_t, vexp_t], pu_t)
                                k.op("dve", OPF("scalar_tensor_tensor", out=Sn[:, :, :], in0=Sfs[:, :, dc, :], scalar=float(GAM[h] ** 8), in1=pu[:, 0:512].rearrange("p (s e) -> p s e", s=2), op0=ALU.mult, op1=ALU.add),
                                     reads=[pu_t, Sfs_t], writes=[Sn_t])
                                s0 = 4 * sq + 2 * pr
                                k.dma("sp", o_sret[l][s0:s0 + 2, h, dc * 128:(dc + 1) * 128, :].rearrange("s d e -> d s e"), Sn[:, :, :], reads=[Sn_t])

                lockstep([c2_front(0)])
                for tt in range(8):
                    gens = [c2_back(tt)]
                    if tt + 1 < ntile:
                        gens.append(c2_front(tt + 1))
                    lockstep(gens)
                if has_s:
                    c2_back_sample(8)
            dbg(f"ocT_{l}_{g}", obT[:, :, 0:T], ob_t, [128, 8, T])
            if cfg.STOP == "C":
                k.emit()
                return nc, k, dbg_out
            k.dma("sp", sscr, Sf[:, :, :, :].rearrange("p h c e -> p (h c e)"), reads=[Sf_t], writes=[sscr_t])
            if g == NG - 1:
                k.dma("sp", o_pret[l], Sf[:, :, :, :].rearrange("p h c e -> p (h c e)"), reads=[Sf_t])
            branch_merge(2, False)
            dbg(f"mgT_{l}_{g}", mergedT[:, :, 0:T], mg_t, [128, KD, T])
            if cfg.STOP == "D":
                k.emit()
                return nc, k, dbg_out
            k.barrier()

            for mb in range(4):
                w, wtr, KC, _ = wload(l, f"e_o{mb}")
                for mloc in range(4):
                    m = mb * 4 + mloc
                    rd = [hscr_tr[g][m]] if l > 0 else []
                    k.dma("sp", hT[:, m, 0:T], src[:, m, c0:c0 + T], reads=rd, writes=[hT_t[m]])
                    for (t0, tn) in tgroups(T):
                        ps, pst = k.ps()
                        for kc in range(KC):
                            mm(ps[:, 0:tn], w[:, kc, mloc * 128:(mloc + 1) * 128], mergedT[:, kc, t0:t0 + tn], kc == 0, kc == KC - 1, [wtr, mg_t[kc]], pst)
                        k.op("dve", OPF("tensor_tensor", out=hT[:, m, t0:t0 + tn], in0=ps[:, 0:tn], in1=hT[:, m, t0:t0 + tn], op=ALU.add), reads=[pst, hT_t[m]], writes=[hT_t[m]])
            dbg(f"h1T_{l}_{g}", hT[:, :, 0:T], hT_t, [128, KD, T])
            if cfg.STOP == "E":
                k.emit()
                return nc, k, dbg_out
            k.barrier()

            rmsnorm_to_uT(l, T, G2, hT, hT_t)
            k.barrier()
            xa = [X0, SB_END]
            qxT, qxT_t = walloc("qxT", [128, 4, TM], BF16, xa)
            oxT, oxT_t = walloc("oxT", [128, 4, TM], BF16, xa)
            xsm = [sm_tmp(f"x{i}", xa) for i in range(4)]
            if g == 0:
                xa_save = xa[0]
                mf, mf_t = walloc("mf", [128, 512], F32, xa)
                memTb, memTb_t = walloc("memTb", [128, KD, 256], BF16, xa)
                k.dma("pool", memTb[:], memT, writes=[memTb_t])
                wk_, wkt, KC, _ = wload(l, "x_k")
                msrc = lambda kc, t0, tn: memTb[:, kc, t0:t0 + tn]
                mtr = [memTb_t] * KD

                def ev_mk(m, t0, tn, ps, pst):
                    k.op("act", OPF("activation", out=memK[:, m, :], in_=ps[:, 0:256], func=Cp), reads=[pst], writes=[memK_t], join=True)
                dense_fm(wk_, wkt, KC, range(4), msrc, mtr, 256, ev_mk)

                def ev_mkt(tt, ps, pst):
                    k.op("act", OPF("activation", out=mf[:, :], in_=ps[:, 0:512], func=Cp), reads=[pst], writes=[mf_t])
                    k.dma("sp", o_pmk[l][tt * 128:(tt + 1) * 128, :], mf[:, :], reads=[mf_t])
                dense_tm(wk_, wkt, KC, 512, msrc, mtr, range(2), ev_mkt)
                wv_, wvt, KC, _ = wload(l, "x_v")

                def ev_mvt(tt, ps, pst):
                    k.op("act", OPF("activation", out=memV[:, tt, :], in_=ps[:, 0:512], func=Cp), reads=[pst], writes=[memV_t], join=True)
                    k.op("act", OPF("activation", out=mf[:, :], in_=ps[:, 0:512], func=Cp), reads=[pst], writes=[mf_t])
                    k.dma("sp", o_pmv[l][tt * 128:(tt + 1) * 128, :], mf[:, :], reads=[mf_t])
                dense_tm(wv_, wvt, KC, 512, msrc, mtr, range(2), ev_mvt)
                k.barrier()
                xa[0] = xa_save
                qzx, qzx_t = walloc("qzx", [128, NSEQ, 128], BF16, xa)
                mkh, mkh_t = walloc("mkh", [128, 8, 256], BF16, xa)
                mvh, mvh_t = walloc("mvh", [128, 8, 2, 128], BF16, xa)
            wq_, wqt, KC, _ = wload(l, "x_q")

            def ev_q(m, t0, tn, ps, pst):
                k.op("act", OPF("activation", out=qxT[:, m, t0:t0 + tn], in_=ps[:, 0:tn], func=Cp), reads=[pst], writes=[qxT_t], join=True)
            dense_fm(wq_, wqt, KC, range(4), uT_src, uT_t, T, ev_q)
            XS = 128.0 ** -0.5
            def x_unit(hd, tt, tmp):
                ps, pst = k.ps()
                mm(ps[:, 0:256], qxT[:, hd, tt * 128:(tt + 1) * 128], memK[:, hd, :], True, True, [qxT_t, memK_t], pst)
                yield
                yield from softmax_rows(ps, pst, None, None, None, XS, tmp)
                PnT, PnT_t = tmp[6], tmp[7]
                for mc in range(2):
                    mm(ps[:, 384:512], memV[:, mc, hd * 128:(hd + 1) * 128], PnT[:, mc * 128:(mc + 1) * 128], mc == 0, mc == 1, [memV_t, PnT_t], pst)
                yield
                k.op("act", OPF("activation", out=oxT[:, hd, tt * 128:(tt + 1) * 128], in_=ps[:, 384:512], func=Cp), reads=[pst], writes=[oxT_t], join=True)
                yield

            for hd in range(4):
                for t4 in range(0, 8, 4):
                    lockstep([x_unit(hd, t4 + i_, xsm[i_]) for i_ in range(4)])
                if has_s:
                    tmp = xsm[0]
                    k.op("dve", OPF("tensor_tensor", out=qzx[:, :, :], in0=bcast(qxT[:, hd, TP:TM], 1, NSEQ), in1=bm[:, :, :], op=ALU.mult), reads=[qxT_t, bm_t], writes=[qzx_t])
                    ps, pst = k.ps()
                    for s_ in range(NSEQ):
                        if s_ % 8 == 0:
                            k.dma("pool", mkh[:], mkT[l][:, s_:s_ + 8, hd, :], writes=[mkh_t])
                        mm(ps[:, 0:256], qzx[:, s_, :], mkh[:, s_ % 8, :], s_ == 0, s_ == NSEQ - 1, [qzx_t, mkh_t], pst)
                    for _ in softmax_rows(ps, pst, None, None, None, XS, tmp):
                        pass
                    PnT, PnT_t = tmp[6], tmp[7]
                    pso, pso_t = k.ps()
                    for s_ in range(NSEQ):
                        if s_ % 8 == 0:
                            k.dma("pool", mvh[:], mv[l][:, s_:s_ + 8, :, hd * 128:(hd + 1) * 128], writes=[mvh_t])
                        for mc in range(2):
                            k.op("pe", OPF("matmul", pso[:, s_ * 8:(s_ + 1) * 8], lhsT=mvh[:, s_ % 8, mc, :], rhs=PnT[:, mc * 128 + s_ * 8:mc * 128 + s_ * 8 + 8], start=(mc == 0), stop=(mc == 1)),
                                 reads=[mvh_t, PnT_t], writes=[pso_t], join=not (s_ == 0 and mc == 0))
                    k.op("act", OPF("activation", out=oxT[:, hd, TP:TM], in_=pso[:, 0:128], func=Cp), reads=[pso_t], writes=[oxT_t], join=True)
            wo_, wot, KC, _ = wload(l, "x_o")
            for m in range(KD):
                for (t0, tn) in tgroups(T):
                    ps, pst = k.ps()
                    for kc in range(KC):
                        mm(ps[:, 0:tn], wo_[:, kc, m * 128:(m + 1) * 128], oxT[:, kc, t0:t0 + tn], kc == 0, kc == KC - 1, [wot, oxT_t], pst)
                    k.op("dve", OPF("tensor_tensor", out=hT[:, m, t0:t0 + tn], in0=ps[:, 0:tn], in1=hT[:, m, t0:t0 + tn], op=ALU.add), reads=[pst, hT_t[m]], writes=[hT_t[m]])
            dbg(f"h2T_{l}_{g}", hT[:, :, 0:T], hT_t, [128, KD, T])
            if cfg.STOP == "X":
                k.emit()
                return nc, k, dbg_out
            k.barrier()

            rmsnorm_to_uT(l, T, G3, hT, hT_t)
            k.barrier()
            xa = [X0, SB_END]
            fT2 = [walloc(f"fT{i}", [128, 8, TM], BF16, xa) for i in range(2)]
            fr2 = [walloc(f"fr{i}", [128, 512], F32, xa) for i in range(2)]
            frc = [0]

            def ffn_up(e_):
                fT, fT_t = fT2[e_ % 2]
                for i in range(2):
                    w, wtr, KC, _ = wload(l, f"f_u{e_}_{i}")
                    for mloc in range(4):
                        fc = i * 4 + mloc
                        for (t0, tn) in tgroups(T):
                            fr, fr_t = fr2[frc[0] % 2]
                            frc[0] += 1
                            ps, pst = k.ps()
                            for kc in range(KC):
                                mm(ps[:, 0:tn], w[:, kc, mloc * 128:(mloc + 1) * 128], uT[:, kc, t0:t0 + tn], kc == 0, kc == KC - 1, [wtr, uT_t[kc]], pst)
                            k.op("act", OPF("activation", out=fr[:, 0:tn], in_=ps[:, 0:tn], func=AF.Relu), reads=[pst], writes=[fr_t])
                            k.op("dve", OPF("tensor_tensor", out=fT[:, fc, t0:t0 + tn], in0=fr[:, 0:tn], in1=fr[:, 0:tn], op=ALU.mult), reads=[fr_t], writes=[fT_t], join=True)

            def ffn_down(e_):
                fT, fT_t = fT2[e_ % 2]
                for i in range(2):
                    w, wtr, KC, _ = wload(l, f"f_d{e_}_{i}")
                    for mloc in range(8):
                        m = i * 8 + mloc
                        for (t0, tn) in tgroups(T):
                            ps, pst = k.ps()
                            for kc in range(KC):
                                mm(ps[:, 0:tn], w[:, kc, mloc * 128:(mloc + 1) * 128], fT[:, kc, t0:t0 + tn], kc == 0, kc == KC - 1, [wtr, fT_t], pst)
                            k.op("dve", OPF("tensor_tensor", out=hT[:, m, t0:t0 + tn], in0=ps[:, 0:tn], in1=hT[:, m, t0:t0 + tn], op=ALU.add), reads=[pst, hT_t[m]], writes=[hT_t[m]])

            ffn_up(0)
            for e_ in range(8):
                if e_ < 7:
                    ffn_up(e_ + 1)
                ffn_down(e_)
            dbg(f"h3T_{l}_{g}", hT[:, :, 0:T], hT_t, [128, KD, T])
            if cfg.STOP == "F":
                k.emit()
                return nc, k, dbg_out
            k.barrier()
            if l < L - 1:
                for kc in range(KD):
                    k.dma("sp", hscr[:, kc, c0:c0 + T], hT[:, kc, 0:T], reads=[hT_t[kc]], writes=[hscr_tr[g][kc]])
            else:
                rmsnorm_to_uT(l, T, GF, hT, hT_t, final_out=(yT, c0))
        k.dma("sp", o_pconv[l], convcar[:], reads=[convcar_t])
        k.dma("sp", o_plru[l], hcar[:], reads=[hcar_t])
    t_emit = k.emit()
    return nc, k, dbg_out


def _fm(a):
    t = a.shape[0]
    return np.ascontiguousarray(a.T.reshape(KD, 128, t).transpose(1, 0, 2))


def pack_weights(inp, l):
    plan, wtot = weight_plan()
    srcs = {"w_in": inp["w_in"][l], "w_out": inp["w_out"][l], "w_xq": inp["w_xq"][l], "w_xk": inp["w_xk"][l],
            "w_xv": inp["w_xv"][l], "w_xo": inp["w_xo"][l], "w_up": inp["w_up"][l], "w_down": inp["w_down"][l]}
    for b in range(3):
        srcs[f"w_branch{b}"] = inp["w_branch"][l, b]
    arr = np.empty((128, wtot), np.float32)
    for key, (src, r0, nr, cols, off) in plan.items():
        W = srcs[src][r0:r0 + nr]
        c0, c1 = int(cols[0]), int(cols[-1]) + 1
        if c1 - c0 == len(cols) and np.all(np.diff(cols) == 1):
            W = W[:, c0:c1]
        else:
            W = W[:, cols]
        KC = nr // 128
        arr[:, off:off + KC * len(cols)] = W.reshape(KC, 128, len(cols)).transpose(1, 0, 2).reshape(128, -1)
    return arr


def prep_inputs(inp, cfg, n_cores=8):
    f32 = np.float32
    inp = {k_: np.asarray(v) for k_, v in inp.items()}
    common = {}
    for l in range(cfg.NL):
        common[f"wl{l}"] = pack_weights(inp, l)
    small = np.zeros((L, 128, 144), f32)
    bdw = np.zeros((L, 2, 128, 8, 128), f32)
    gnb = np.zeros((L, 128, 1024), f32)
    pp = lambda v: v.reshape(-1, 128).T
    for l in range(L):
        small[l, :, 0:16] = pp(inp["norm_mix"][l])
        small[l, :, 16:32] = pp(inp["norm_cross"][l])
        small[l, :, 32:48] = pp(inp["norm_ffn"][l])
        small[l, :, 48:64] = pp(inp["norm_final"])
        small[l, :, 64:96] = inp["conv_w"][l].reshape(4, 8, 128).transpose(2, 1, 0).reshape(128, 32)
        small[l, :, 96:104] = pp(inp["conv_b"][l])
        small[l, :, 104:112] = pp(inp["lru_ba"][l])
        small[l, :, 112:120] = pp(inp["lru_bx"][l])
        small[l, :, 120:128] = pp(inp["lru_lambda"][l])
        small[l, :, 128:144] = np.broadcast_to(inp["attn_sink"][l][None, :], (128, 16))
        for i, nm in enumerate(("lru_wa", "lru_wx")):
            w = inp[nm][l]
            for cc in range(8):
                for bn in range(2):
                    bdw[l, i, bn * 64:(bn + 1) * 64, cc, bn * 64:(bn + 1) * 64] = w[2 * cc + bn]
        gnb[l] = np.broadcast_to(inp["ret_gn"][l].reshape(1, 1024), (128, 1024))
    common["small"] = small
    common["bdw"] = bdw
    common["gnb"] = gnb
    for n, v in const_tables().items():
        common["c_" + n] = np.ascontiguousarray(v, dtype=f32)
    maps = []
    for c in range(n_cores):
        m = dict(common)
        xs = inp["x_sample"][c * NSEQ:(c + 1) * NSEQ].reshape(TS, D)
        if c < 2:
            xp = inp["x_prompt"][c]
            mem = inp["mem_prompt"][c]
        else:
            xp = np.zeros((SEQ, D), f32)
            mem = np.zeros((256, D), f32)
        m["xT"] = _fm(np.concatenate([xp[0:TP], xs, xp[TP:]], axis=0))
        m["memT"] = _fm(mem)
        sl = slice(c * NSEQ, (c + 1) * NSEQ)
        ck, cv = inp["cache_win_k"][:, sl], inp["cache_win_v"][:, sl]
        m["kcT"] = np.ascontiguousarray(ck.transpose(0, 4, 1, 3, 2))
        m["vc"] = np.ascontiguousarray(cv.reshape(L, NSEQ, 128, 256).transpose(0, 2, 1, 3))
        m["kc_tm"] = np.ascontiguousarray(ck.reshape(L, NSEQ, 128, 256))
        m["vc_tm"] = np.ascontiguousarray(cv.reshape(L, NSEQ, 128, 256))
        m["sconv"] = np.ascontiguousarray(inp["state_conv"][:, sl].reshape(L, NSEQ, 3, 8, 128).transpose(0, 4, 3, 1, 2))
        m["slru"] = np.ascontiguousarray(inp["state_lru"][:, sl].reshape(L, NSEQ, 8, 128).transpose(0, 3, 2, 1))
        m["sret"] = np.ascontiguousarray(inp["state_ret"][:, sl])
        m["mkT"] = np.ascontiguousarray(inp["cache_mem_k"][:, sl].transpose(0, 4, 1, 3, 2))
        m["mv"] = np.ascontiguousarray(inp["cache_mem_v"][:, sl].reshape(L, NSEQ, 2, 128, 512).transpose(0, 3, 1, 2, 4))
        maps.append(m)
    return maps


_PROG = {}


def run(inp, cfg=None, n_cores=8):
    cfg = cfg or Cfg()
    key = (cfg.NL, cfg.NG, cfg.DBG, cfg.STOP)
    if key not in _PROG:
        _PROG[key] = build_program(cfg)
    nc, k, dbg_out = _PROG[key]
    maps = prep_inputs(inp, cfg, n_cores)
    res = run_bass_kernel_spmd(nc, maps, core_ids=list(range(n_cores)))
    return res.results


def kernel(**inputs):
    r = run(inputs, Cfg(), 8)
    f32 = np.float32
    yp = np.zeros((2, SEQ, D), f32)
    ys = np.zeros((128, 8, D), f32)
    for c in range(8):
        y = r[c]["yT"].transpose(1, 0, 2).reshape(D, TTOT).T
        ys[c * NSEQ:(c + 1) * NSEQ] = y[TP:TP + TS].reshape(NSEQ, 8, D)
        if c < 2:
            yp[c, 0:TP] = y[0:TP]
            yp[c, TP:] = y[TP + TS:]
    st = lambda name, f: np.stack([f(r[c][name]) for c in range(2)], axis=1)
    p_wk = st("o_pwk", lambda a: a.reshape(L, 128, 4, 64))
    p_wv = st("o_pwv", lambda a: a.reshape(L, 128, 4, 64))
    p_conv = st("o_pconv", lambda a: a.transpose(0, 3, 2, 1).reshape(L, 3, 1024))
    p_lru = st("o_plru", lambda a: a.transpose(0, 2, 1).reshape(L, 1024))
    p_ret = st("o_pret", lambda a: a.reshape(L, 128, 4, 2, 256).transpose(0, 2, 3, 1, 4).reshape(L, 4, 256, 256))
    p_mk = st("o_pmk", lambda a: a.reshape(L, 256, 4, 128))
    p_mv = st("o_pmv", lambda a: a.reshape(L, 256, 4, 128))
    cat = lambda name, f: np.concatenate([f(r[c][name]) for c in range(8)], axis=1)
    s_wk = cat("o_swk", lambda a: a.reshape(L, NSEQ, 128, 4, 64))
    s_wv = cat("o_swv", lambda a: a.reshape(L, NSEQ, 128, 4, 64))
    s_conv = cat("o_sconv", lambda a: a.transpose(0, 3, 4, 2, 1).reshape(L, NSEQ, 3, 1024))
    s_lru = cat("o_slru", lambda a: a.transpose(0, 3, 2, 1).reshape(L, NSEQ, 1024))
    s_ret = cat("o_sret", lambda a: a)
    outs = (yp, ys, p_wk, p_wv, p_conv, p_lru, p_ret, p_mk, p_mv, s_wk, s_wv, s_conv, s_lru, s_ret)
    return tuple(np.ascontiguousarray(o, dtype=f32) for o in outs)
```
